# Optimizing a Trainium2 kernel written in Bass

```python
import math
import functools
import jax
import jax.numpy as jnp
from jax import lax
import numpy as np

D_MODEL = 4096
BATCH = 4
SEQ = 2048
DEPTH = 2
DEC_BATCH = 8
DEC_SEQ = 4
PAST_LEN = 16384
PAGE_SIZE = 128

D_MIX = D_MODEL
W_A = D_MIX // 4
A_GROUP = 128
A_HEADS = W_A // A_GROUP
A_CHUNK = 128
W_B = D_MIX // 2
DN_HEAD = 128
DN_HEADS = W_B // DN_HEAD
CONV_K = 4
DN_CONV_DIM = 3 * W_B
DN_CHUNK = 64
W_C = D_MIX - W_A - W_B
C_HEAD = 128
C_HEADS = W_C // C_HEAD
C_KV_HEADS = 2
C_GROUPS = C_HEADS // C_KV_HEADS
C_KV = C_KV_HEADS * C_HEAD
IDX_HEADS = 16
IDX_DIM = 64
TOPK_MAX = 256
Q_BLOCK = 128
EPS = 1e-6

IN_SPLITS = (W_A, W_A, W_A,
             DN_CONV_DIM, W_B, DN_HEADS, DN_HEADS,
             W_C, C_KV, C_KV, W_C,
             IDX_HEADS * IDX_DIM, IDX_DIM, IDX_HEADS)
D_IN = sum(IN_SPLITS)

kernel_name = 'hybrid_gmlp_deltanet_dsa_step'


def _rmsnorm(x, g):
    xf = x.astype(jnp.float32)
    y = xf * lax.rsqrt(jnp.mean(xf * xf, axis=-1, keepdims=True) + EPS)
    return (y * g.astype(jnp.float32)).astype(x.dtype)


def _layernorm(x, g):
    xf = x.astype(jnp.float32)
    mu = jnp.mean(xf, axis=-1, keepdims=True)
    d = xf - mu
    y = d * lax.rsqrt(jnp.mean(d * d, axis=-1, keepdims=True) + EPS)
    return (y * g.astype(jnp.float32)).astype(x.dtype)


def _l2norm(x):
    xf = x.astype(jnp.float32)
    return (xf * lax.rsqrt(jnp.sum(xf * xf, axis=-1, keepdims=True) + EPS)).astype(x.dtype)


def _split_in(p):
    points = np.cumsum(np.array(IN_SPLITS))[:-1].tolist()
    return jnp.split(p, points, axis=-1)


def _gather_rows(x, idx):
    return jax.vmap(lambda xb, ib: xb[ib])(x, idx)


def _causal_conv(x, prev, w):
    L = x.shape[1]
    xp = jnp.concatenate([prev, x], axis=1)
    y = xp[:, 0:L] * w[0]
    for j in range(1, CONV_K):
        y = y + xp[:, j:j + L] * w[j]
    return jax.nn.silu(y), xp[:, -(CONV_K - 1):]


def _chunk_mlp(u, v, w_s, b_s):
    B, L, _ = u.shape
    C = min(A_CHUNK, L)
    n = L // C
    vr = v.reshape(B, n, C, A_HEADS, A_GROUP)
    wm = jnp.where(jnp.tril(jnp.ones((C, C), bool)), w_s[:, :C, :C], 0.0)
    mixed = jnp.einsum('hts,bnshd->bnthd', wm, vr) + b_s[:, :C].T[None, None, :, :, None]
    return u * mixed.reshape(B, L, W_A)


def _gated_delta(q, k, v, g, beta, s0):
    out_dtype = v.dtype
    B, L, H, DK = q.shape
    DV = v.shape[-1]
    C = math.gcd(L, DN_CHUNK)
    n = L // C
    f = jnp.float32

    def chunks(t):
        t = t.astype(f).reshape((B, n, C, H) + t.shape[3:])
        return jnp.moveaxis(t, (1, 3), (0, 2))

    qc = chunks(q) * DK ** -0.5
    kc, vc, gc, bc = chunks(k), chunks(v), chunks(g), chunks(beta)
    G = jnp.cumsum(gc, axis=-1)
    incl = jnp.tril(jnp.ones((C, C), bool))
    strict = jnp.tril(jnp.ones((C, C), bool), -1)
    diff = G[..., :, None] - G[..., None, :]
    decay = jnp.where(incl, jnp.exp(jnp.where(incl, diff, 0.0)), 0.0)
    kb = kc * bc[..., None]
    a = jnp.where(strict, jnp.einsum('nbhcd,nbhsd->nbhcs', kb, kc) * decay, 0.0)
    eye = jnp.broadcast_to(jnp.eye(C, dtype=f), a.shape)
    rhs = jnp.concatenate([vc * bc[..., None], kb * jnp.exp(G)[..., None]], axis=-1)
    sol = lax.linalg.triangular_solve(eye + a, rhs, left_side=True, lower=True, unit_diagonal=True)
    uc, wc = sol[..., :DV], sol[..., DV:]
    attn = jnp.where(incl, jnp.einsum('nbhcd,nbhsd->nbhcs', qc, kc) * decay, 0.0)
    qg = qc * jnp.exp(G)[..., None]
    kg = kc * jnp.exp(G[..., -1:] - G)[..., None]
    gl = jnp.exp(G[..., -1])

    def step(s, xs):
        qg_i, kg_i, u_i, w_i, attn_i, gl_i = xs
        v_new = u_i - jnp.einsum('bhcd,bhde->bhce', w_i, s)
        o = jnp.einsum('bhcd,bhde->bhce', qg_i, s) + jnp.einsum('bhcs,bhse->bhce', attn_i, v_new)
        s = s * gl_i[..., None, None] + jnp.einsum('bhcd,bhce->bhde', kg_i, v_new)
        return s, o

    s_fin, o = lax.scan(step, s0.astype(f), (qg, kg, uc, wc, attn, gl))
    o = jnp.swapaxes(jnp.moveaxis(o, 0, 1), 2, 3).reshape(B, L, H, DV)
    return o.astype(out_dtype), s_fin.astype(s0.dtype)


def _indexer_topk(q_idx, w_idx, q_pos, k_idx, top_k):
    n_keys = k_idx.shape[1]
    logits = jnp.einsum('bthd,bld->bthl', q_idx, k_idx).astype(jnp.float32) * IDX_DIM ** -0.5
    w = w_idx.astype(jnp.float32) * IDX_HEADS ** -0.5
    score = jnp.einsum('bth,bthl->btl', w, jax.nn.relu(logits))
    causal = jnp.arange(n_keys, dtype=jnp.int32)[None, :] <= q_pos[:, None]
    score = jnp.where(causal[None], score, -jnp.inf)
    _, idx = lax.top_k(score, top_k)
    valid = idx <= q_pos[None, :, None]
    return idx, valid


def _sparse_attend(q, k_sel, v_sel, valid):
    s = jnp.einsum('bthgd,btkhd->bthgk', q, k_sel).astype(jnp.float32) * C_HEAD ** -0.5
    s = jnp.where(valid[:, :, None, None, :], s, -jnp.inf)
    p = jax.nn.softmax(s, axis=-1)
    return jnp.einsum('bthgk,btkhd->bthgd', p.astype(v_sel.dtype), v_sel)


def _dsa_prompt(q, k, v, q_idx, k_idx, w_idx):
    B, L = q.shape[:2]
    top_k = min(TOPK_MAX, L // 4)
    qb = min(Q_BLOCK, L)
    nb = L // qb

    def blocks(t):
        return jnp.moveaxis(t.reshape((B, nb, qb) + t.shape[2:]), 1, 0)

    pos = jnp.arange(L, dtype=jnp.int32).reshape(nb, qb)

    def one_block(xs):
        q_b, qi_b, wi_b, pos_b = xs
        idx, valid = _indexer_topk(qi_b, wi_b, pos_b, k_idx, top_k)
        return _sparse_attend(q_b, _gather_rows(k, idx), _gather_rows(v, idx), valid)

    o = lax.map(one_block, (blocks(q), blocks(q_idx), blocks(w_idx), pos))
    return jnp.moveaxis(o, 0, 1).reshape(B, L, W_C)


def _dsa_sample(q, k, v, q_idx, k_idx, w_idx, cache_k, cache_v, cache_kidx, page_table):
    B, T = q.shape[:2]
    past = page_table.shape[1] * PAGE_SIZE
    top_k = min(TOPK_MAX, (past + T) // 4)
    kidx_past = cache_kidx[page_table].reshape(B, past, IDX_DIM)
    kidx_all = jnp.concatenate([kidx_past.astype(k_idx.dtype), k_idx], axis=1)
    pos = past + jnp.arange(T, dtype=jnp.int32)
    idx, valid = _indexer_topk(q_idx, w_idx, pos, kidx_all, top_k)
    is_past = (idx < past)[..., None, None]
    pc = jnp.minimum(idx, past - 1)
    phys = jax.vmap(lambda pt, p: pt[p])(page_table, pc // PAGE_SIZE)
    off = pc % PAGE_SIZE
    nidx = jnp.clip(idx - past, 0, T - 1)
    k_sel = jnp.where(is_past, cache_k[phys, off].astype(k.dtype), _gather_rows(k, nidx))
    v_sel = jnp.where(is_past, cache_v[phys, off].astype(v.dtype), _gather_rows(v, nidx))
    return _sparse_attend(q, k_sel, v_sel, valid).reshape(B, T, W_C)


def _layer(x, c, w_ada, b_ada, g_norm, w_in, a_vnorm, a_ws, a_bs, dn_conv_w, dn_a_log, dn_dt_bias,
           dn_onorm, w_out, conv_prev, s0, attend):
    B, L, _ = x.shape
    m = jax.nn.silu(c) @ w_ada + b_ada
    shift, scale, gate = jnp.split(m[:, None, :], 3, axis=-1)
    h = _rmsnorm(x, g_norm) * (1 + scale) + shift
    (u_a, v_a, z_a, qkv_b, z_b, a_b, b_b, q_c, k_c, v_c, z_c, qi_c, ki_c, wi_c) = _split_in(h @ w_in)
    u_a = jax.nn.gelu(u_a)
    v_a = _layernorm(jax.nn.gelu(v_a), a_vnorm)
    y_a = _chunk_mlp(u_a, v_a, a_ws, a_bs) * jax.nn.silu(z_a)
    qkv_conv, conv_new = _causal_conv(qkv_b, conv_prev, dn_conv_w)
    q_b, k_b, v_b = jnp.split(qkv_conv, 3, axis=-1)
    hs = (B, L, DN_HEADS, DN_HEAD)
    g = -jnp.exp(dn_a_log.astype(jnp.float32)) * jax.nn.softplus(a_b.astype(jnp.float32) + dn_dt_bias.astype(jnp.float32))
    beta = jax.nn.sigmoid(b_b.astype(jnp.float32))
    o_b, s_new = _gated_delta(_l2norm(q_b.reshape(hs)), _l2norm(k_b.reshape(hs)), v_b.reshape(hs), g, beta, s0)
    y_b = (_rmsnorm(o_b, dn_onorm) * jax.nn.silu(z_b.reshape(hs))).reshape(B, L, W_B)
    k_rows = k_c.reshape(B, L, C_KV_HEADS, C_HEAD)
    v_rows = v_c.reshape(B, L, C_KV_HEADS, C_HEAD)
    o_c = attend(q_c.reshape(B, L, C_KV_HEADS, C_GROUPS, C_HEAD), k_rows, v_rows,
                 qi_c.reshape(B, L, IDX_HEADS, IDX_DIM), ki_c, wi_c)
    y_c = o_c * jax.nn.silu(z_c)
    mix = jnp.concatenate([y_a, y_b, y_c], axis=-1)
    x = x + gate * (mix @ w_out)
    return x, v_a, conv_new, s_new, k_rows, v_rows, ki_c


def setup_inputs(seed: int = 0) -> dict:
    key = jax.random.key(seed)
    ks = jax.random.split(key, 24)
    f = jnp.float32
    n_pages = PAST_LEN // PAGE_SIZE
    n_used = DEC_BATCH * n_pages
    n_pool = n_used + max(1, n_used // 4)
    page_table = jax.random.permutation(ks[0], n_pool)[:n_used].reshape(DEC_BATCH, n_pages).astype(jnp.int32)

    def nrm(k, shape, s=1.0):
        return s * jax.random.normal(k, shape, f)

    dt = jnp.exp(jax.random.uniform(ks[1], (DEPTH, DN_HEADS), f, math.log(1e-3), math.log(1e-1)))
    return {
        'x_prompt': nrm(ks[2], (BATCH, SEQ, D_MODEL)),
        'x_sample': nrm(ks[3], (DEC_BATCH, DEC_SEQ, D_MODEL)),
        'cache_k': nrm(ks[4], (DEPTH, n_pool, PAGE_SIZE, C_KV_HEADS, C_HEAD)),
        'cache_v': nrm(ks[5], (DEPTH, n_pool, PAGE_SIZE, C_KV_HEADS, C_HEAD)),
        'cache_kidx': nrm(ks[6], (DEPTH, n_pool, PAGE_SIZE, IDX_DIM)),
        'state_dn': nrm(ks[7], (DEPTH, DEC_BATCH, DN_HEADS, DN_HEAD, DN_HEAD), DN_HEAD ** -0.5),
        'state_conv': nrm(ks[8], (DEPTH, DEC_BATCH, CONV_K - 1, DN_CONV_DIM)),
        'page_table': page_table,
        'c_prompt': nrm(ks[9], (BATCH, D_MODEL)),
        'c_sample': nrm(ks[10], (DEC_BATCH, D_MODEL)),
        'w_ada': nrm(ks[11], (DEPTH, D_MODEL, 3 * D_MODEL), 0.5 * D_MODEL ** -0.5),
        'b_ada': nrm(ks[12], (DEPTH, 3 * D_MODEL), 0.02),
        'g_norm': 1.0 + nrm(ks[13], (DEPTH, D_MODEL), 0.02),
        'w_in': nrm(ks[14], (DEPTH, D_MODEL, D_IN), D_MODEL ** -0.5),
        'a_vnorm': 1.0 + nrm(ks[15], (DEPTH, W_A), 0.02),
        'a_ws': nrm(ks[16], (DEPTH, A_HEADS, A_CHUNK, A_CHUNK), 0.5 * A_CHUNK ** -0.5),
        'a_bs': 1.0 + nrm(ks[17], (DEPTH, A_HEADS, A_CHUNK), 0.02),
        'dn_conv_w': nrm(ks[18], (DEPTH, CONV_K, DN_CONV_DIM), CONV_K ** -0.5),
        'dn_a_log': jnp.log(jax.random.uniform(ks[19], (DEPTH, DN_HEADS), f, 1.0, 16.0)),
        'dn_dt_bias': dt + jnp.log(-jnp.expm1(-dt)),
        'dn_onorm': 1.0 + nrm(ks[20], (DEPTH, DN_HEAD), 0.02),
        'w_out': nrm(ks[21], (DEPTH, D_MIX, D_MODEL), D_MIX ** -0.5),
        'g_final': 1.0 + nrm(ks[22], (D_MODEL,), 0.02),
    }


def reference(x_prompt, x_sample, cache_k, cache_v, cache_kidx, state_dn, state_conv, page_table,
              c_prompt, c_sample, w_ada, b_ada, g_norm, w_in, a_vnorm, a_ws, a_bs, dn_conv_w,
              dn_a_log, dn_dt_bias, dn_onorm, w_out, g_final):
    b_p = x_prompt.shape[0]
    xp, xs = x_prompt, x_sample
    pk, pv, pki, pdn, pconv = [], [], [], [], []
    sk, sv, ski, sdn, sconv, samlp = [], [], [], [], [], []
    for l in range(DEPTH):
        lw = (w_ada[l], b_ada[l], g_norm[l], w_in[l], a_vnorm[l], a_ws[l], a_bs[l], dn_conv_w[l],
              dn_a_log[l], dn_dt_bias[l], dn_onorm[l], w_out[l])
        conv0 = jnp.zeros((b_p, CONV_K - 1, DN_CONV_DIM), x_prompt.dtype)
        s0 = jnp.zeros((b_p, DN_HEADS, DN_HEAD, DN_HEAD), x_prompt.dtype)
        xp, _, conv_p, s_p, k_p, v_p, ki_p = _layer(xp, c_prompt, *lw, conv0, s0, _dsa_prompt)
        attend_s = functools.partial(_dsa_sample, cache_k=cache_k[l], cache_v=cache_v[l],
                                     cache_kidx=cache_kidx[l], page_table=page_table)
        xs, va_s, conv_s, s_s, k_s, v_s, ki_s = _layer(xs, c_sample, *lw, state_conv[l], state_dn[l], attend_s)
        pk.append(k_p); pv.append(v_p); pki.append(ki_p); pdn.append(s_p); pconv.append(conv_p)
        sk.append(k_s); sv.append(v_s); ski.append(ki_s); sdn.append(s_s); sconv.append(conv_s); samlp.append(va_s)
    y_prompt = _rmsnorm(xp, g_final)
    y_sample = _rmsnorm(xs, g_final)
    return (y_prompt, y_sample,
            jnp.stack(pk), jnp.stack(pv), jnp.stack(pki), jnp.stack(pdn), jnp.stack(pconv),
            jnp.stack(sk), jnp.stack(sv), jnp.stack(ski), jnp.stack(sdn), jnp.stack(sconv), jnp.stack(samlp))
```

```python
import numpy as np
from contextlib import ExitStack, contextmanager
import concourse.bass as bass
import concourse.mybir as mybir
from concourse.bass_utils import run_bass_kernel_spmd

F32 = mybir.dt.float32
BF16 = mybir.dt.bfloat16
I32 = mybir.dt.int32
AF = mybir.ActivationFunctionType
ALU = mybir.AluOpType
AX = mybir.AxisListType

SAME_ENG_SYNC = True
N_DSEM = 20

D = 4096
T = 2048
NT = 16
TS = 4
DEPTH = 2
D_IN = 14960
NPOOL = 1280
EPS = 1e-6
NEG = -1.0e30

SEGS = [("u", 0, 1024), ("v", 1024, 1024), ("za", 2048, 1024), ("qkv", 3072, 6144), ("zb", 9216, 2048),
        ("ab", 11264, 32), ("qc", 11296, 1024), ("kc", 12320, 256), ("vc", 12576, 256), ("zc", 12832, 1024),
        ("qi", 13856, 1024), ("ki", 14880, 64), ("wi", 14944, 16)]
SEG = {n: (s, w) for n, s, w in SEGS}
FMSEG = ["u", "za", "qkv", "qc", "kc", "zc", "qi", "ki"]
TMSEG = ["v", "zb", "ab", "kc", "vc", "ki", "wi"]
FM_ROWS = {}
_r = 0
for _n in FMSEG:
    FM_ROWS[_n] = _r
    _r += max(SEG[_n][1], 128)
FM_TOT = _r
SG = {}
_g = 0
for _n in FMSEG:
    SG[_n] = _g
    _g += max(SEG[_n][1], 128) // 128
NSG = _g


class Buf:
    __slots__ = ("t", "wr", "rd", "name", "excl")

    def __init__(self, t, name="", excl=False):
        self.t = t
        self.wr = {}
        self.rd = {}
        self.name = name
        self.excl = excl

    def __getitem__(self, idx):
        return self.t[idx]


class K:
    def __init__(self):
        self.nc = bass.Bass("TRN2", target_bir_lowering=False)
        self.stack = [ExitStack()]
        nc = self.nc
        self.eng = {"pe": nc.tensor, "act": nc.scalar, "dve": nc.vector, "pool": nc.gpsimd, "sp": nc.sync}
        self.sem = {}
        self.cnt = {}
        es = self.stack[0]
        for e in ("pe", "act", "dve", "pool"):
            self.sem[e] = es.enter_context(nc.semaphore("s_" + e))
            self.cnt[e] = 0
        self.dq = {}
        for q in ("sp", "pool", "act"):
            sems = []
            for i in range(N_DSEM):
                key = "d_%s_%d" % (q, i)
                self.sem[key] = es.enter_context(nc.semaphore(key))
                self.cnt[key] = 0
                sems.append(key)
            self.dq[q] = [sems, 0]
        self.waited = {e: {} for e in ("pe", "act", "dve", "pool", "sp")}
        self.nbuf = 0
        self.banks = []
        self.bank_i = 0
        self.reserved = []
        self.ev_i = 0

    @contextmanager
    def scope(self):
        es = ExitStack()
        self.stack.append(es)
        try:
            yield
        finally:
            self.barrier()
            self.stack.pop()
            es.close()

    def sb(self, shape, dt=F32, name=None):
        self.nbuf += 1
        name = (name or "sb") + "_%d" % self.nbuf
        t = self.stack[-1].enter_context(self.nc.sbuf_tensor(name, list(shape), dt))
        return Buf(t, name)

    def ps(self, shape, dt=F32, name=None):
        self.nbuf += 1
        name = (name or "ps") + "_%d" % self.nbuf
        t = self.stack[-1].enter_context(self.nc.psum_tensor(name, list(shape), dt))
        return Buf(t, name, excl=True)

    def dram(self, name, shape, dt=F32, kind="Internal"):
        t = self.nc.dram_tensor(name, list(shape), dt, kind=kind)
        return Buf(t.ap(), name)

    def bank(self):
        while True:
            b = self.banks[self.bank_i % len(self.banks)]
            self.bank_i += 1
            if b not in self.reserved:
                return b

    def ev(self):
        self.ev_i += 1
        return ("act", "dve")[self.ev_i % 2]

    def _deps(self, reads, writes, merge, e=None):
        deps = {}

        def add(d):
            for kk, c in d.items():
                if deps.get(kk, 0) < c:
                    deps[kk] = c
        for b in reads:
            add(b.wr)
            if b.excl:
                add({kk: c for kk, c in b.rd.items() if kk != e})
        for b in writes:
            add(b.rd)
            if not merge:
                add(b.wr)
        return deps

    def _emit_waits(self, e, deps):
        w = self.waited[e]
        for kk, c in deps.items():
            if kk == e and (e == "pe" or not SAME_ENG_SYNC):
                continue
            if w.get(kk, 0) >= c:
                continue
            self.eng[e].wait_ge(self.sem[kk], c)
            w[kk] = c

    def _mark(self, key, c, reads, writes, merge):
        for b in reads:
            if b.rd.get(key, 0) < c:
                b.rd[key] = c
        for b in writes:
            if merge:
                b.wr[key] = c
            else:
                b.wr = {key: c}
            b.rd = {}

    def op(self, e, fn, reads=(), writes=(), merge=False):
        deps = self._deps(reads, writes, merge, e)
        self._emit_waits(e, deps)
        ins = fn()
        self.cnt[e] += 1
        ins.then_inc(self.sem[e], 1)
        self._mark(e, self.cnt[e], reads, writes, merge)
        return ins

    def dma(self, q, out, in_, reads=(), writes=(), merge=False, indirect=None, elem_off=0, **kw):
        sems, i = self.dq[q]
        key = sems[i % len(sems)]
        self.dq[q][1] = i + 1
        deps = self._deps(reads, writes, merge)
        if self.cnt[key] > 0:
            deps[key] = max(deps.get(key, 0), self.cnt[key])
        self._emit_waits(q, deps)
        if indirect is not None:
            ins = self.nc.gpsimd.indirect_dma_start(out=out, out_offset=None, in_=in_,
                                                    in_offset=bass.IndirectOffsetOnAxis(ap=indirect, axis=0), element_offset=elem_off)
        else:
            ins = self.eng[q].dma_start(out=out, in_=in_, **kw)
        self.cnt[key] += 16
        ins.then_inc(self.sem[key], 16)
        self._mark(key, self.cnt[key], reads, writes, merge)

    def finish(self):
        deps = {kk: c for kk, c in self.cnt.items() if c > 0}
        self._emit_waits("sp", deps)

    def barrier(self):
        snap = {kk: c for kk, c in self.cnt.items() if c > 0}
        for e in ("pe", "act", "dve", "pool", "sp"):
            self._emit_waits(e, {kk: c for kk, c in snap.items() if kk != e})

    def mm(self, out_ap, pairs, reads, bank, merge=False):
        nc = self.nc
        n = len(pairs)

        def fn():
            ins = None
            for i, (l, r) in enumerate(pairs):
                ins = nc.tensor.matmul(out_ap, lhsT=l, rhs=r, start=(i == 0), stop=(i == n - 1))
            return ins
        self.op("pe", fn, reads=reads, writes=[bank], merge=merge)

    def tr(self, out_ap, in_ap, ident_ap, reads, bank, merge=True):
        nc = self.nc
        self.op("pe", lambda: nc.tensor.transpose(out_ap, in_ap, ident_ap), reads=reads, writes=[bank], merge=merge)

    def copy(self, e, out_ap, in_ap, reads, writes, merge=False):
        nc = self.nc
        if e == "act":
            self.op("act", lambda: nc.scalar.copy(out=out_ap, in_=in_ap), reads=reads, writes=writes, merge=merge)
        elif e == "dve":
            self.op("dve", lambda: nc.vector.tensor_copy(out_ap, in_ap), reads=reads, writes=writes, merge=merge)
        else:
            self.op("pool", lambda: nc.gpsimd.tensor_copy(out_ap, in_ap), reads=reads, writes=writes, merge=merge)


def make_consts():
    c = {}
    c["ident"] = np.eye(128, dtype=np.float32)
    i = np.arange(128)
    c["tri_le"] = (i[:, None] <= i[None, :]).astype(np.float32)
    c["neg_lt"] = np.where(i[:, None] < i[None, :], NEG, 0.0).astype(np.float32)
    c["neg_gt"] = np.where(i[:, None] > i[None, :], NEG, 0.0).astype(np.float32)
    c["pos_le"] = np.where(i[:, None] <= i[None, :], -NEG, 0.0).astype(np.float32)
    c["ones"] = np.ones((128, 128), np.float32)
    c["bm16"] = (i[:, None] // 16 == i[None, :]).astype(np.float32)
    for nm, r in (("sel127", 127), ("sel3", 3)):
        sel = np.zeros((128, 128), np.float32)
        sel[r, :] = 1.0
        c[nm] = sel
    names = list(c.keys())
    arr = np.stack([c[n] for n in names], 0)
    return names, arr


CONST_NAMES, CONST_ARR = make_consts()


def build(nl=DEPTH, pb=4, sbn=8, stage="all", with_cache=True):
    k = K()
    nc = k.nc
    NS = sbn * TS
    NR = pb + sbn
    x_p = k.dram("x_p", [pb * T, D], kind="ExternalInput")
    x_s = k.dram("x_s", [NS, D], kind="ExternalInput")
    c_ps = k.dram("c_ps", [NR, D], kind="ExternalInput")
    if with_cache:
        cache_k = k.dram("cache_k", [nl * NPOOL * 8, 16 * 256], kind="ExternalInput")
        cache_v = k.dram("cache_v", [nl * NPOOL * 8, 16 * 256], kind="ExternalInput")
        cache_kidx = k.dram("cache_kidx", [nl * NPOOL, 128 * 64], kind="ExternalInput")
        page_table = k.dram("page_table", [sbn * 128, 1], I32, kind="ExternalInput")
    state_dn = k.dram("state_dn", [nl, sbn * 16, 128, 128], kind="ExternalInput")
    state_conv = k.dram("state_conv", [nl, sbn * 3, 6144], kind="ExternalInput")
    w_ada = k.dram("w_ada", [nl, D, 3 * D], kind="ExternalInput")
    b_ada = k.dram("b_ada", [nl, 3 * D], kind="ExternalInput")
    g_norm = k.dram("g_norm", [nl, D], kind="ExternalInput")
    w_in = k.dram("w_in", [nl, D, D_IN], kind="ExternalInput")
    a_vnorm = k.dram("a_vnorm", [nl, 1024], kind="ExternalInput")
    a_ws = k.dram("a_ws", [nl, 8, 128, 128], kind="ExternalInput")
    a_bs = k.dram("a_bs", [nl, 8 * 128], kind="ExternalInput")
    dn_conv_w = k.dram("dn_conv_w", [nl, 4, 6144], kind="ExternalInput")
    dn_a_log = k.dram("dn_a_log", [nl, 16], kind="ExternalInput")
    dn_dt_bias = k.dram("dn_dt_bias", [nl, 16], kind="ExternalInput")
    dn_onorm = k.dram("dn_onorm", [nl, 128], kind="ExternalInput")
    w_out = k.dram("w_out", [nl, D, D], kind="ExternalInput")
    g_final = k.dram("g_final", [1, D], kind="ExternalInput")
    consts = k.dram("consts", list(CONST_ARR.shape), kind="ExternalInput")

    O = {}
    for name, shape in [("y_p", [pb * T, D]), ("y_s", [NS, D]), ("p_k", [nl, pb * T, 256]), ("p_v", [nl, pb * T, 256]),
                        ("p_kidx", [nl, pb * T, 64]), ("p_dn", [nl, pb * 16, 128, 128]), ("p_conv", [nl, pb * 3, 6144]),
                        ("s_k", [nl, NS, 256]), ("s_v", [nl, NS, 256]), ("s_kidx", [nl, NS, 64]),
                        ("s_dn", [nl, sbn * 16, 128, 128]), ("s_conv", [nl, sbn * 3, 6144]), ("s_amlp_v", [nl, NS, 1024])]:
        O[name] = k.dram(name, shape, kind="ExternalOutput")

    m_scr = k.dram("m_scr", [NR, 3 * D])
    P_fm = k.dram("P_fm", [FM_TOT, T])
    P_v = k.dram("P_v", [T, 1024])
    P_zb = k.dram("P_zb", [T, 2048])
    P_ab = k.dram("P_ab", [T, 32])
    P_wi = k.dram("P_wi", [T, 16])
    sTM = k.dram("sTM", [NS, D_IN])
    xa = [k.dram("xa_p", [pb * T, D]), k.dram("xa_s", [NS, D])]
    xb = [k.dram("xb_p", [pb * T, D]), k.dram("xb_s", [NS, D])]
    mixT = k.dram("mixT", [D, T], BF16)

    CT = {}
    for i, n in enumerate(CONST_NAMES):
        CT[n] = k.sb([128, 128], F32, "c_" + n)
        k.dma("sp", CT[n][:], consts.t[i], reads=[consts], writes=[CT[n]])
    ident = CT["ident"]
    ones = CT["ones"]
    k.banks = [k.ps([128, 512], F32, "bank") for _ in range(8)]
    wbuf = [k.sb([128, 32, 256], BF16, "wbuf") for _ in range(2)]
    wb_i = [0]
    stage_bufs = [k.sb([128, 512], F32, "stage") for _ in range(4)]
    st_i = [0]

    def next_w():
        b = wbuf[wb_i[0] % 2]
        wb_i[0] += 1
        return b

    def next_stage():
        b = stage_bufs[st_i[0] % len(stage_bufs)]
        st_i[0] += 1
        return b

    def evac_to_dram(bank, P, W, dst_ap, dst_buf, extra=()):
        st = next_stage()
        k.copy(k.ev(), st[0:P, 0:W], bank[0:P, 0:W], reads=[bank], writes=[st])
        k.dma("sp", dst_ap, st[0:P, 0:W], reads=[st], writes=[dst_buf], merge=True)
        for ap, buf in extra:
            k.dma("sp", ap, st[0:P, 0:W], reads=[st], writes=[buf], merge=True)

    def A(e, f, reads, writes, merge=False):
        return k.op(e, f, reads=reads, writes=writes, merge=merge)

    def rstd_from_ss(ss_ap, out_ap, n, buf, P):
        A("act", lambda: nc.scalar.activation(out=out_ap, in_=ss_ap, func=AF.Sqrt, bias=EPS, scale=1.0 / n), [buf], [buf])
        A("dve", lambda: nc.vector.reciprocal(out_ap, out_ap), [buf], [buf])

    mixTs = k.sb([128, 32, NS], BF16, "mixTs")
    sFM = k.sb([128, NSG, NS], F32, "sFM")

    def adaln(l):
        with k.scope():
            craw = k.sb([128, NR, 32], F32, "craw")
            cbf = k.sb([128, NR, 32], BF16, "cbf")
            k.dma("sp", craw[:], c_ps.t.rearrange("r (p c) -> p r c", c=32), reads=[c_ps], writes=[craw])
            A("act", lambda: nc.scalar.activation(out=cbf[:], in_=craw[:], func=AF.Silu), [craw], [cbf])
            wv = w_ada.t[l].rearrange("(p c) n -> p c n", c=32)
            for blk in range(3 * D // 256):
                wb = next_w()
                k.dma("pool", wb[:], wv[:, :, blk * 256:(blk + 1) * 256], reads=[w_ada], writes=[wb])
                bank = k.bank()
                k.mm(bank[0:NR, 0:256], [(cbf[:, :, c], wb[:, c, :]) for c in range(32)], reads=[cbf, wb], bank=bank)
                evac_to_dram(bank, NR, 256, m_scr.t[:, blk * 256:(blk + 1) * 256], m_scr)

    def mod_cols(l, Acol, Bcol):
        with k.scope():
            nrow = 2 * NR + 3
            rows = k.sb([32, nrow, 128], F32, "rows")
            for r in range(NR):
                for j in range(2):
                    k.dma("sp", rows[:, r * 2 + j, :], m_scr.t[r, j * D:(j + 1) * D].rearrange("(c p) -> c p", p=128),
                          reads=[m_scr], writes=[rows], merge=True)
            for j in range(2):
                k.dma("sp", rows[:, 2 * NR + j, :], b_ada.t[l, j * D:(j + 1) * D].rearrange("(c p) -> c p", p=128),
                      reads=[b_ada], writes=[rows], merge=True)
            k.dma("sp", rows[:, 2 * NR + 2, :], g_norm.t[l].rearrange("(c p) -> c p", p=128), reads=[g_norm], writes=[rows], merge=True)
            cols = k.sb([128, nrow, 32], F32, "cols")
            for j0 in range(0, nrow, 16):
                bank = k.bank()
                n = min(16, nrow - j0)
                for j in range(n):
                    k.tr(bank[:, j * 32:(j + 1) * 32], rows[:, j0 + j, :], ident[0:32, 0:32], reads=[rows, ident], bank=bank, merge=(j > 0))
                k.copy("dve", cols[:, j0:j0 + n, :].rearrange("p a b -> p (a b)"), bank[:, 0:n * 32], reads=[bank], writes=[cols], merge=True)
            for r in range(NR):
                A("dve", lambda: nc.vector.tensor_tensor(Bcol[:, r, :], cols[:, r * 2 + 0, :], cols[:, 2 * NR, :], op=ALU.add), [cols], [Bcol], True)
                A("dve", lambda: nc.vector.scalar_tensor_tensor(out=Acol[:, r, :], in0=cols[:, r * 2 + 1, :], scalar=1.0,
                                                                 in1=cols[:, 2 * NR + 1, :], op0=ALU.add, op1=ALU.add), [cols], [Acol], True)
                A("dve", lambda: nc.vector.tensor_tensor(Acol[:, r, :], Acol[:, r, :], cols[:, 2 * NR + 2, :], op=ALU.mult), [cols, Acol], [Acol], True)

    def h_phase(xin, row0, ntile, P, hT, Acol, Bcol, rows_of):
        with k.scope():
            xt = k.sb([128, D], F32, "xt")
            junk = k.sb([128, 1024], BF16, "junk")
            ssq = k.sb([128, 8], F32, "ssq")
            for ti in range(ntile):
                k.dma("act", xt[0:P, :], xin.t[row0 + ti * P:row0 + (ti + 1) * P, :], reads=[xin], writes=[xt])
                for hh in range(4):
                    A("act", lambda: nc.scalar.activation(out=junk[0:P, :], in_=xt[0:P, hh * 1024:(hh + 1) * 1024],
                                                          func=AF.Square, accum_out=ssq[0:P, hh:hh + 1]), [xt], [junk, ssq])
                A("dve", lambda: nc.vector.tensor_reduce(out=ssq[0:P, 4:5], in_=ssq[0:P, 0:4], axis=AX.X, op=ALU.add), [ssq], [ssq])
                rstd_from_ss(ssq[0:P, 4:5], ssq[0:P, 5:6], D, ssq, P)
                A("dve", lambda: nc.vector.tensor_scalar(xt[0:P, :], xt[0:P, :], ssq[0:P, 5:6], None, op0=ALU.mult), [xt, ssq], [xt])
                for c4 in range(8):
                    bank = k.bank()
                    for j in range(4):
                        c = c4 * 4 + j
                        k.tr(bank[:, j * 128:j * 128 + P], xt[0:P, c * 128:(c + 1) * 128], ident[0:P, 0:P],
                             reads=[xt, ident], bank=bank, merge=(j > 0))
                    eng_ = k.ev()
                    for j in range(4):
                        c = c4 * 4 + j
                        for (p0, p1, r) in rows_of(ti, P):
                            dst = hT[:, c, ti * P + p0:ti * P + p1]
                            src = bank[:, j * 128 + p0:j * 128 + p1]
                            if eng_ == "act":
                                A("act", lambda: nc.scalar.activation(out=dst, in_=src, func=AF.Identity,
                                                                      bias=Bcol[:, r, c:c + 1], scale=Acol[:, r, c:c + 1]),
                                  [bank, Acol, Bcol], [hT], True)
                            else:
                                A("dve", lambda: nc.vector.tensor_scalar(dst, src, Acol[:, r, c:c + 1], Bcol[:, r, c:c + 1],
                                                                         op0=ALU.mult, op1=ALU.add), [bank, Acol, Bcol], [hT], True)

    def inproj(l, hT, ntok, prompt, b):
        wv = w_in.t[l].rearrange("(c p) n -> p c n", p=128)
        for name, s0, sw in SEGS:
            for off in range(0, sw, 256):
                w = min(256, sw - off)
                col0 = s0 + off
                wb = next_w()
                k.dma("pool", wb[:, :, 0:w], wv[:, :, col0:col0 + w], reads=[w_in], writes=[wb])
                if name == "ki":
                    k.dma("pool", wb[:, :, 64:128], wv[:, :, col0:col0 + w], reads=[w_in], writes=[wb], merge=True)
                if not prompt:
                    bank = k.bank()
                    k.mm(bank[0:NS, 0:w], [(hT[:, c, :], wb[:, c, 0:w]) for c in range(32)], reads=[hT, wb], bank=bank)
                    extra = []
                    if name == "kc":
                        extra.append((O["s_k"].t[l, :, :], O["s_k"]))
                    if name == "vc":
                        extra.append((O["s_v"].t[l, :, :], O["s_v"]))
                    if name == "ki":
                        extra.append((O["s_kidx"].t[l, :, :], O["s_kidx"]))
                    evac_to_dram(bank, NS, w, sTM.t[:, col0:col0 + w], sTM, extra)
                    if name == "qkv":
                        for j in range(sbn):
                            k.dma("sp", O["s_conv"].t[l, j * 3:(j + 1) * 3, off:off + w], sTM.t[j * 4 + 1:j * 4 + 4, col0:col0 + w],
                                  reads=[sTM], writes=[O["s_conv"]], merge=True)
                    if name in FMSEG:
                        wfm = 128 if name == "ki" else w
                        for g0 in range(0, wfm, 128):
                            bank = k.bank()
                            k.mm(bank[:, 0:NS], [(wb[:, c, g0:g0 + 128], hT[:, c, :]) for c in range(32)], reads=[hT, wb], bank=bank)
                            sg = SG[name] + (off + g0) // 128
                            k.copy(k.ev(), sFM[:, sg, :], bank[:, 0:NS], reads=[bank], writes=[sFM], merge=True)
                    continue
                if name == "qkv":
                    bank = k.bank()
                    k.mm(bank[0:3, 0:w], [(hT[:, c, T - 3:T], wb[:, c, 0:w]) for c in range(32)], reads=[hT, wb], bank=bank)
                    evac_to_dram(bank, 3, w, O["p_conv"].t[l, b * 3:(b + 1) * 3, off:off + w], O["p_conv"])
                if name in TMSEG:
                    for ti in range(NT):
                        bank = k.bank()
                        k.mm(bank[:, 0:w], [(hT[:, c, ti * 128:(ti + 1) * 128], wb[:, c, 0:w]) for c in range(32)],
                             reads=[hT, wb], bank=bank)
                        rs = slice(ti * 128, (ti + 1) * 128)
                        ors = slice(b * T + ti * 128, b * T + (ti + 1) * 128)
                        if name == "v":
                            dst, dbuf = P_v.t[rs, off:off + w], P_v
                        elif name == "zb":
                            dst, dbuf = P_zb.t[rs, off:off + w], P_zb
                        elif name == "ab":
                            dst, dbuf = P_ab.t[rs, :], P_ab
                        elif name == "wi":
                            dst, dbuf = P_wi.t[rs, :], P_wi
                        elif name == "kc":
                            dst, dbuf = O["p_k"].t[l, ors, :], O["p_k"]
                        elif name == "vc":
                            dst, dbuf = O["p_v"].t[l, ors, :], O["p_v"]
                        else:
                            dst, dbuf = O["p_kidx"].t[l, ors, :], O["p_kidx"]
                        evac_to_dram(bank, 128, w, dst, dbuf)
                if name in FMSEG:
                    wfm = 128 if name == "ki" else w
                    for g0 in range(0, wfm, 128):
                        row0 = FM_ROWS[name] + off + g0
                        for tb in range(4):
                            bank = k.bank()
                            k.mm(bank[:, :], [(wb[:, c, g0:g0 + 128], hT[:, c, tb * 512:(tb + 1) * 512]) for c in range(32)],
                                 reads=[hT, wb], bank=bank)
                            evac_to_dram(bank, 128, 512, P_fm.t[row0:row0 + 128, tb * 512:(tb + 1) * 512], P_fm)

    def mixerA_setup(l, wmT, absr, avn):
        with k.scope():
            wnat = k.sb([128, 8, 128], F32, "wnat")
            k.dma("sp", wnat[:], a_ws.t[l].rearrange("h t s -> t h s"), reads=[a_ws], writes=[wnat])
            for hg in range(2):
                bank = k.bank()
                for hh in range(4):
                    k.tr(bank[:, hh * 128:(hh + 1) * 128], wnat[:, hg * 4 + hh, :], ident[:, :], reads=[wnat, ident], bank=bank, merge=(hh > 0))
                for hh in range(4):
                    A("dve", lambda: nc.vector.tensor_tensor(wmT[:, hg * 4 + hh, :], bank[:, hh * 128:(hh + 1) * 128], CT["tri_le"][:, :], op=ALU.mult),
                      [bank, CT["tri_le"]], [wmT], True)
        k.dma("sp", absr[0:1, :], a_bs.t[l:l + 1, :], reads=[a_bs], writes=[absr])
        k.dma("sp", avn[:], a_vnorm.t[l:l + 1, :].to_broadcast([128, 1024]), reads=[a_vnorm], writes=[avn])

    def mixerA_chunk(l, C, vt, uT, zaT, wmT, absr, avn, tmp, out_cb, vout_cb=None):
        (uT_ap, uT_buf), (zaT_ap, zaT_buf) = uT, zaT
        g1, dd, st4, gu, sz, yb = tmp
        A("act", lambda: nc.scalar.activation(out=g1[0:C, :], in_=vt[0:C, :], func=AF.Gelu_apprx_tanh, accum_out=st4[0:C, 0:1]), [vt], [g1, st4])
        A("dve", lambda: nc.vector.tensor_scalar(st4[0:C, 1:2], st4[0:C, 0:1], -1.0 / 1024, None, op0=ALU.mult), [st4], [st4])
        A("dve", lambda: nc.vector.tensor_scalar(dd[0:C, :], g1[0:C, :], st4[0:C, 1:2], None, op0=ALU.add), [g1, st4], [dd])
        A("act", lambda: nc.scalar.activation(out=g1[0:C, :], in_=dd[0:C, :], func=AF.Square, accum_out=st4[0:C, 2:3]), [dd], [g1, st4])
        rstd_from_ss(st4[0:C, 2:3], st4[0:C, 3:4], 1024, st4, C)
        A("dve", lambda: nc.vector.scalar_tensor_tensor(out=dd[0:C, :], in0=dd[0:C, :], scalar=st4[0:C, 3:4], in1=avn[0:C, :],
                                                         op0=ALU.mult, op1=ALU.mult), [dd, st4, avn], [dd])
        if vout_cb is not None:
            vout_cb(dd)
        A("act", lambda: nc.scalar.activation(out=gu[:, :, 0:C], in_=uT_ap, func=AF.Gelu_apprx_tanh), [uT_buf], [gu])
        A("act", lambda: nc.scalar.activation(out=sz[:, :, 0:C], in_=zaT_ap, func=AF.Silu), [zaT_buf], [sz])
        for hg in range(2):
            bank = k.bank()
            for hh in range(4):
                h = hg * 4 + hh
                k.mm(bank[:, hh * C:(hh + 1) * C], [(dd[0:C, h * 128:(h + 1) * 128], wmT[0:C, h, 0:C]),
                                                   (ones[0:1, :], absr[0:1, h * 128:h * 128 + C])],
                     reads=[dd, wmT, ones, absr], bank=bank, merge=(hh > 0))
            A("dve", lambda: nc.vector.tensor_tensor(gu[:, hg * 4:(hg + 1) * 4, 0:C], gu[:, hg * 4:(hg + 1) * 4, 0:C],
                                                     bank[:, 0:4 * C].rearrange("p (h c) -> p h c", c=C), op=ALU.mult), [gu, bank], [gu])
        A("dve", lambda: nc.vector.tensor_tensor(yb[:, :, 0:C], gu[:, :, 0:C], sz[:, :, 0:C], op=ALU.mult), [gu, sz], [yb])
        out_cb(yb)

    def mixerA_prompt(l):
        with k.scope():
            wmT = k.sb([128, 8, 128], F32, "wmT")
            absr = k.sb([1, 1024], F32, "absr")
            avn = k.sb([128, 1024], F32, "avn")
            mixerA_setup(l, wmT, absr, avn)
            vt = [k.sb([128, 1024], F32, "vt") for _ in range(2)]
            uT = [k.sb([128, 8, 128], F32, "uT") for _ in range(2)]
            zaT = [k.sb([128, 8, 128], F32, "zaT") for _ in range(2)]
            tmp = (k.sb([128, 1024], F32, "g1"), k.sb([128, 1024], F32, "dd"), k.sb([128, 4], F32, "st4"),
                   k.sb([128, 8, 128], F32, "gu"), k.sb([128, 8, 128], F32, "sz"), k.sb([128, 8, 128], BF16, "yb"))
            for ci in range(NT):
                ts = slice(ci * 128, (ci + 1) * 128)
                v_, u_, z_ = vt[ci % 2], uT[ci % 2], zaT[ci % 2]
                k.dma("sp", v_[:], P_v.t[ts, :], reads=[P_v], writes=[v_])
                r0 = FM_ROWS["u"]
                k.dma("act", u_[:], P_fm.t[r0:r0 + 1024, ts].rearrange("(h d) t -> d h t", d=128), reads=[P_fm], writes=[u_])
                r0 = FM_ROWS["za"]
                k.dma("act", z_[:], P_fm.t[r0:r0 + 1024, ts].rearrange("(h d) t -> d h t", d=128), reads=[P_fm], writes=[z_])

                def out_cb(yb, ts=ts):
                    k.dma("sp", mixT.t[0:1024, ts].rearrange("(h d) t -> d h t", d=128), yb[:], reads=[yb], writes=[mixT], merge=True)
                mixerA_chunk(l, 128, v_, (u_[:], u_), (z_[:], z_), wmT, absr, avn, tmp, out_cb)

    def mixerA_sample(l):
        with k.scope():
            wmT = k.sb([128, 8, 128], F32, "wmT")
            absr = k.sb([1, 1024], F32, "absr")
            avn = k.sb([128, 1024], F32, "avn")
            mixerA_setup(l, wmT, absr, avn)
            vt = k.sb([TS, 1024], F32, "vt")
            tmp = (k.sb([TS, 1024], F32, "g1"), k.sb([TS, 1024], F32, "dd"), k.sb([TS, 4], F32, "st4"),
                   k.sb([128, 8, TS], F32, "gu"), k.sb([128, 8, TS], F32, "sz"), k.sb([128, 8, TS], BF16, "yb"))
            for j in range(sbn):
                ts = slice(j * TS, (j + 1) * TS)
                k.dma("sp", vt[:], sTM.t[ts, 1024:2048], reads=[sTM], writes=[vt])

                def out_cb(yb, ts=ts):
                    A("dve", lambda: nc.vector.tensor_copy(mixTs[:, 0:8, ts], yb[:]), [yb], [mixTs], True)

                def vout_cb(dd, ts=ts):
                    k.dma("sp", O["s_amlp_v"].t[l, ts, :], dd[0:TS, :], reads=[dd], writes=[O["s_amlp_v"]], merge=True)
                mixerA_chunk(l, TS, vt, (sFM[:, SG["u"]:SG["u"] + 8, ts], sFM), (sFM[:, SG["za"]:SG["za"] + 8, ts], sFM),
                             wmT, absr, avn, tmp, out_cb, vout_cb)

    def mixerB(l, prompt, bj):
        L = T if prompt else TS
        C = 128 if prompt else TS
        nch = L // C
        sel = CT["sel127"] if prompt else CT["sel3"]
        nsq = 6 if prompt else 1
        with k.scope():
            cw_nat = k.sb([4, 6144], F32, "cw_nat")
            k.dma("sp", cw_nat[:], dn_conv_w.t[l], reads=[dn_conv_w], writes=[cw_nat])
            convw = k.sb([128, 48, 4], F32, "convw")
            bank = k.bank()
            for g in range(48):
                k.tr(bank[:, g * 4:(g + 1) * 4], cw_nat[0:4, g * 128:(g + 1) * 128], ident[0:4, 0:4], reads=[cw_nat, ident], bank=bank, merge=(g > 0))
            k.copy("dve", convw[:].rearrange("p g j -> p (g j)"), bank[:, 0:192], reads=[bank], writes=[convw])
            hp = k.sb([128, 3, 16], F32, "hp")
            k.dma("sp", hp[:, 0, :], dn_a_log.t[l:l + 1, :].to_broadcast([128, 16]), reads=[dn_a_log], writes=[hp], merge=True)
            k.dma("sp", hp[:, 1, :], dn_dt_bias.t[l:l + 1, :].to_broadcast([128, 16]), reads=[dn_dt_bias], writes=[hp], merge=True)
            A("act", lambda: nc.scalar.activation(out=hp[:, 2, :], in_=hp[:, 0, :], func=AF.Exp), [hp], [hp])
            A("dve", lambda: nc.vector.tensor_scalar(hp[:, 2, :], hp[:, 2, :], -1.0, None, op0=ALU.mult), [hp], [hp])
            onb = k.sb([128, 128], F32, "onb")
            k.dma("sp", onb[:], dn_onorm.t[l:l + 1, :].to_broadcast([128, 128]), reads=[dn_onorm], writes=[onb])
            abt = k.sb([128, nch, 32], F32, "abt")
            if prompt:
                k.dma("sp", abt[:], P_ab.t.rearrange("(n c) f -> c n f", c=128), reads=[P_ab], writes=[abt])
            else:
                k.dma("sp", abt[0:C, 0, :], sTM.t[bj * TS:(bj + 1) * TS, SEG["ab"][0]:SEG["ab"][0] + 32], reads=[sTM], writes=[abt])
            gt = k.sb([128, nch, 16], F32, "gt")
            beta = k.sb([128, nch, 16], F32, "beta")
            Gt = k.sb([128, nch, 16], F32, "Gt")
            eG = k.sb([128, nch, 16], F32, "eG")
            eGd = k.sb([128, nch, 16], F32, "eGd")
            eGl = k.sb([128, nch, 16], F32, "eGl")
            bE = k.sb([128, nch, 16], F32, "bE")
            nbeta = k.sb([128, nch, 16], F32, "nbeta")
            for n in range(nch):
                A("dve", lambda: nc.vector.tensor_tensor(gt[0:C, n, :], abt[0:C, n, 0:16], hp[0:C, 1, :], op=ALU.add), [abt, hp], [gt], True)
            A("act", lambda: nc.scalar.activation(out=gt[0:C], in_=gt[0:C], func=AF.Exp), [gt], [gt])
            A("act", lambda: nc.scalar.activation(out=gt[0:C], in_=gt[0:C], func=AF.Ln, bias=1.0, scale=1.0), [gt], [gt])
            for n in range(nch):
                A("dve", lambda: nc.vector.tensor_tensor(gt[0:C, n, :], gt[0:C, n, :], hp[0:C, 2, :], op=ALU.mult), [gt, hp], [gt], True)
            A("act", lambda: nc.scalar.activation(out=beta[0:C], in_=abt[0:C, :, 16:32], func=AF.Sigmoid), [abt], [beta])
            A("dve", lambda: nc.vector.tensor_scalar(nbeta[0:C], beta[0:C], -1.0, None, op0=ALU.mult), [beta], [nbeta])
            bank = k.bank()
            k.mm(bank[0:C, 0:nch * 16], [(CT["tri_le"][0:C, 0:C], gt[0:C].rearrange("p n h -> p (n h)"))], reads=[CT["tri_le"], gt], bank=bank)
            k.copy("dve", Gt[0:C].rearrange("p n h -> p (n h)"), bank[0:C, 0:nch * 16], reads=[bank], writes=[Gt])
            A("act", lambda: nc.scalar.activation(out=eG[0:C], in_=Gt[0:C], func=AF.Exp), [Gt], [eG])
            A("dve", lambda: nc.vector.tensor_tensor(bE[0:C], eG[0:C], beta[0:C], op=ALU.mult), [eG, beta], [bE])
            bank = k.bank()
            k.mm(bank[:, 0:nch * 16], [(sel[0:C, :], Gt[0:C].rearrange("p n h -> p (n h)"))], reads=[sel, Gt], bank=bank)
            A("act", lambda: nc.scalar.activation(out=eGl[:].rearrange("p n h -> p (n h)"), in_=bank[:, 0:nch * 16], func=AF.Exp), [bank], [eGl])
            A("dve", lambda: nc.vector.tensor_tensor(eGd[0:C].rearrange("p n h -> p (n h)"), bank[0:C, 0:nch * 16],
                                                     Gt[0:C].rearrange("p n h -> p (n h)"), op=ALU.subtract), [bank, Gt], [eGd])
            A("act", lambda: nc.scalar.activation(out=eGd[0:C], in_=eGd[0:C], func=AF.Exp), [eGd], [eGd])

            xp = [k.sb([128, 3 + L], F32, "xp") for _ in range(3)]
            qkv = [k.sb([128, L], F32, "qkvc") for _ in range(3)]
            sq = k.sb([128, min(L, 512)], F32, "sq")
            rs = k.sb([128, min(L, 512)], F32, "rs")
            S = k.sb([128, 128], F32, "S")
            zb_t = k.sb([128, 128], F32, "zb_t")
            U = [dict(kv=k.sb([128, 256], F32, "kv"), vb=k.sb([128, 128], F32, "vb"), kbg=k.sb([128, 128], F32, "kbg"),
                      kg=k.sb([128, 128], F32, "kg"), dg=k.sb([128, 128], F32, "dg"), z1=k.sb([128, 128], F32, "z1"),
                      z2=k.sb([128, 128], F32, "z2"), X=[k.sb([128, 128], F32, "X") for _ in range(2)],
                      XT=[k.sb([128, 128], F32, "XT") for _ in range(2)], R=[k.sb([128, 128], F32, "R") for _ in range(2)],
                      attnT=k.sb([128, 128], F32, "attnT"), u=k.sb([128, 128], F32, "u"), wT=k.sb([128, 128], F32, "wT"),
                      vnew=k.sb([128, 128], F32, "vnew"), o1=k.sb([128, 128], F32, "o1"), o=k.sb([128, 128], F32, "o"),
                      st=k.sb([128, 4], F32, "ost"), y=k.sb([128, 128], F32, "y"), yT=k.sb([128, 128], BF16, "yT"))
                 for _ in range(2)]
            ui = 0
            qrow = FM_ROWS["qkv"]
            for h in range(16):
                for i3 in range(3):
                    ch0 = i3 * 2048 + h * 128
                    if prompt:
                        A("pool", lambda: nc.gpsimd.memset(xp[i3][:, 0:3], 0.0), [], [xp[i3]])
                        k.dma("sp", xp[i3][:, 3:3 + L], P_fm.t[qrow + ch0:qrow + ch0 + 128, :], reads=[P_fm], writes=[xp[i3]], merge=True)
                    else:
                        sg = SG["qkv"] + ch0 // 128
                        k.dma("sp", xp[i3][:, 0:3], state_conv.t[l, bj * 3:(bj + 1) * 3, ch0:ch0 + 128].rearrange("j d -> d j"),
                              reads=[state_conv], writes=[xp[i3]], allow_slow_non_contiguous=True)
                        A("dve", lambda: nc.vector.tensor_copy(xp[i3][:, 3:3 + L], sFM[:, sg, bj * TS:(bj + 1) * TS]), [sFM], [xp[i3]], True)
                    g = ch0 // 128
                    o_ = qkv[i3]
                    A("dve", lambda: nc.vector.tensor_scalar(o_[:, :], xp[i3][:, 0:L], convw[:, g, 0:1], None, op0=ALU.mult), [xp[i3], convw], [o_])
                    for j in range(1, 4):
                        A("dve",
                          lambda: nc.vector.scalar_tensor_tensor(out=o_[:, :], in0=xp[i3][:, j:j + L], scalar=convw[:, g, j:j + 1],
                                                                                              in1=o_[:, :], op0=ALU.mult, op1=ALU.add),
                          [xp[i3], convw, o_], [o_])
                    A("act", lambda: nc.scalar.activation(out=o_[:, :], in_=o_[:, :], func=AF.Silu), [o_], [o_])
                for i3 in range(2):
                    o_ = qkv[i3]
                    for t0 in range(0, L, 512):
                        wd = min(512, L - t0)
                        A("act", lambda: nc.scalar.activation(out=sq[:, 0:wd], in_=o_[:, t0:t0 + wd], func=AF.Square), [o_], [sq])
                        bank = k.bank()
                        k.mm(bank[:, 0:wd], [(ones[:, :], sq[:, 0:wd])], reads=[ones, sq], bank=bank)
                        A("act", lambda: nc.scalar.activation(out=rs[:, 0:wd], in_=bank[:, 0:wd], func=AF.Sqrt, bias=EPS, scale=1.0), [bank], [rs])
                        A("dve", lambda: nc.vector.reciprocal(rs[:, 0:wd], rs[:, 0:wd]), [rs], [rs])
                        if i3 == 0:
                            A("dve", lambda: nc.vector.scalar_tensor_tensor(out=o_[:, t0:t0 + wd], in0=o_[:, t0:t0 + wd], scalar=128.0 ** -0.5,
                                                                             in1=rs[:, 0:wd], op0=ALU.mult, op1=ALU.mult), [o_, rs], [o_])
                        else:
                            A("dve", lambda: nc.vector.tensor_tensor(o_[:, t0:t0 + wd], o_[:, t0:t0 + wd], rs[:, 0:wd], op=ALU.mult), [o_, rs], [o_])
                qT, kT, vT = qkv
                if prompt:
                    A("pool", lambda: nc.gpsimd.memset(S[:], 0.0), [], [S])
                else:
                    k.dma("sp", S[:], state_dn.t[l, bj * 16 + h], reads=[state_dn], writes=[S])
                def par(n, u_):
                    cs = slice(n * C, (n + 1) * C)
                    col = lambda tl: tl[0:C, n, h:h + 1]
                    X, XT, R = u_["X"], u_["XT"], u_["R"]
                    bank = k.bank()
                    k.tr(bank[0:C, 0:128], kT[:, cs], ident[:, :], reads=[kT, ident], bank=bank, merge=False)
                    k.tr(bank[0:C, 128:256], vT[:, cs], ident[:, :], reads=[vT, ident], bank=bank, merge=True)
                    k.copy("act", u_["kv"][0:C, :], bank[0:C, 0:256], reads=[bank], writes=[u_["kv"]])
                    A("pool", lambda: nc.gpsimd.tensor_scalar(u_["vb"][0:C, :], u_["kv"][0:C, 128:256], col(beta), None, op0=ALU.mult), [u_["kv"], beta], [u_["vb"]])
                    A("pool", lambda: nc.gpsimd.tensor_scalar(u_["kbg"][0:C, :], u_["kv"][0:C, 0:128], col(bE), None, op0=ALU.mult), [u_["kv"], bE], [u_["kbg"]])
                    A("pool", lambda: nc.gpsimd.tensor_scalar(u_["kg"][0:C, :], u_["kv"][0:C, 0:128], col(eGd), None, op0=ALU.mult), [u_["kv"], eGd], [u_["kg"]])
                    A("pool", lambda: nc.gpsimd.tensor_scalar(u_["dg"][0:C, 0:C], ident[0:C, 0:C], col(Gt), None, op0=ALU.mult), [ident, Gt], [u_["dg"]])
                    bk = k.bank()
                    k.mm(bk[0:C, 0:C], [(kT[:, cs], kT[:, cs])], reads=[kT], bank=bk)
                    k.mm(bk[0:C, 128:128 + C], [(kT[:, cs], qT[:, cs])], reads=[kT, qT], bank=bk, merge=True)
                    k.mm(bk[0:C, 256:256 + C], [(ones[0:C, 0:C], u_["dg"][0:C, 0:C])], reads=[ones, u_["dg"]], bank=bk, merge=True)
                    A("dve", lambda: nc.vector.scalar_tensor_tensor(out=u_["z1"][0:C, 0:C], in0=bk[0:C, 256:256 + C], scalar=col(Gt),
                                                                     in1=CT["pos_le"][0:C, 0:C], op0=ALU.subtract, op1=ALU.add), [bk, Gt, CT["pos_le"]], [u_["z1"]])
                    A("dve", lambda: nc.vector.scalar_tensor_tensor(out=u_["z2"][0:C, 0:C], in0=bk[0:C, 256:256 + C], scalar=col(Gt),
                                                                     in1=CT["neg_gt"][0:C, 0:C], op0=ALU.subtract, op1=ALU.add), [bk, Gt, CT["neg_gt"]], [u_["z2"]])
                    A("act", lambda: nc.scalar.activation(out=u_["z1"][0:C, 0:C], in_=u_["z1"][0:C, 0:C], func=AF.Exp, scale=-1.0), [u_["z1"]], [u_["z1"]])
                    A("act", lambda: nc.scalar.activation(out=u_["z2"][0:C, 0:C], in_=u_["z2"][0:C, 0:C], func=AF.Exp), [u_["z2"]], [u_["z2"]])
                    X, XT, R = u_["X"], u_["XT"], u_["R"]
                    A("dve", lambda: nc.vector.scalar_tensor_tensor(out=XT[0][0:C, 0:C], in0=bk[0:C, 0:C], scalar=col(nbeta), in1=u_["z1"][0:C, 0:C],
                                                                     op0=ALU.mult, op1=ALU.mult), [bk, nbeta, u_["z1"]], [XT[0]])
                    A("dve", lambda: nc.vector.tensor_tensor(u_["attnT"][0:C, 0:C], bk[0:C, 128:128 + C], u_["z2"][0:C, 0:C], op=ALU.mult), [bk, u_["z2"]], [u_["attnT"]])
                    b2 = k.bank()
                    k.tr(b2[0:C, 0:C], XT[0][0:C, 0:C], ident[0:C, 0:C], reads=[XT[0], ident], bank=b2, merge=False)
                    k.copy("act", X[0][0:C, 0:C], b2[0:C, 0:C], reads=[b2], writes=[X[0]])
                    A("dve", lambda: nc.vector.tensor_tensor(R[0][0:C, 0:C], b2[0:C, 0:C], ident[0:C, 0:C], op=ALU.add), [b2, ident], [R[0]])
                    cur = 0
                    for kk in range(1, nsq + 1):
                        nx = 1 - cur
                        b3 = k.bank()
                        last = (kk == nsq)
                        k.mm(b3[0:C, 0:C], [(X[cur][0:C, 0:C], XT[cur][0:C, 0:C])], reads=[X[cur], XT[cur]], bank=b3)
                        if not last:
                            k.mm(b3[0:C, 128:128 + C], [(XT[cur][0:C, 0:C], X[cur][0:C, 0:C])], reads=[X[cur], XT[cur]], bank=b3, merge=True)
                        k.copy("act", XT[nx][0:C, 0:C], b3[0:C, 0:C], reads=[b3], writes=[XT[nx]])
                        if not last:
                            k.copy("dve", X[nx][0:C, 0:C], b3[0:C, 128:128 + C], reads=[b3], writes=[X[nx]])
                        b4 = k.bank()
                        k.mm(b4[0:C, 0:C], [(XT[nx][0:C, 0:C], R[cur][0:C, 0:C])], reads=[XT[nx], R[cur]], bank=b4)
                        A("dve", lambda: nc.vector.tensor_tensor(R[nx][0:C, 0:C], b4[0:C, 0:C], R[cur][0:C, 0:C], op=ALU.add), [b4, R[cur]], [R[nx]])
                        cur = nx
                    TT = R[cur]
                    b5 = k.bank()
                    k.mm(b5[0:C, 0:128], [(TT[0:C, 0:C], u_["vb"][0:C, :])], reads=[TT, u_["vb"]], bank=b5)
                    k.mm(b5[:, 128:128 + C], [(u_["kbg"][0:C, :], TT[0:C, 0:C])], reads=[TT, u_["kbg"]], bank=b5, merge=True)
                    k.copy("act", u_["u"][0:C, :], b5[0:C, 0:128], reads=[b5], writes=[u_["u"]])
                    k.copy("dve", u_["wT"][:, 0:C], b5[:, 128:128 + C], reads=[b5], writes=[u_["wT"]])
                    u_["TT"] = TT
                def seq(n, u_):
                    cs = slice(n * C, (n + 1) * C)
                    col = lambda tl: tl[0:C, n, h:h + 1]
                    b6 = k.bank()
                    k.mm(b6[0:C, 0:128], [(u_["wT"][:, 0:C], S[:, :])], reads=[u_["wT"], S], bank=b6)
                    k.mm(b6[0:C, 128:256], [(qT[:, cs], S[:, :])], reads=[qT, S], bank=b6, merge=True)
                    A("dve", lambda: nc.vector.tensor_tensor(u_["vnew"][0:C, :], u_["u"][0:C, :], b6[0:C, 0:128], op=ALU.subtract), [u_["u"], b6], [u_["vnew"]])
                    A("act", lambda: nc.scalar.activation(out=u_["o1"][0:C, :], in_=b6[0:C, 128:256], func=AF.Identity, scale=col(eG)), [b6, eG], [u_["o1"]])
                    b7 = k.bank()
                    k.mm(b7[0:C, 0:128], [(u_["attnT"][0:C, 0:C], u_["vnew"][0:C, :])], reads=[u_["attnT"], u_["vnew"]], bank=b7)
                    k.mm(b7[:, 128:256], [(u_["kg"][0:C, :], u_["vnew"][0:C, :])], reads=[u_["kg"], u_["vnew"]], bank=b7, merge=True)
                    A("dve", lambda: nc.vector.tensor_tensor(u_["o"][0:C, :], u_["o1"][0:C, :], b7[0:C, 0:128], op=ALU.add), [u_["o1"], b7], [u_["o"]])
                    A("dve", lambda: nc.vector.scalar_tensor_tensor(out=S[:, :], in0=S[:, :], scalar=eGl[:, n, h:h + 1], in1=b7[:, 128:256],
                                                                     op0=ALU.mult, op1=ALU.add), [S, eGl, b7], [S])
                    if prompt:
                        k.dma("act", zb_t[0:C, :], P_zb.t[cs, h * 128:(h + 1) * 128], reads=[P_zb], writes=[zb_t])
                    else:
                        c0 = SEG["zb"][0] + h * 128
                        k.dma("act", zb_t[0:C, :], sTM.t[bj * TS:(bj + 1) * TS, c0:c0 + 128], reads=[sTM], writes=[zb_t])
                    A("act", lambda: nc.scalar.activation(out=u_["y"][0:C, :], in_=u_["o"][0:C, :], func=AF.Square, accum_out=u_["st"][0:C, 0:1]), [u_["o"]], [u_["y"], u_["st"]])
                    rstd_from_ss(u_["st"][0:C, 0:1], u_["st"][0:C, 1:2], 128, u_["st"], C)
                    A("dve", lambda: nc.vector.scalar_tensor_tensor(out=u_["y"][0:C, :], in0=u_["o"][0:C, :], scalar=u_["st"][0:C, 1:2], in1=onb[0:C, :],
                                                                     op0=ALU.mult, op1=ALU.mult), [u_["o"], u_["st"], onb], [u_["y"]])
                    A("act", lambda: nc.scalar.activation(out=zb_t[0:C, :], in_=zb_t[0:C, :], func=AF.Silu), [zb_t], [zb_t])
                    A("dve", lambda: nc.vector.tensor_tensor(u_["y"][0:C, :], u_["y"][0:C, :], zb_t[0:C, :], op=ALU.mult), [u_["y"], zb_t], [u_["y"]])
                    b8 = k.bank()
                    k.tr(b8[:, 0:C], u_["y"][0:C, :], ident[0:C, 0:C], reads=[u_["y"], ident], bank=b8, merge=False)
                    if prompt:
                        k.copy("act", u_["yT"][:, 0:C], b8[:, 0:C], reads=[b8], writes=[u_["yT"]])
                        k.dma("sp", mixT.t[1024 + h * 128:1024 + (h + 1) * 128, cs], u_["yT"][:, 0:C], reads=[u_["yT"]], writes=[mixT], merge=True)
                    else:
                        k.copy("act", mixTs[:, 8 + h, bj * TS:(bj + 1) * TS], b8[:, 0:C], reads=[b8], writes=[mixTs], merge=True)
                us = [U[(ui + n_) % 2] for n_ in range(nch)]
                par(0, us[0])
                for n_ in range(nch):
                    if n_ + 1 < nch:
                        par(n_ + 1, us[n_ + 1])
                    seq(n_, us[n_])
                ui += nch
                dst = O["p_dn"] if prompt else O["s_dn"]
                k.dma("sp", dst.t[l, bj * 16 + h], S[:, :], reads=[S], writes=[dst], merge=True)

    def mixerC_prompt(l, b):
        with k.scope():
            kiT2 = k.sb([128, 2, T], F32, "kiT2")
            kT = k.sb([128, 2, T], F32, "kT")
            V = k.sb([128, NT, 256], F32, "V")
            A("pool", lambda: nc.gpsimd.memset(kiT2[:], 0.0), [], [kiT2])
            r0 = FM_ROWS["ki"]
            k.dma("sp", kiT2[0:64, 0, :], P_fm.t[r0:r0 + 64, :], reads=[P_fm], writes=[kiT2])
            k.dma("sp", kiT2[64:128, 1, :], P_fm.t[r0 + 64:r0 + 128, :], reads=[P_fm], writes=[kiT2], merge=True)
            r0 = FM_ROWS["kc"]
            k.dma("sp", kT[:], P_fm.t[r0:r0 + 256, :].rearrange("(h d) t -> d h t", d=128), reads=[P_fm], writes=[kT])
            k.dma("sp", V[:], O["p_v"].t[l, b * T:(b + 1) * T, :].rearrange("(n s) f -> s n f", s=128), reads=[O["p_v"]], writes=[V])
            qiT = [k.sb([128, 8, 128], F32, "qiT") for _ in range(2)]
            qT = [k.sb([128, 8, 128], F32, "qT") for _ in range(2)]
            zcT = [k.sb([128, 8, 128], F32, "zcT") for _ in range(2)]
            wi = [k.sb([128, 16], F32, "wi") for _ in range(2)]
            rr = [k.sb([128, 2, 256], F32, "rr") for _ in range(2)]
            Iacc = k.sb([128, T], F32, "Iacc")
            work = k.sb([128, T], F32, "work")
            M = k.sb([128, T], F32, "M")
            MT = k.sb([128, NT, 128], F32, "MT")
            m8 = k.sb([128, 8], F32, "m8")
            thr = k.sb([128, 1], F32, "thr")
            ee = [k.sb([128, 4, 128], F32, "ee") for _ in range(2)]
            pp = [k.sb([128, 4, 128], F32, "pp") for _ in range(2)]
            rden = k.sb([128, 4, 128], F32, "rden")
            oo = k.sb([128, 4, 128], F32, "oo")
            szc = k.sb([128, 8, 128], F32, "szc")
            yc = k.sb([128, 4, 128], BF16, "yc")
            for qi in range(NT):
                ts = slice(qi * 128, (qi + 1) * 128)
                Sk = (qi + 1) * 128
                q_i, q_, z_, w_ = qiT[qi % 2], qT[qi % 2], zcT[qi % 2], wi[qi % 2]
                for buf, nm in ((q_i, "qi"), (q_, "qc"), (z_, "zc")):
                    r0 = FM_ROWS[nm]
                    k.dma("act", buf[:], P_fm.t[r0:r0 + 1024, ts].rearrange("(h d) t -> d h t", d=128), reads=[P_fm], writes=[buf])
                k.dma("act", w_[:], P_wi.t[ts, :], reads=[P_wi], writes=[w_])
                A("dve", lambda: nc.vector.tensor_scalar(w_[:], w_[:], 1.0 / 32.0, None, op0=ALU.mult), [w_], [w_])
                for kb0 in range(0, Sk, 256):
                    wd = min(256, Sk - kb0)
                    for c in range(8):
                        bank = k.bank()
                        k.mm(bank[:, 0:2 * wd].rearrange("p (a s) -> p a s", a=2), [(q_i[:, c, :], kiT2[:, :, kb0:kb0 + wd])], reads=[q_i, kiT2], bank=bank)
                        r_ = rr[c % 2]
                        A("act", lambda: nc.scalar.activation(out=r_[:, :, 0:wd], in_=bank[:, 0:2 * wd].rearrange("p (a s) -> p a s", a=2), func=AF.Relu), [bank], [r_])
                        for h2 in range(2):
                            hh = 2 * c + h2
                            e = "dve"
                            E = nc.vector
                            if hh == 0:
                                A(e, lambda: E.tensor_scalar(Iacc[:, kb0:kb0 + wd], r_[:, h2, 0:wd], w_[:, hh:hh + 1], None, op0=ALU.mult), [r_, w_], [Iacc], True)
                            else:
                                A(e, lambda: E.scalar_tensor_tensor(out=Iacc[:, kb0:kb0 + wd], in0=r_[:, h2, 0:wd], scalar=w_[:, hh:hh + 1],
                                                                    in1=Iacc[:, kb0:kb0 + wd], op0=ALU.mult, op1=ALU.add), [r_, w_, Iacc], [Iacc])
                A("dve", lambda: nc.vector.tensor_tensor(Iacc[:, qi * 128:Sk], Iacc[:, qi * 128:Sk], CT["neg_lt"][:, :], op=ALU.add), [Iacc, CT["neg_lt"]], [Iacc])
                if qi < 2:
                    A("dve", lambda: nc.vector.memset(thr[:], -1.0e29), [], [thr])
                else:
                    src = Iacc
                    for rd in range(32):
                        A("dve", lambda: nc.vector.max(out=m8[:], in_=src[:, 0:Sk]), [src], [m8])
                        if rd < 31:
                            A("dve", lambda: nc.vector.match_replace(out=work[:, 0:Sk], in_to_replace=m8[:], in_values=src[:, 0:Sk], imm_value=NEG), [src, m8], [work])
                            src = work
                    A("dve", lambda: nc.vector.tensor_reduce(out=thr[:], in_=m8[:], axis=AX.X, op=ALU.min), [m8], [thr])
                A("dve", lambda: nc.vector.tensor_scalar(M[:, 0:Sk], Iacc[:, 0:Sk], thr[:, 0:1], None, op0=ALU.is_ge), [Iacc, thr], [M])
                for s4 in range(0, qi + 1, 4):
                    n4 = min(4, qi + 1 - s4)
                    bank = k.bank()
                    for j in range(n4):
                        k.tr(bank[:, j * 128:(j + 1) * 128], M[:, (s4 + j) * 128:(s4 + j + 1) * 128], ident[:, :], reads=[M, ident], bank=bank, merge=(j > 0))
                    k.copy(k.ev(), MT[:, s4:s4 + n4, :].rearrange("p a b -> p (a b)"), bank[:, 0:n4 * 128], reads=[bank], writes=[MT], merge=True)
                A("act", lambda: nc.scalar.activation(out=szc[:], in_=z_[:], func=AF.Silu), [z_], [szc])
                for hkv in range(2):
                    k.reserved = []
                    num = k.bank()
                    den = k.bank()
                    k.reserved = [num, den]
                    for sb_ in range(qi + 1):
                        ks = slice(sb_ * 128, (sb_ + 1) * 128)
                        sc = k.bank()
                        k.mm(sc[:, :].rearrange("p (g t) -> p g t", g=4), [(kT[:, hkv, ks], q_[:, 4 * hkv:4 * hkv + 4, :])], reads=[kT, q_], bank=sc)
                        e_, p_ = ee[sb_ % 2], pp[sb_ % 2]
                        A("act", lambda: nc.scalar.activation(out=e_[:].rearrange("p g t -> p (g t)"), in_=sc[:, :], func=AF.Exp, scale=128.0 ** -0.5), [sc], [e_])
                        for g in range(4):
                            e = "dve" if g % 2 == 0 else "pool"
                            E = nc.vector if g % 2 == 0 else nc.gpsimd
                            A(e, lambda: E.tensor_tensor(p_[:, g, :], e_[:, g, :], MT[:, sb_, :], op=ALU.mult), [e_, MT], [p_], g > 0)
                        k.op("pe", lambda: nc.tensor.matmul(num[:, :], lhsT=V[:, sb_, hkv * 128:(hkv + 1) * 128], rhs=p_[:].rearrange("p g t -> p (g t)"),
                                                            start=(sb_ == 0), stop=(sb_ == qi)), reads=[V, p_], writes=[num], merge=(sb_ > 0))
                        k.op("pe", lambda: nc.tensor.matmul(den[:, :], lhsT=ones[:, :], rhs=p_[:].rearrange("p g t -> p (g t)"),
                                                            start=(sb_ == 0), stop=(sb_ == qi)), reads=[ones, p_], writes=[den], merge=(sb_ > 0))
                    A("dve", lambda: nc.vector.reciprocal(rden[:].rearrange("p g t -> p (g t)"), den[:, :]), [den], [rden])
                    A("dve", lambda: nc.vector.tensor_tensor(oo[:].rearrange("p g t -> p (g t)"), num[:, :], rden[:].rearrange("p g t -> p (g t)"), op=ALU.mult), [num, rden], [oo])
                    A("dve", lambda: nc.vector.tensor_tensor(yc[:], oo[:], szc[:, 4 * hkv:4 * hkv + 4, :], op=ALU.mult), [oo, szc], [yc])
                    k.reserved = []
                    r0 = 3072 + hkv * 512
                    k.dma("sp", mixT.t[r0:r0 + 512, ts].rearrange("(g d) t -> d g t", d=128), yc[:], reads=[yc], writes=[mixT], merge=True)


    def mixerC_sample(l, j):
        NK = 16384
        j4 = j * TS
        ts = slice(j4, j4 + TS)
        with k.scope():
            pt = k.sb([128, 1], I32, "pt")
            k.dma("sp", pt[:], page_table.t[j * 128:(j + 1) * 128, :], reads=[page_table], writes=[pt])
            pt8 = k.sb([128, 1], I32, "pt8")
            A("dve", lambda: nc.vector.tensor_single_scalar(out=pt8[:], in_=pt[:], scalar=3, op=ALU.logical_shift_left), [pt], [pt8])
            MT = k.sb([128, 128, TS], F32, "MTs")
            MTn = k.sb([TS, TS], F32, "MTn")
            NK = 16384
            NKT = NK + TS
            with k.scope():
                I_s = k.sb([TS, NK + 16], F32, "I_s")
                qiT = k.sb([64, TS, 8, 2], F32, "qiT")
                wcol = k.sb([64, 1], F32, "wcol")
                Wsel = k.sb([64, TS], F32, "Wsel")
                gq = SG["qi"]
                A("dve", lambda: nc.vector.tensor_copy(qiT[:, :, :, 0], sFM[0:64, gq:gq + 8, ts].rearrange("p c t -> p t c")), [sFM], [qiT], True)
                bank = k.bank()
                k.mm(bank[0:64, 0:8 * TS], [(ident[:, 64:128], sFM[:, gq:gq + 8, ts])], reads=[ident, sFM], bank=bank)
                A("dve", lambda: nc.vector.tensor_copy(qiT[:, :, :, 1], bank[0:64, 0:8 * TS].rearrange("p (c t) -> p t c", t=TS)), [bank], [qiT], True)
                w0 = SEG["wi"][0]
                for t_ in range(TS):
                    k.dma("sp", wcol[t_ * 16:(t_ + 1) * 16, :], sTM.t[j4 + t_:j4 + t_ + 1, w0:w0 + 16].rearrange("o h -> h o"), reads=[sTM], writes=[wcol], merge=True,
                          allow_slow_non_contiguous=True)
                A("dve", lambda: nc.vector.tensor_scalar(Wsel[:, :], CT["bm16"][0:64, 0:TS], wcol[:, 0:1], 1.0 / 32.0, op0=ALU.mult, op1=ALU.mult), [CT["bm16"], wcol], [Wsel])
                qiT2 = qiT[:].rearrange("p t c h -> p (t c h)")

                def score_block(rhs_ap, rhs_buf, wd, col0, rl):
                    b1 = k.bank()
                    k.mm(b1[0:64, 0:wd], [(qiT2, rhs_ap)], reads=[qiT, rhs_buf], bank=b1)
                    A("act", lambda: nc.scalar.activation(out=rl[:, 0:wd], in_=b1[0:64, 0:wd], func=AF.Relu), [b1], [rl])
                    b2 = k.bank()
                    k.mm(b2[0:TS, 0:wd], [(Wsel[:, :], rl[:, 0:wd])], reads=[Wsel, rl], bank=b2)
                    k.copy("dve", I_s[:, col0:col0 + wd], b2[0:TS, 0:wd], reads=[b2], writes=[I_s], merge=True)
                with k.scope():
                    kx = k.sb([128, 128 * 64], F32, "kx")
                    k.dma("pool", kx[:, :], cache_kidx.t[:, :], reads=[cache_kidx, pt], writes=[kx], indirect=pt[:, 0:1], elem_off=l * NPOOL * 8192)
                    kxT = [k.sb([64, 512], F32, "kxT") for _ in range(2)]
                    rl = [k.sb([64, 512], F32, "rl") for _ in range(2)]
                    for blk in range(NK // 512):
                        bank = k.bank()
                        for q in range(4):
                            kk = blk * 4 + q
                            k.tr(bank[0:64, q * 128:(q + 1) * 128], kx[:, kk * 64:(kk + 1) * 64], ident[:, :], reads=[kx, ident], bank=bank, merge=(q > 0))
                        kt_ = kxT[blk % 2]
                        k.copy(k.ev(), kt_[:, :], bank[0:64, :], reads=[bank], writes=[kt_])
                        score_block(kt_[:, :], kt_, 512, blk * 512, rl[blk % 2])
                    score_block(sFM[0:64, SG["ki"], ts], sFM, TS, NK, rl[0])
                    A("dve", lambda: nc.vector.tensor_tensor(I_s[:, NK:NK + TS], I_s[:, NK:NK + TS], CT["neg_lt"][0:TS, 0:TS], op=ALU.add), [I_s, CT["neg_lt"]], [I_s])
                NKT = NK + TS
                M_s = k.sb([TS, NK + 16], F32, "M_s")
                m8 = k.sb([TS, 8], F32, "m8s")
                thr = k.sb([TS, 1], F32, "thrs")
                cand = k.sb([TS, 16], F32, "cand")
                A("dve", lambda: nc.vector.memset(cand[:, :], NEG), [], [cand])
                A("dve", lambda: nc.vector.tensor_copy(cand[:, 8:8 + TS], I_s[:, NK:NK + TS]), [I_s], [cand])
                src = I_s
                for rd in range(32):
                    A("dve", lambda: nc.vector.max(out=cand[:, 0:8], in_=src[:, 0:NK]), [src, cand], [cand])
                    A("dve", lambda: nc.vector.max(out=m8[:], in_=cand[:, 0:16]), [cand], [m8])
                    if rd < 31:
                        A("dve", lambda: nc.vector.match_replace(out=M_s[:, 0:NK], in_to_replace=m8[:], in_values=src[:, 0:NK], imm_value=NEG), [src, m8], [M_s])
                        A("dve", lambda: nc.vector.match_replace(out=cand[:, 8:16], in_to_replace=m8[:], in_values=cand[:, 8:16], imm_value=NEG), [cand, m8], [cand])
                        src = M_s
                A("dve", lambda: nc.vector.tensor_reduce(out=thr[:], in_=m8[:], axis=AX.X, op=ALU.min), [m8], [thr])
                A("dve", lambda: nc.vector.tensor_scalar(M_s[:, 0:NKT], I_s[:, 0:NKT], thr[:, 0:1], None, op0=ALU.is_ge), [I_s, thr], [M_s])
                bank = k.bank()
                for kk in range(128):
                    k.tr(bank[:, kk * TS:(kk + 1) * TS], M_s[:, kk * 128:(kk + 1) * 128], ident[0:TS, 0:TS], reads=[M_s, ident], bank=bank, merge=(kk > 0))
                k.copy("dve", MT[:].rearrange("p a b -> p (a b)"), bank[:, 0:128 * TS], reads=[bank], writes=[MT])
                bank = k.bank()
                k.tr(bank[0:TS, 0:TS], M_s[:, NK:NK + TS], ident[0:TS, 0:TS], reads=[M_s, ident], bank=bank, merge=False)
                k.copy("dve", MTn[:, :], bank[0:TS, 0:TS], reads=[bank], writes=[MTn])
            qTs = k.sb([128, 2, TS, 4], F32, "qTs")
            gc = SG["qc"]
            for hkv in range(2):
                A("dve", lambda: nc.vector.tensor_copy(qTs[:, hkv, :, :], sFM[:, gc + hkv * 4:gc + hkv * 4 + 4, ts].rearrange("p g t -> p t g")), [sFM], [qTs], True)
            Ks = [k.sb([128, 16 * 256], F32, "Ks") for _ in range(1)]
            Vs = [k.sb([128, 16 * 256], F32, "Vs") for _ in range(1)]
            KT = k.sb([128, 32, 128], F32, "KTs")
            E = k.sb([128, 16, 2, TS, 4], F32, "Es")
            PT = k.sb([128, 16, 2, TS, 4], F32, "PTs")
            vnew = k.sb([TS, 256], F32, "vnews")
            v0 = SEG["vc"][0]
            k.dma("sp", vnew[:, :], sTM.t[ts, v0:v0 + 256], reads=[sTM], writes=[vnew])
            k.reserved = []
            acc = [k.bank() for _ in range(4)]
            k.reserved = list(acc)
            first = [True, True, True, True]

            def accum(bi, out_ap, lhsT, rhs, rd_bufs, last):
                k.op("pe", lambda: nc.tensor.matmul(out_ap, lhsT=lhsT, rhs=rhs, start=first[bi], stop=last), reads=rd_bufs, writes=[acc[bi]], merge=(not first[bi]))
                first[bi] = False
            for grp in range(8):
                K_, V_ = Ks[0], Vs[0]
                c0 = grp * 16 * 256
                k.dma("pool", K_[:, :], cache_k.t[:, :], reads=[cache_k, pt8], writes=[K_], indirect=pt8[:, 0:1], elem_off=(l * NPOOL * 8 + grp) * 4096)
                k.dma("pool", V_[:, :], cache_v.t[:, :], reads=[cache_v, pt8], writes=[V_], indirect=pt8[:, 0:1], elem_off=(l * NPOOL * 8 + grp) * 4096)
                for b8 in range(8):
                    bank = k.bank()
                    for q in range(4):
                        idx = b8 * 4 + q
                        k.tr(bank[:, q * 128:(q + 1) * 128], K_[:, idx * 128:(idx + 1) * 128], ident[:, :], reads=[K_, ident], bank=bank, merge=(q > 0))
                    k.copy(k.ev(), KT[:, b8 * 4:(b8 + 1) * 4, :].rearrange("p a b -> p (a b)"), bank[:, :], reads=[bank], writes=[KT], merge=True)
                sbank = k.bank()
                for idx in range(32):
                    hkv = idx % 2
                    k.mm(sbank[:, idx * 16:(idx + 1) * 16], [(KT[:, idx, :], qTs[:, hkv, :, :].rearrange("p t g -> p (t g)"))], reads=[KT, qTs], bank=sbank, merge=(idx > 0))
                A("act", lambda: nc.scalar.activation(out=E[:].rearrange("p a b c d -> p (a b c d)"), in_=sbank[:, :], func=AF.Exp, scale=128.0 ** -0.5), [sbank], [E])
                for hkv in range(2):
                    for g in range(4):
                        e = "dve" if g % 2 == 0 else "pool"
                        En = nc.vector if g % 2 == 0 else nc.gpsimd
                        A(e, lambda: En.tensor_tensor(PT[:, :, hkv, :, g], E[:, :, hkv, :, g], MT[:, grp * 16:(grp + 1) * 16, :], op=ALU.mult), [E, MT], [PT], True)
                for kk in range(16):
                    for hkv in range(2):
                        lhsT = PT[:, kk, hkv, :, :].rearrange("p t g -> p (t g)")
                        accum(hkv, acc[hkv][0:16, 0:128], lhsT, V_[:, (kk * 2 + hkv) * 128:(kk * 2 + hkv + 1) * 128], [PT, V_], False)
                        accum(2 + hkv, acc[2 + hkv][0:16, 0:1], lhsT, ones[:, 0:1], [PT, ones], False)
            En_ = k.sb([TS, 2, TS, 4], F32, "Enew")
            Pn_ = k.sb([TS, 2, TS, 4], F32, "Pnew")
            sbank = k.bank()
            gk = SG["kc"]
            for hkv in range(2):
                k.mm(sbank[0:TS, hkv * 16:(hkv + 1) * 16], [(sFM[:, gk + hkv, ts], qTs[:, hkv, :, :].rearrange("p t g -> p (t g)"))], reads=[sFM, qTs], bank=sbank, merge=(hkv > 0))
            A("act", lambda: nc.scalar.activation(out=En_[:].rearrange("p b c d -> p (b c d)"), in_=sbank[0:TS, 0:32], func=AF.Exp, scale=128.0 ** -0.5), [sbank], [En_])
            for hkv in range(2):
                for g in range(4):
                    A("dve", lambda: nc.vector.tensor_tensor(Pn_[:, hkv, :, g], En_[:, hkv, :, g], MTn[:, :], op=ALU.mult), [En_, MTn], [Pn_], True)
            for hkv in range(2):
                lhsT = Pn_[:, hkv, :, :].rearrange("p t g -> p (t g)")
                accum(hkv, acc[hkv][0:16, 0:128], lhsT, vnew[:, hkv * 128:(hkv + 1) * 128], [Pn_, vnew], True)
                accum(2 + hkv, acc[2 + hkv][0:16, 0:1], lhsT, ones[0:TS, 0:1], [Pn_, ones], True)
            rd_ = k.sb([16, 2], F32, "rdens")
            oo = k.sb([16, 128], F32, "oos")
            zc = k.sb([16, 128], F32, "zcs")
            z0 = SEG["zc"][0]
            for hkv in range(2):
                A("dve", lambda: nc.vector.reciprocal(rd_[:, hkv:hkv + 1], acc[2 + hkv][0:16, 0:1]), [acc[2 + hkv]], [rd_], True)
                k.dma("sp", zc[:, :], sTM.t[ts, z0 + hkv * 512:z0 + (hkv + 1) * 512].rearrange("t (g d) -> t g d", d=128), reads=[sTM], writes=[zc])
                A("act", lambda: nc.scalar.activation(out=zc[:, :], in_=zc[:, :], func=AF.Silu), [zc], [zc])
                A("dve", lambda: nc.vector.scalar_tensor_tensor(out=oo[:, :], in0=acc[hkv][0:16, 0:128], scalar=rd_[:, hkv:hkv + 1], in1=zc[:, :],
                                                                 op0=ALU.mult, op1=ALU.mult), [acc[hkv], rd_, zc], [oo])
                bank = k.bank()
                k.tr(bank[:, 0:16], oo[:, :], ident[0:16, 0:16], reads=[oo, ident], bank=bank, merge=False)
                A("dve", lambda: nc.vector.tensor_copy(mixTs[:, 24 + hkv * 4:24 + hkv * 4 + 4, ts], bank[:, 0:16].rearrange("p (t g) -> p g t", g=4)), [bank], [mixTs], True)
            k.reserved = []

    def outproj(l, prompt, b, xin, xout):
        ntok = T if prompt else NS
        with k.scope():
            HT = T // 2
            if prompt:
                mT = k.sb([128, 32, HT], BF16, "mT")
            else:
                mT = mixTs
            gate = k.sb([128, D], F32, "gate")
            gb = k.sb([128, 1024], F32, "gb")
            P = 128 if prompt else NS
            if prompt:
                k.dma("sp", gate[:], m_scr.t[b:b + 1, 2 * D:3 * D].to_broadcast([128, D]), reads=[m_scr], writes=[gate])
            else:
                for j in range(sbn):
                    k.dma("sp", gate[j * TS:(j + 1) * TS, :], m_scr.t[pb + j:pb + j + 1, 2 * D:3 * D].to_broadcast([TS, D]), reads=[m_scr], writes=[gate], merge=True)
            for q4 in range(4):
                k.dma("sp", gb[0:P, :], b_ada.t[l:l + 1, 2 * D + q4 * 1024:2 * D + (q4 + 1) * 1024].to_broadcast([P, 1024]), reads=[b_ada], writes=[gb])
                A("dve", lambda: nc.vector.tensor_tensor(gate[0:P, q4 * 1024:(q4 + 1) * 1024], gate[0:P, q4 * 1024:(q4 + 1) * 1024], gb[0:P, :], op=ALU.add), [gate, gb], [gate])
            xt = [k.sb([128, 256], F32, "xo") for _ in range(3)]
            xi = 0
            wv = w_out.t[l].rearrange("(c p) n -> p c n", p=128)
            row_base = b * T if prompt else 0
            for half, blk in [(hf, bl) for hf in range(2 if prompt else 1) for bl in range(D // 256)]:
                if prompt and blk == 0:
                    k.dma("sp", mT[:], mixT.t[:, half * HT:(half + 1) * HT].rearrange("(c p) t -> p c t", p=128), reads=[mixT], writes=[mT])
                wb = next_w()
                cs = slice(blk * 256, (blk + 1) * 256)
                k.dma("pool", wb[:], wv[:, :, cs], reads=[w_out], writes=[wb])
                ntl = (HT // P) if prompt else 1
                for ti in range(ntl):
                    r_off = half * HT + ti * P
                    rs = slice(row_base + r_off, row_base + r_off + P)
                    x_ = xt[xi % 3]
                    xi += 1
                    k.dma("act", x_[0:P, :], xin.t[rs, cs], reads=[xin], writes=[x_])
                    bank = k.bank()
                    k.mm(bank[0:P, 0:256], [(mT[:, c, ti * P:(ti + 1) * P], wb[:, c, :]) for c in range(32)], reads=[mT, wb], bank=bank)
                    st = next_stage()
                    A("dve", lambda: nc.vector.tensor_tensor(st[0:P, 0:256], bank[0:P, 0:256], gate[0:P, cs], op=ALU.mult), [bank, gate], [st])
                    A("pool", lambda: nc.gpsimd.tensor_tensor(st[0:P, 0:256], st[0:P, 0:256], x_[0:P, :], op=ALU.add), [st, x_], [st])
                    k.dma("sp", xout.t[rs, cs], st[0:P, 0:256], reads=[st], writes=[xout], merge=True)

    def final_norm(xin, yout, ntok):
        with k.scope():
            gf = k.sb([128, D], F32, "gf")
            k.dma("sp", gf[:], g_final.t[0:1, :].to_broadcast([128, D]), reads=[g_final], writes=[gf])
            xt = [k.sb([128, D], F32, "xf") for _ in range(2)]
            junk = k.sb([128, 2048], BF16, "junkf")
            ssq = k.sb([128, 4], F32, "ssqf")
            P = min(128, ntok)
            for ti in range(ntok // P):
                x_ = xt[ti % 2]
                rs = slice(ti * P, (ti + 1) * P)
                k.dma("act", x_[0:P, :], xin.t[rs, :], reads=[xin], writes=[x_])
                for hh in range(2):
                    A("act", lambda: nc.scalar.activation(out=junk[0:P, :], in_=x_[0:P, hh * 2048:(hh + 1) * 2048], func=AF.Square,
                                                          accum_out=ssq[0:P, hh:hh + 1]), [x_], [junk, ssq])
                A("dve", lambda: nc.vector.tensor_tensor(ssq[0:P, 2:3], ssq[0:P, 0:1], ssq[0:P, 1:2], op=ALU.add), [ssq], [ssq])
                rstd_from_ss(ssq[0:P, 2:3], ssq[0:P, 3:4], D, ssq, P)
                A("dve", lambda: nc.vector.scalar_tensor_tensor(out=x_[0:P, :], in0=x_[0:P, :], scalar=ssq[0:P, 3:4], in1=gf[0:P, :],
                                                                 op0=ALU.mult, op1=ALU.mult), [x_, ssq, gf], [x_])
                k.dma("sp", yout.t[rs, :], x_[0:P, :], reads=[x_], writes=[yout], merge=True)

    for l in range(nl):
        xin = [x_p, x_s] if l == 0 else xa
        xout = xa if l == 0 else xb
        adaln(l)
        if stage == "ada":
            break
        with k.scope():
            Acol = k.sb([128, NR, 32], F32, "Acol")
            Bcol = k.sb([128, NR, 32], F32, "Bcol")
            mod_cols(l, Acol, Bcol)
            for b in range(pb):
                with k.scope():
                    hT = k.sb([128, 32, T], BF16, "hT")
                    h_phase(xin[0], b * T, NT, 128, hT, Acol, Bcol, lambda ti, P, b=b: [(0, P, b)])
                    if stage != "h":
                        inproj(l, hT, T, True, b)
                if stage in ("inproj", "h"):
                    continue
                if stage in ("all", "A"):
                    mixerA_prompt(l)
                if stage in ("all", "B"):
                    mixerB(l, True, b)
                if stage in ("all", "C"):
                    mixerC_prompt(l, b)
                if stage == "all":
                    outproj(l, True, b, xin[0], xout[0])
            with k.scope():
                hTs = k.sb([128, 32, NS], BF16, "hTs")
                h_phase(xin[1], 0, 1, NS, hTs, Acol, Bcol, lambda ti, P: [(j * TS, (j + 1) * TS, pb + j) for j in range(sbn)])
                if stage != "h":
                    inproj(l, hTs, NS, False, 0)
            if stage in ("inproj", "h"):
                continue
            if stage in ("all", "A"):
                mixerA_sample(l)
            if stage in ("all", "B"):
                for j in range(sbn):
                    mixerB(l, False, j)
            if stage == "Cs":
                for j in range(sbn):
                    mixerC_sample(l, j)
            if stage == "all":
                if with_cache:
                    for j in range(sbn):
                        mixerC_sample(l, j)
                else:
                    A("dve", lambda: nc.vector.memset(mixTs[:, 24:32, :], 0.0), [], [mixTs], True)
                outproj(l, False, 0, xin[1], xout[1])
    if stage == "all":
        last = xa if nl == 1 else xb
        final_norm(last[0], O["y_p"], pb * T)
        final_norm(last[1], O["y_s"], NS)
    k.finish()
    return k


_BUILT = {}
NCORES = 4


def kernel(_build_args=None, **inputs):
    inp = {kk: np.asarray(v) for kk, v in inputs.items()}
    ba = dict(nl=DEPTH, pb=4 // NCORES, sbn=8 // NCORES, stage="all", with_cache=True)
    if _build_args:
        ba.update(_build_args)
    ncores = ba.pop("ncores", NCORES)
    key = tuple(sorted(ba.items()))
    if key not in _BUILT:
        _BUILT[key] = build(**ba)
    k = _BUILT[key]
    nl, pb, sbn = ba["nl"], ba["pb"], ba["sbn"]
    f = np.float32
    in_maps = []
    for c in range(ncores):
        ps = slice(c * pb, (c + 1) * pb)
        ss = slice(c * sbn, (c + 1) * sbn)
        m = {
            "x_p": np.ascontiguousarray(inp["x_prompt"][ps], f).reshape(pb * T, D),
            "x_s": np.ascontiguousarray(inp["x_sample"][ss], f).reshape(sbn * TS, D),
            "c_ps": np.ascontiguousarray(np.concatenate([inp["c_prompt"][ps], inp["c_sample"][ss]], 0), f),
            "state_dn": np.ascontiguousarray(inp["state_dn"][:nl, ss], f).reshape(nl, sbn * 16, 128, 128),
            "state_conv": np.ascontiguousarray(inp["state_conv"][:nl, ss], f).reshape(nl, sbn * 3, 6144),
            "w_ada": np.ascontiguousarray(inp["w_ada"][:nl], f),
            "b_ada": np.ascontiguousarray(inp["b_ada"][:nl], f),
            "g_norm": np.ascontiguousarray(inp["g_norm"][:nl], f),
            "w_in": np.ascontiguousarray(inp["w_in"][:nl], f),
            "a_vnorm": np.ascontiguousarray(inp["a_vnorm"][:nl], f),
            "a_ws": np.ascontiguousarray(inp["a_ws"][:nl], f),
            "a_bs": np.ascontiguousarray(inp["a_bs"][:nl], f).reshape(nl, 1024),
            "dn_conv_w": np.ascontiguousarray(inp["dn_conv_w"][:nl], f),
            "dn_a_log": np.ascontiguousarray(inp["dn_a_log"][:nl], f),
            "dn_dt_bias": np.ascontiguousarray(inp["dn_dt_bias"][:nl], f),
            "dn_onorm": np.ascontiguousarray(inp["dn_onorm"][:nl], f),
            "w_out": np.ascontiguousarray(inp["w_out"][:nl], f),
            "g_final": np.ascontiguousarray(inp["g_final"].reshape(1, D), f),
            "consts": CONST_ARR,
        }
        if ba["with_cache"]:
            m["cache_k"] = np.ascontiguousarray(inp["cache_k"][:nl], f).reshape(nl * NPOOL * 8, 16 * 256)
            m["cache_v"] = np.ascontiguousarray(inp["cache_v"][:nl], f).reshape(nl * NPOOL * 8, 16 * 256)
            m["cache_kidx"] = np.ascontiguousarray(inp["cache_kidx"][:nl], f).reshape(nl * NPOOL, 128 * 64)
            m["page_table"] = np.ascontiguousarray(inp["page_table"][ss].reshape(sbn * 128, 1), np.int32)
        in_maps.append(m)
    res = run_bass_kernel_spmd(k.nc, in_maps, core_ids=list(range(ncores)))
    R = res.results

    def cat(name, tail, per):
        parts = [R[c][name].reshape((nl, per) + tail) for c in range(ncores)]
        return np.concatenate(parts, 1)

    y_p = np.concatenate([R[c]["y_p"].reshape(pb, T, D) for c in range(ncores)], 0)
    y_s = np.concatenate([R[c]["y_s"].reshape(sbn, TS, D) for c in range(ncores)], 0)
    outs = (y_p, y_s,
            cat("p_k", (T, 2, 128), pb), cat("p_v", (T, 2, 128), pb), cat("p_kidx", (T, 64), pb),
            cat("p_dn", (16, 128, 128), pb), cat("p_conv", (3, 6144), pb),
            cat("s_k", (TS, 2, 128), sbn), cat("s_v", (TS, 2, 128), sbn), cat("s_kidx", (TS, 64), sbn),
            cat("s_dn", (16, 128, 128), sbn), cat("s_conv", (3, 6144), sbn), cat("s_amlp_v", (TS, 1024), sbn))
    return tuple(np.ascontiguousarray(o, dtype=np.float32) for o in outs)
```

```python
import numpy as np
from contextlib import ExitStack, contextmanager
import concourse.bass as bass
import concourse.mybir as mybir
from concourse.bass_utils import run_bass_kernel_spmd

F32 = mybir.dt.float32
BF16 = mybir.dt.bfloat16
I32 = mybir.dt.int32
AF = mybir.ActivationFunctionType
ALU = mybir.AluOpType
AX = mybir.AxisListType

SAME_ENG_SYNC = True
N_DSEM = 20

D = 4096
T = 2048
NT = 16
TS = 4
DEPTH = 2
D_IN = 14960
NPOOL = 1280
EPS = 1e-6
NEG = -1.0e30

SEGS = [("u", 0, 1024), ("v", 1024, 1024), ("za", 2048, 1024), ("qkv", 3072, 6144), ("zb", 9216, 2048),
        ("ab", 11264, 32), ("qc", 11296, 1024), ("kc", 12320, 256), ("vc", 12576, 256), ("zc", 12832, 1024),
        ("qi", 13856, 1024), ("ki", 14880, 64), ("wi", 14944, 16)]
SEG = {n: (s, w) for n, s, w in SEGS}
FMSEG = ["u", "za", "qkv", "qc", "kc", "zc", "qi", "ki"]
TMSEG = ["v", "zb", "ab", "kc", "vc", "ki", "wi"]
FM_ROWS = {}
_r = 0
for _n in FMSEG:
    FM_ROWS[_n] = _r
    _r += max(SEG[_n][1], 128)
FM_TOT = _r
SG = {}
_g = 0
for _n in FMSEG:
    SG[_n] = _g
    _g += max(SEG[_n][1], 128) // 128
NSG = _g


class Buf:
    __slots__ = ("t", "wr", "rd", "name", "excl")

    def __init__(self, t, name="", excl=False):
        self.t = t
        self.wr = {}
        self.rd = {}
        self.name = name
        self.excl = excl

    def __getitem__(self, idx):
        return self.t[idx]


class K:
    def __init__(self):
        self.nc = bass.Bass("TRN2", target_bir_lowering=False)
        self.stack = [ExitStack()]
        nc = self.nc
        self.eng = {"pe": nc.tensor, "act": nc.scalar, "dve": nc.vector, "pool": nc.gpsimd, "sp": nc.sync}
        self.sem = {}
        self.cnt = {}
        es = self.stack[0]
        for e in ("pe", "act", "dve", "pool"):
            self.sem[e] = es.enter_context(nc.semaphore("s_" + e))
            self.cnt[e] = 0
        self.dq = {}
        for q in ("sp", "pool", "act"):
            sems = []
            for i in range(N_DSEM):
                key = "d_%s_%d" % (q, i)
                self.sem[key] = es.enter_context(nc.semaphore(key))
                self.cnt[key] = 0
                sems.append(key)
            self.dq[q] = [sems, 0]
        self.waited = {e: {} for e in ("pe", "act", "dve", "pool", "sp")}
        self.nbuf = 0
        self.banks = []
        self.bank_i = 0
        self.reserved = []
        self.ev_i = 0

    @contextmanager
    def scope(self):
        es = ExitStack()
        self.stack.append(es)
        try:
            yield
        finally:
            self.barrier()
            self.stack.pop()
            es.close()

    def sb(self, shape, dt=F32, name=None):
        self.nbuf += 1
        name = (name or "sb") + "_%d" % self.nbuf
        t = self.stack[-1].enter_context(self.nc.sbuf_tensor(name, list(shape), dt))
        return Buf(t, name)

    def ps(self, shape, dt=F32, name=None):
        self.nbuf += 1
        name = (name or "ps") + "_%d" % self.nbuf
        t = self.stack[-1].enter_context(self.nc.psum_tensor(name, list(shape), dt))
        return Buf(t, name, excl=True)

    def dram(self, name, shape, dt=F32, kind="Internal"):
        t = self.nc.dram_tensor(name, list(shape), dt, kind=kind)
        return Buf(t.ap(), name)

    def bank(self):
        while True:
            b = self.banks[self.bank_i % len(self.banks)]
            self.bank_i += 1
            if b not in self.reserved:
                return b

    def ev(self):
        self.ev_i += 1
        return ("act", "dve")[self.ev_i % 2]

    def _deps(self, reads, writes, merge, e=None):
        deps = {}

        def add(d):
            for kk, c in d.items():
                if deps.get(kk, 0) < c:
                    deps[kk] = c
        for b in reads:
            add(b.wr)
            if b.excl:
                add({kk: c for kk, c in b.rd.items() if kk != e})
        for b in writes:
            add(b.rd)
            if not merge:
                add(b.wr)
        return deps

    def _emit_waits(self, e, deps):
        w = self.waited[e]
        for kk, c in deps.items():
            if kk == e and (e == "pe" or not SAME_ENG_SYNC):
                continue
            if w.get(kk, 0) >= c:
                continue
            self.eng[e].wait_ge(self.sem[kk], c)
            w[kk] = c

    def _mark(self, key, c, reads, writes, merge):
        for b in reads:
            if b.rd.get(key, 0) < c:
                b.rd[key] = c
        for b in writes:
            if merge:
                b.wr[key] = c
            else:
                b.wr = {key: c}
            b.rd = {}

    def op(self, e, fn, reads=(), writes=(), merge=False):
        deps = self._deps(reads, writes, merge, e)
        self._emit_waits(e, deps)
        ins = fn()
        self.cnt[e] += 1
        ins.then_inc(self.sem[e], 1)
        self._mark(e, self.cnt[e], reads, writes, merge)
        return ins

    def dma(self, q, out, in_, reads=(), writes=(), merge=False, indirect=None, elem_off=0, **kw):
        sems, i = self.dq[q]
        key = sems[i % len(sems)]
        self.dq[q][1] = i + 1
        deps = self._deps(reads, writes, merge)
        if self.cnt[key] > 0:
            deps[key] = max(deps.get(key, 0), self.cnt[key])
        self._emit_waits(q, deps)
        if indirect is not None:
            ins = self.nc.gpsimd.indirect_dma_start(out=out, out_offset=None, in_=in_,
                                                    in_offset=bass.IndirectOffsetOnAxis(ap=indirect, axis=0), element_offset=elem_off)
        else:
            ins = self.eng[q].dma_start(out=out, in_=in_, **kw)
        self.cnt[key] += 16
        ins.then_inc(self.sem[key], 16)
        self._mark(key, self.cnt[key], reads, writes, merge)

    def finish(self):
        deps = {kk: c for kk, c in self.cnt.items() if c > 0}
        self._emit_waits("sp", deps)

    def barrier(self):
        snap = {kk: c for kk, c in self.cnt.items() if c > 0}
        for e in ("pe", "act", "dve", "pool", "sp"):
            self._emit_waits(e, {kk: c for kk, c in snap.items() if kk != e})

    def mm(self, out_ap, pairs, reads, bank, merge=False):
        nc = self.nc
        n = len(pairs)

        def fn():
            ins = None
            for i, (l, r) in enumerate(pairs):
                ins = nc.tensor.matmul(out_ap, lhsT=l, rhs=r, start=(i == 0), stop=(i == n - 1))
            return ins
        self.op("pe", fn, reads=reads, writes=[bank], merge=merge)

    def tr(self, out_ap, in_ap, ident_ap, reads, bank, merge=True):
        nc = self.nc
        self.op("pe", lambda: nc.tensor.transpose(out_ap, in_ap, ident_ap), reads=reads, writes=[bank], merge=merge)

    def copy(self, e, out_ap, in_ap, reads, writes, merge=False):
        nc = self.nc
        if e == "act":
            self.op("act", lambda: nc.scalar.copy(out=out_ap, in_=in_ap), reads=reads, writes=writes, merge=merge)
        elif e == "dve":
            self.op("dve", lambda: nc.vector.tensor_copy(out_ap, in_ap), reads=reads, writes=writes, merge=merge)
        else:
            self.op("pool", lambda: nc.gpsimd.tensor_copy(out_ap, in_ap), reads=reads, writes=writes, merge=merge)


def make_consts():
    c = {}
    c["ident"] = np.eye(128, dtype=np.float32)
    i = np.arange(128)
    c["tri_le"] = (i[:, None] <= i[None, :]).astype(np.float32)
    c["neg_lt"] = np.where(i[:, None] < i[None, :], NEG, 0.0).astype(np.float32)
    c["neg_gt"] = np.where(i[:, None] > i[None, :], NEG, 0.0).astype(np.float32)
    c["pos_le"] = np.where(i[:, None] <= i[None, :], -NEG, 0.0).astype(np.float32)
    c["ones"] = np.ones((128, 128), np.float32)
    c["bm16"] = (i[:, None] // 16 == i[None, :]).astype(np.float32)
    for nm, r in (("sel127", 127), ("sel3", 3)):
        sel = np.zeros((128, 128), np.float32)
        sel[r, :] = 1.0
        c[nm] = sel
    names = list(c.keys())
    arr = np.stack([c[n] for n in names], 0)
    return names, arr


CONST_NAMES, CONST_ARR = make_consts()


def build(nl=DEPTH, pb=4, sbn=8, stage="all", with_cache=True):
    k = K()
    nc = k.nc
    NS = sbn * TS
    NR = pb + sbn
    x_p = k.dram("x_p", [pb * T, D], kind="ExternalInput")
    x_s = k.dram("x_s", [NS, D], kind="ExternalInput")
    c_ps = k.dram("c_ps", [NR, D], kind="ExternalInput")
    if with_cache:
        cache_k = k.dram("cache_k", [nl * NPOOL * 8, 16 * 256], kind="ExternalInput")
        cache_v = k.dram("cache_v", [nl * NPOOL * 8, 16 * 256], kind="ExternalInput")
        cache_kidx = k.dram("cache_kidx", [nl * NPOOL, 128 * 64], kind="ExternalInput")
        page_table = k.dram("page_table", [sbn * 128, 1], I32, kind="ExternalInput")
    state_dn = k.dram("state_dn", [nl, sbn * 16, 128, 128], kind="ExternalInput")
    state_conv = k.dram("state_conv", [nl, sbn * 3, 6144], kind="ExternalInput")
    w_ada = k.dram("w_ada", [nl, D, 3 * D], kind="ExternalInput")
    b_ada = k.dram("b_ada", [nl, 3 * D], kind="ExternalInput")
    g_norm = k.dram("g_norm", [nl, D], kind="ExternalInput")
    w_in = k.dram("w_in", [nl, D, D_IN], kind="ExternalInput")
    a_vnorm = k.dram("a_vnorm", [nl, 1024], kind="ExternalInput")
    a_ws = k.dram("a_ws", [nl, 8, 128, 128], kind="ExternalInput")
    a_bs = k.dram("a_bs", [nl, 8 * 128], kind="ExternalInput")
    dn_conv_w = k.dram("dn_conv_w", [nl, 4, 6144], kind="ExternalInput")
    dn_a_log = k.dram("dn_a_log", [nl, 16], kind="ExternalInput")
    dn_dt_bias = k.dram("dn_dt_bias", [nl, 16], kind="ExternalInput")
    dn_onorm = k.dram("dn_onorm", [nl, 128], kind="ExternalInput")
    w_out = k.dram("w_out", [nl, D, D], kind="ExternalInput")
    g_final = k.dram("g_final", [1, D], kind="ExternalInput")
    consts = k.dram("consts", list(CONST_ARR.shape), kind="ExternalInput")

    O = {}
    for name, shape in [("y_p", [pb * T, D]), ("y_s", [NS, D]), ("p_k", [nl, pb * T, 256]), ("p_v", [nl, pb * T, 256]),
                        ("p_kidx", [nl, pb * T, 64]), ("p_dn", [nl, pb * 16, 128, 128]), ("p_conv", [nl, pb * 3, 6144]),
                        ("s_k", [nl, NS, 256]), ("s_v", [nl, NS, 256]), ("s_kidx", [nl, NS, 64]),
                        ("s_dn", [nl, sbn * 16, 128, 128]), ("s_conv", [nl, sbn * 3, 6144]), ("s_amlp_v", [nl, NS, 1024])]:
        O[name] = k.dram(name, shape, kind="ExternalOutput")

    m_scr = k.dram("m_scr", [NR, 3 * D])
    P_fm = k.dram("P_fm", [FM_TOT, T])
    P_v = k.dram("P_v", [T, 1024])
    P_zb = k.dram("P_zb", [T, 2048])
    P_ab = k.dram("P_ab", [T, 32])
    P_wi = k.dram("P_wi", [T, 16])
    sTM = k.dram("sTM", [NS, D_IN])
    xa = [k.dram("xa_p", [pb * T, D]), k.dram("xa_s", [NS, D])]
    xb = [k.dram("xb_p", [pb * T, D]), k.dram("xb_s", [NS, D])]
    mixT = k.dram("mixT", [D, T], BF16)

    CT = {}
    for i, n in enumerate(CONST_NAMES):
        CT[n] = k.sb([128, 128], F32, "c_" + n)
        k.dma("sp", CT[n][:], consts.t[i], reads=[consts], writes=[CT[n]])
    ident = CT["ident"]
    ones = CT["ones"]
    k.banks = [k.ps([128, 512], F32, "bank") for _ in range(8)]
    wbuf = [k.sb([128, 32, 256], BF16, "wbuf") for _ in range(2)]
    wb_i = [0]
    stage_bufs = [k.sb([128, 512], F32, "stage") for _ in range(4)]
    st_i = [0]

    def next_w():
        b = wbuf[wb_i[0] % 2]
        wb_i[0] += 1
        return b

    def next_stage():
        b = stage_bufs[st_i[0] % len(stage_bufs)]
        st_i[0] += 1
        return b

    def evac_to_dram(bank, P, W, dst_ap, dst_buf, extra=()):
        st = next_stage()
        k.copy(k.ev(), st[0:P, 0:W], bank[0:P, 0:W], reads=[bank], writes=[st])
        k.dma("sp", dst_ap, st[0:P, 0:W], reads=[st], writes=[dst_buf], merge=True)
        for ap, buf in extra:
            k.dma("sp", ap, st[0:P, 0:W], reads=[st], writes=[buf], merge=True)

    def A(e, f, reads, writes, merge=False):
        return k.op(e, f, reads=reads, writes=writes, merge=merge)

    def rstd_from_ss(ss_ap, out_ap, n, buf, P):
        A("act", lambda: nc.scalar.activation(out=out_ap, in_=ss_ap, func=AF.Sqrt, bias=EPS, scale=1.0 / n), [buf], [buf])
        A("dve", lambda: nc.vector.reciprocal(out_ap, out_ap), [buf], [buf])

    mixTs = k.sb([128, 32, NS], BF16, "mixTs")
    sFM = k.sb([128, NSG, NS], F32, "sFM")

    def adaln(l):
        with k.scope():
            craw = k.sb([128, NR, 32], F32, "craw")
            cbf = k.sb([128, NR, 32], BF16, "cbf")
            k.dma("sp", craw[:], c_ps.t.rearrange("r (p c) -> p r c", c=32), reads=[c_ps], writes=[craw])
            A("act", lambda: nc.scalar.activation(out=cbf[:], in_=craw[:], func=AF.Silu), [craw], [cbf])
            wv = w_ada.t[l].rearrange("(p c) n -> p c n", c=32)
            for blk in range(3 * D // 256):
                wb = next_w()
                k.dma("pool", wb[:], wv[:, :, blk * 256:(blk + 1) * 256], reads=[w_ada], writes=[wb])
                bank = k.bank()
                k.mm(bank[0:NR, 0:256], [(cbf[:, :, c], wb[:, c, :]) for c in range(32)], reads=[cbf, wb], bank=bank)
                evac_to_dram(bank, NR, 256, m_scr.t[:, blk * 256:(blk + 1) * 256], m_scr)

    def mod_cols(l, Acol, Bcol):
        with k.scope():
            nrow = 2 * NR + 3
            rows = k.sb([32, nrow, 128], F32, "rows")
            for r in range(NR):
                for j in range(2):
                    k.dma("sp", rows[:, r * 2 + j, :], m_scr.t[r, j * D:(j + 1) * D].rearrange("(c p) -> c p", p=128),
                          reads=[m_scr], writes=[rows], merge=True)
            for j in range(2):
                k.dma("sp", rows[:, 2 * NR + j, :], b_ada.t[l, j * D:(j + 1) * D].rearrange("(c p) -> c p", p=128),
                      reads=[b_ada], writes=[rows], merge=True)
            k.dma("sp", rows[:, 2 * NR + 2, :], g_norm.t[l].rearrange("(c p) -> c p", p=128), reads=[g_norm], writes=[rows], merge=True)
            cols = k.sb([128, nrow, 32], F32, "cols")
            for j0 in range(0, nrow, 16):
                bank = k.bank()
                n = min(16, nrow - j0)
                for j in range(n):
                    k.tr(bank[:, j * 32:(j + 1) * 32], rows[:, j0 + j, :], ident[0:32, 0:32], reads=[rows, ident], bank=bank, merge=(j > 0))
                k.copy("dve", cols[:, j0:j0 + n, :].rearrange("p a b -> p (a b)"), bank[:, 0:n * 32], reads=[bank], writes=[cols], merge=True)
            for r in range(NR):
                A("dve", lambda: nc.vector.tensor_tensor(Bcol[:, r, :], cols[:, r * 2 + 0, :], cols[:, 2 * NR, :], op=ALU.add), [cols], [Bcol], True)
                A("dve", lambda: nc.vector.scalar_tensor_tensor(out=Acol[:, r, :], in0=cols[:, r * 2 + 1, :], scalar=1.0,
                                                                 in1=cols[:, 2 * NR + 1, :], op0=ALU.add, op1=ALU.add), [cols], [Acol], True)
                A("dve", lambda: nc.vector.tensor_tensor(Acol[:, r, :], Acol[:, r, :], cols[:, 2 * NR + 2, :], op=ALU.mult), [cols, Acol], [Acol], True)

    def h_phase(xin, row0, ntile, P, hT, Acol, Bcol, rows_of):
        with k.scope():
            xt = k.sb([128, D], F32, "xt")
            junk = k.sb([128, 1024], BF16, "junk")
            ssq = k.sb([128, 8], F32, "ssq")
            for ti in range(ntile):
                k.dma("act", xt[0:P, :], xin.t[row0 + ti * P:row0 + (ti + 1) * P, :], reads=[xin], writes=[xt])
                for hh in range(4):
                    A("act", lambda: nc.scalar.activation(out=junk[0:P, :], in_=xt[0:P, hh * 1024:(hh + 1) * 1024],
                                                          func=AF.Square, accum_out=ssq[0:P, hh:hh + 1]), [xt], [junk, ssq])
                A("dve", lambda: nc.vector.tensor_reduce(out=ssq[0:P, 4:5], in_=ssq[0:P, 0:4], axis=AX.X, op=ALU.add), [ssq], [ssq])
                rstd_from_ss(ssq[0:P, 4:5], ssq[0:P, 5:6], D, ssq, P)
                A("dve", lambda: nc.vector.tensor_scalar(xt[0:P, :], xt[0:P, :], ssq[0:P, 5:6], None, op0=ALU.mult), [xt, ssq], [xt])
                for c4 in range(8):
                    bank = k.bank()
                    for j in range(4):
                        c = c4 * 4 + j
                        k.tr(bank[:, j * 128:j * 128 + P], xt[0:P, c * 128:(c + 1) * 128], ident[0:P, 0:P],
                             reads=[xt, ident], bank=bank, merge=(j > 0))
                    eng_ = k.ev()
                    for j in range(4):
                        c = c4 * 4 + j
                        for (p0, p1, r) in rows_of(ti, P):
                            dst = hT[:, c, ti * P + p0:ti * P + p1]
                            src = bank[:, j * 128 + p0:j * 128 + p1]
                            if eng_ == "act":
                                A("act", lambda: nc.scalar.activation(out=dst, in_=src, func=AF.Identity,
                                                                      bias=Bcol[:, r, c:c + 1], scale=Acol[:, r, c:c + 1]),
                                  [bank, Acol, Bcol], [hT], True)
                            else:
                                A("dve", lambda: nc.vector.tensor_scalar(dst, src, Acol[:, r, c:c + 1], Bcol[:, r, c:c + 1],
                                                                         op0=ALU.mult, op1=ALU.add), [bank, Acol, Bcol], [hT], True)

    def inproj(l, hT, ntok, prompt, b):
        wv = w_in.t[l].rearrange("(c p) n -> p c n", p=128)
        for name, s0, sw in SEGS:
            for off in range(0, sw, 256):
                w = min(256, sw - off)
                col0 = s0 + off
                wb = next_w()
                k.dma("pool", wb[:, :, 0:w], wv[:, :, col0:col0 + w], reads=[w_in], writes=[wb])
                if name == "ki":
                    k.dma("pool", wb[:, :, 64:128], wv[:, :, col0:col0 + w], reads=[w_in], writes=[wb], merge=True)
                if not prompt:
                    bank = k.bank()
                    k.mm(bank[0:NS, 0:w], [(hT[:, c, :], wb[:, c, 0:w]) for c in range(32)], reads=[hT, wb], bank=bank)
                    extra = []
                    if name == "kc":
                        extra.append((O["s_k"].t[l, :, :], O["s_k"]))
                    if name == "vc":
                        extra.append((O["s_v"].t[l, :, :], O["s_v"]))
                    if name == "ki":
                        extra.append((O["s_kidx"].t[l, :, :], O["s_kidx"]))
                    evac_to_dram(bank, NS, w, sTM.t[:, col0:col0 + w], sTM, extra)
                    if name == "qkv":
                        for j in range(sbn):
                            k.dma("sp", O["s_conv"].t[l, j * 3:(j + 1) * 3, off:off + w], sTM.t[j * 4 + 1:j * 4 + 4, col0:col0 + w],
                                  reads=[sTM], writes=[O["s_conv"]], merge=True)
                    if name in FMSEG:
                        wfm = 128 if name == "ki" else w
                        for g0 in range(0, wfm, 128):
                            bank = k.bank()
                            k.mm(bank[:, 0:NS], [(wb[:, c, g0:g0 + 128], hT[:, c, :]) for c in range(32)], reads=[hT, wb], bank=bank)
                            sg = SG[name] + (off + g0) // 128
                            k.copy(k.ev(), sFM[:, sg, :], bank[:, 0:NS], reads=[bank], writes=[sFM], merge=True)
                    continue
                if name == "qkv":
                    bank = k.bank()
                    k.mm(bank[0:3, 0:w], [(hT[:, c, T - 3:T], wb[:, c, 0:w]) for c in range(32)], reads=[hT, wb], bank=bank)
                    evac_to_dram(bank, 3, w, O["p_conv"].t[l, b * 3:(b + 1) * 3, off:off + w], O["p_conv"])
                if name in TMSEG:
                    for ti in range(NT):
                        bank = k.bank()
                        k.mm(bank[:, 0:w], [(hT[:, c, ti * 128:(ti + 1) * 128], wb[:, c, 0:w]) for c in range(32)],
                             reads=[hT, wb], bank=bank)
                        rs = slice(ti * 128, (ti + 1) * 128)
                        ors = slice(b * T + ti * 128, b * T + (ti + 1) * 128)
                        if name == "v":
                            dst, dbuf = P_v.t[rs, off:off + w], P_v
                        elif name == "zb":
                            dst, dbuf = P_zb.t[rs, off:off + w], P_zb
                        elif name == "ab":
                            dst, dbuf = P_ab.t[rs, :], P_ab
                        elif name == "wi":
                            dst, dbuf = P_wi.t[rs, :], P_wi
                        elif name == "kc":
                            dst, dbuf = O["p_k"].t[l, ors, :], O["p_k"]
                        elif name == "vc":
                            dst, dbuf = O["p_v"].t[l, ors, :], O["p_v"]
                        else:
                            dst, dbuf = O["p_kidx"].t[l, ors, :], O["p_kidx"]
                        evac_to_dram(bank, 128, w, dst, dbuf)
                if name in FMSEG:
                    wfm = 128 if name == "ki" else w
                    for g0 in range(0, wfm, 128):
                        row0 = FM_ROWS[name] + off + g0
                        for tb in range(4):
                            bank = k.bank()
                            k.mm(bank[:, :], [(wb[:, c, g0:g0 + 128], hT[:, c, tb * 512:(tb + 1) * 512]) for c in range(32)],
                                 reads=[hT, wb], bank=bank)
                            evac_to_dram(bank, 128, 512, P_fm.t[row0:row0 + 128, tb * 512:(tb + 1) * 512], P_fm)

    def mixerA_setup(l, wmT, absr, avn):
        with k.scope():
            wnat = k.sb([128, 8, 128], F32, "wnat")
            k.dma("sp", wnat[:], a_ws.t[l].rearrange("h t s -> t h s"), reads=[a_ws], writes=[wnat])
            for hg in range(2):
                bank = k.bank()
                for hh in range(4):
                    k.tr(bank[:, hh * 128:(hh + 1) * 128], wnat[:, hg * 4 + hh, :], ident[:, :], reads=[wnat, ident], bank=bank, merge=(hh > 0))
                for hh in range(4):
                    A("dve", lambda: nc.vector.tensor_tensor(wmT[:, hg * 4 + hh, :], bank[:, hh * 128:(hh + 1) * 128], CT["tri_le"][:, :], op=ALU.mult),
                      [bank, CT["tri_le"]], [wmT], True)
        k.dma("sp", absr[0:1, :], a_bs.t[l:l + 1, :], reads=[a_bs], writes=[absr])
        k.dma("sp", avn[:], a_vnorm.t[l:l + 1, :].to_broadcast([128, 1024]), reads=[a_vnorm], writes=[avn])

    def mixerA_chunk(l, C, vt, uT, zaT, wmT, absr, avn, tmp, out_cb, vout_cb=None):
        (uT_ap, uT_buf), (zaT_ap, zaT_buf) = uT, zaT
        g1, dd, st4, gu, sz, yb = tmp
        A("act", lambda: nc.scalar.activation(out=g1[0:C, :], in_=vt[0:C, :], func=AF.Gelu_apprx_tanh, accum_out=st4[0:C, 0:1]), [vt], [g1, st4])
        A("dve", lambda: nc.vector.tensor_scalar(st4[0:C, 1:2], st4[0:C, 0:1], -1.0 / 1024, None, op0=ALU.mult), [st4], [st4])
        A("dve", lambda: nc.vector.tensor_scalar(dd[0:C, :], g1[0:C, :], st4[0:C, 1:2], None, op0=ALU.add), [g1, st4], [dd])
        A("act", lambda: nc.scalar.activation(out=g1[0:C, :], in_=dd[0:C, :], func=AF.Square, accum_out=st4[0:C, 2:3]), [dd], [g1, st4])
        rstd_from_ss(st4[0:C, 2:3], st4[0:C, 3:4], 1024, st4, C)
        A("dve", lambda: nc.vector.scalar_tensor_tensor(out=dd[0:C, :], in0=dd[0:C, :], scalar=st4[0:C, 3:4], in1=avn[0:C, :],
                                                         op0=ALU.mult, op1=ALU.mult), [dd, st4, avn], [dd])
        if vout_cb is not None:
            vout_cb(dd)
        A("act", lambda: nc.scalar.activation(out=gu[:, :, 0:C], in_=uT_ap, func=AF.Gelu_apprx_tanh), [uT_buf], [gu])
        A("act", lambda: nc.scalar.activation(out=sz[:, :, 0:C], in_=zaT_ap, func=AF.Silu), [zaT_buf], [sz])
        for hg in range(2):
            bank = k.bank()
            for hh in range(4):
                h = hg * 4 + hh
                k.mm(bank[:, hh * C:(hh + 1) * C], [(dd[0:C, h * 128:(h + 1) * 128], wmT[0:C, h, 0:C]),
                                                   (ones[0:1, :], absr[0:1, h * 128:h * 128 + C])],
                     reads=[dd, wmT, ones, absr], bank=bank, merge=(hh > 0))
            A("dve", lambda: nc.vector.tensor_tensor(gu[:, hg * 4:(hg + 1) * 4, 0:C], gu[:, hg * 4:(hg + 1) * 4, 0:C],
                                                     bank[:, 0:4 * C].rearrange("p (h c) -> p h c", c=C), op=ALU.mult), [gu, bank], [gu])
        A("dve", lambda: nc.vector.tensor_tensor(yb[:, :, 0:C], gu[:, :, 0:C], sz[:, :, 0:C], op=ALU.mult), [gu, sz], [yb])
        out_cb(yb)

    def mixerA_prompt(l):
        with k.scope():
            wmT = k.sb([128, 8, 128], F32, "wmT")
            absr = k.sb([1, 1024], F32, "absr")
            avn = k.sb([128, 1024], F32, "avn")
            mixerA_setup(l, wmT, absr, avn)
            vt = [k.sb([128, 1024], F32, "vt") for _ in range(2)]
            uT = [k.sb([128, 8, 128], F32, "uT") for _ in range(2)]
            zaT = [k.sb([128, 8, 128], F32, "zaT") for _ in range(2)]
            tmp = (k.sb([128, 1024], F32, "g1"), k.sb([128, 1024], F32, "dd"), k.sb([128, 4], F32, "st4"),
                   k.sb([128, 8, 128], F32, "gu"), k.sb([128, 8, 128], F32, "sz"), k.sb([128, 8, 128], BF16, "yb"))
            for ci in range(NT):
                ts = slice(ci * 128, (ci + 1) * 128)
                v_, u_, z_ = vt[ci % 2], uT[ci % 2], zaT[ci % 2]
                k.dma("sp", v_[:], P_v.t[ts, :], reads=[P_v], writes=[v_])
                r0 = FM_ROWS["u"]
                k.dma("act", u_[:], P_fm.t[r0:r0 + 1024, ts].rearrange("(h d) t -> d h t", d=128), reads=[P_fm], writes=[u_])
                r0 = FM_ROWS["za"]
                k.dma("act", z_[:], P_fm.t[r0:r0 + 1024, ts].rearrange("(h d) t -> d h t", d=128), reads=[P_fm], writes=[z_])

                def out_cb(yb, ts=ts):
                    k.dma("sp", mixT.t[0:1024, ts].rearrange("(h d) t -> d h t", d=128), yb[:], reads=[yb], writes=[mixT], merge=True)
                mixerA_chunk(l, 128, v_, (u_[:], u_), (z_[:], z_), wmT, absr, avn, tmp, out_cb)

    def mixerA_sample(l):
        with k.scope():
            wmT = k.sb([128, 8, 128], F32, "wmT")
            absr = k.sb([1, 1024], F32, "absr")
            avn = k.sb([128, 1024], F32, "avn")
            mixerA_setup(l, wmT, absr, avn)
            vt = k.sb([TS, 1024], F32, "vt")
            tmp = (k.sb([TS, 1024], F32, "g1"), k.sb([TS, 1024], F32, "dd"), k.sb([TS, 4], F32, "st4"),
                   k.sb([128, 8, TS], F32, "gu"), k.sb([128, 8, TS], F32, "sz"), k.sb([128, 8, TS], BF16, "yb"))
            for j in range(sbn):
                ts = slice(j * TS, (j + 1) * TS)
                k.dma("sp", vt[:], sTM.t[ts, 1024:2048], reads=[sTM], writes=[vt])

                def out_cb(yb, ts=ts):
                    A("dve", lambda: nc.vector.tensor_copy(mixTs[:, 0:8, ts], yb[:]), [yb], [mixTs], True)

                def vout_cb(dd, ts=ts):
                    k.dma("sp", O["s_amlp_v"].t[l, ts, :], dd[0:TS, :], reads=[dd], writes=[O["s_amlp_v"]], merge=True)
                mixerA_chunk(l, TS, vt, (sFM[:, SG["u"]:SG["u"] + 8, ts], sFM), (sFM[:, SG["za"]:SG["za"] + 8, ts], sFM),
                             wmT, absr, avn, tmp, out_cb, vout_cb)

    def mixerB(l, prompt, bj):
        L = T if prompt else TS
        C = 128 if prompt else TS
        nch = L // C
        sel = CT["sel127"] if prompt else CT["sel3"]
        nsq = 6 if prompt else 1
        with k.scope():
            convw = k.sb([128, 48, 4], F32, "convw")
            with k.scope():
                cw_nat = k.sb([4, 6144], F32, "cw_nat")
                k.dma("sp", cw_nat[:], dn_conv_w.t[l], reads=[dn_conv_w], writes=[cw_nat])
                bank = k.bank()
                for g in range(48):
                    k.tr(bank[:, g * 4:(g + 1) * 4], cw_nat[0:4, g * 128:(g + 1) * 128], ident[0:4, 0:4], reads=[cw_nat, ident], bank=bank, merge=(g > 0))
                k.copy("dve", convw[:].rearrange("p g j -> p (g j)"), bank[:, 0:192], reads=[bank], writes=[convw])
            hp = k.sb([128, 3, 16], F32, "hp")
            k.dma("sp", hp[:, 0, :], dn_a_log.t[l:l + 1, :].to_broadcast([128, 16]), reads=[dn_a_log], writes=[hp], merge=True)
            k.dma("sp", hp[:, 1, :], dn_dt_bias.t[l:l + 1, :].to_broadcast([128, 16]), reads=[dn_dt_bias], writes=[hp], merge=True)
            A("act", lambda: nc.scalar.activation(out=hp[:, 2, :], in_=hp[:, 0, :], func=AF.Exp), [hp], [hp])
            A("dve", lambda: nc.vector.tensor_scalar(hp[:, 2, :], hp[:, 2, :], -1.0, None, op0=ALU.mult), [hp], [hp])
            onb = k.sb([128, 128], F32, "onb")
            k.dma("sp", onb[:], dn_onorm.t[l:l + 1, :].to_broadcast([128, 128]), reads=[dn_onorm], writes=[onb])
            abt = k.sb([128, nch, 32], F32, "abt")
            if prompt:
                k.dma("sp", abt[:], P_ab.t.rearrange("(n c) f -> c n f", c=128), reads=[P_ab], writes=[abt])
            else:
                k.dma("sp", abt[0:C, 0, :], sTM.t[bj * TS:(bj + 1) * TS, SEG["ab"][0]:SEG["ab"][0] + 32], reads=[sTM], writes=[abt])
            gt = k.sb([128, nch, 16], F32, "gt")
            beta = k.sb([128, nch, 16], F32, "beta")
            Gt = k.sb([128, nch, 16], F32, "Gt")
            eG = k.sb([128, nch, 16], F32, "eG")
            eGd = k.sb([128, nch, 16], F32, "eGd")
            eGl = k.sb([128, nch, 16], F32, "eGl")
            bE = k.sb([128, nch, 16], F32, "bE")
            nbeta = k.sb([128, nch, 16], F32, "nbeta")
            for n in range(nch):
                A("dve", lambda: nc.vector.tensor_tensor(gt[0:C, n, :], abt[0:C, n, 0:16], hp[0:C, 1, :], op=ALU.add), [abt, hp], [gt], True)
            A("act", lambda: nc.scalar.activation(out=gt[0:C], in_=gt[0:C], func=AF.Exp), [gt], [gt])
            A("act", lambda: nc.scalar.activation(out=gt[0:C], in_=gt[0:C], func=AF.Ln, bias=1.0, scale=1.0), [gt], [gt])
            for n in range(nch):
                A("dve", lambda: nc.vector.tensor_tensor(gt[0:C, n, :], gt[0:C, n, :], hp[0:C, 2, :], op=ALU.mult), [gt, hp], [gt], True)
            A("act", lambda: nc.scalar.activation(out=beta[0:C], in_=abt[0:C, :, 16:32], func=AF.Sigmoid), [abt], [beta])
            A("dve", lambda: nc.vector.tensor_scalar(nbeta[0:C], beta[0:C], -1.0, None, op0=ALU.mult), [beta], [nbeta])
            bank = k.bank()
            k.mm(bank[0:C, 0:nch * 16], [(CT["tri_le"][0:C, 0:C], gt[0:C].rearrange("p n h -> p (n h)"))], reads=[CT["tri_le"], gt], bank=bank)
            k.copy("dve", Gt[0:C].rearrange("p n h -> p (n h)"), bank[0:C, 0:nch * 16], reads=[bank], writes=[Gt])
            A("act", lambda: nc.scalar.activation(out=eG[0:C], in_=Gt[0:C], func=AF.Exp), [Gt], [eG])
            A("dve", lambda: nc.vector.tensor_tensor(bE[0:C], eG[0:C], beta[0:C], op=ALU.mult), [eG, beta], [bE])
            bank = k.bank()
            k.mm(bank[:, 0:nch * 16], [(sel[0:C, :], Gt[0:C].rearrange("p n h -> p (n h)"))], reads=[sel, Gt], bank=bank)
            A("act", lambda: nc.scalar.activation(out=eGl[:].rearrange("p n h -> p (n h)"), in_=bank[:, 0:nch * 16], func=AF.Exp), [bank], [eGl])
            A("dve", lambda: nc.vector.tensor_tensor(eGd[0:C].rearrange("p n h -> p (n h)"), bank[0:C, 0:nch * 16],
                                                     Gt[0:C].rearrange("p n h -> p (n h)"), op=ALU.subtract), [bank, Gt], [eGd])
            A("act", lambda: nc.scalar.activation(out=eGd[0:C], in_=eGd[0:C], func=AF.Exp), [eGd], [eGd])

            xp = [k.sb([128, 3 + L], F32, "xp") for _ in range(3)]
            qkv = [k.sb([128, L], F32, "qkvc") for _ in range(3)]
            sq = k.sb([128, min(L, 512)], F32, "sq")
            rs = k.sb([128, min(L, 512)], F32, "rs")
            S = k.sb([128, 128], F32, "S")
            zb_t = k.sb([128, 128], F32, "zb_t")
            U = [dict(kv=k.sb([128, 256], F32, "kv"), vb=k.sb([128, 128], F32, "vb"), kbg=k.sb([128, 128], F32, "kbg"),
                      kg=k.sb([128, 128], F32, "kg"), dg=k.sb([128, 128], F32, "dg"), z1=k.sb([128, 128], F32, "z1"),
                      z2=k.sb([128, 128], F32, "z2"), X=[k.sb([128, 128], F32, "X") for _ in range(2)],
                      XT=[k.sb([128, 128], F32, "XT") for _ in range(2)], R=[k.sb([128, 128], F32, "R") for _ in range(2)],
                      attnT=k.sb([128, 128], F32, "attnT"), u=k.sb([128, 128], F32, "u"), wT=k.sb([128, 128], F32, "wT"),
                      vnew=k.sb([128, 128], F32, "vnew"), o1=k.sb([128, 128], F32, "o1"), o=k.sb([128, 128], F32, "o"),
                      st=k.sb([128, 4], F32, "ost"), y=k.sb([128, 128], F32, "y"), yT=k.sb([128, 128], BF16, "yT"))
                 for _ in range(8 if prompt else 2)]
            ui = 0
            qrow = FM_ROWS["qkv"]
            for h in range(16):
                for i3 in range(3):
                    ch0 = i3 * 2048 + h * 128
                    if prompt:
                        A("pool", lambda: nc.gpsimd.memset(xp[i3][:, 0:3], 0.0), [], [xp[i3]])
                        k.dma("sp", xp[i3][:, 3:3 + L], P_fm.t[qrow + ch0:qrow + ch0 + 128, :], reads=[P_fm], writes=[xp[i3]], merge=True)
                    else:
                        sg = SG["qkv"] + ch0 // 128
                        k.dma("sp", xp[i3][:, 0:3], state_conv.t[l, bj * 3:(bj + 1) * 3, ch0:ch0 + 128].rearrange("j d -> d j"),
                              reads=[state_conv], writes=[xp[i3]], allow_slow_non_contiguous=True)
                        A("dve", lambda: nc.vector.tensor_copy(xp[i3][:, 3:3 + L], sFM[:, sg, bj * TS:(bj + 1) * TS]), [sFM], [xp[i3]], True)
                    g = ch0 // 128
                    o_ = qkv[i3]
                    A("dve", lambda: nc.vector.tensor_scalar(o_[:, :], xp[i3][:, 0:L], convw[:, g, 0:1], None, op0=ALU.mult), [xp[i3], convw], [o_])
                    for j in range(1, 4):
                        A("dve",
                          lambda: nc.vector.scalar_tensor_tensor(out=o_[:, :], in0=xp[i3][:, j:j + L], scalar=convw[:, g, j:j + 1],
                                                                                              in1=o_[:, :], op0=ALU.mult, op1=ALU.add),
                          [xp[i3], convw, o_], [o_])
                    A("act", lambda: nc.scalar.activation(out=o_[:, :], in_=o_[:, :], func=AF.Silu), [o_], [o_])
                for i3 in range(2):
                    o_ = qkv[i3]
                    for t0 in range(0, L, 512):
                        wd = min(512, L - t0)
                        A("act", lambda: nc.scalar.activation(out=sq[:, 0:wd], in_=o_[:, t0:t0 + wd], func=AF.Square), [o_], [sq])
                        bank = k.bank()
                        k.mm(bank[:, 0:wd], [(ones[:, :], sq[:, 0:wd])], reads=[ones, sq], bank=bank)
                        A("act", lambda: nc.scalar.activation(out=rs[:, 0:wd], in_=bank[:, 0:wd], func=AF.Sqrt, bias=EPS, scale=1.0), [bank], [rs])
                        A("dve", lambda: nc.vector.reciprocal(rs[:, 0:wd], rs[:, 0:wd]), [rs], [rs])
                        if i3 == 0:
                            A("dve", lambda: nc.vector.scalar_tensor_tensor(out=o_[:, t0:t0 + wd], in0=o_[:, t0:t0 + wd], scalar=128.0 ** -0.5,
                                                                             in1=rs[:, 0:wd], op0=ALU.mult, op1=ALU.mult), [o_, rs], [o_])
                        else:
                            A("dve", lambda: nc.vector.tensor_tensor(o_[:, t0:t0 + wd], o_[:, t0:t0 + wd], rs[:, 0:wd], op=ALU.mult), [o_, rs], [o_])
                qT, kT, vT = qkv
                if prompt:
                    A("pool", lambda: nc.gpsimd.memset(S[:], 0.0), [], [S])
                else:
                    k.dma("sp", S[:], state_dn.t[l, bj * 16 + h], reads=[state_dn], writes=[S])
                def par(n, u_):
                    cs = slice(n * C, (n + 1) * C)
                    col = lambda tl: tl[0:C, n, h:h + 1]
                    X, XT, R = u_["X"], u_["XT"], u_["R"]
                    bank = k.bank()
                    k.tr(bank[0:C, 0:128], kT[:, cs], ident[:, :], reads=[kT, ident], bank=bank, merge=False)
                    k.tr(bank[0:C, 128:256], vT[:, cs], ident[:, :], reads=[vT, ident], bank=bank, merge=True)
                    yield
                    k.copy("act", u_["kv"][0:C, :], bank[0:C, 0:256], reads=[bank], writes=[u_["kv"]])
                    A("pool", lambda: nc.gpsimd.tensor_scalar(u_["vb"][0:C, :], u_["kv"][0:C, 128:256], col(beta), None, op0=ALU.mult), [u_["kv"], beta], [u_["vb"]])
                    A("pool", lambda: nc.gpsimd.tensor_scalar(u_["kbg"][0:C, :], u_["kv"][0:C, 0:128], col(bE), None, op0=ALU.mult), [u_["kv"], bE], [u_["kbg"]])
                    A("pool", lambda: nc.gpsimd.tensor_scalar(u_["kg"][0:C, :], u_["kv"][0:C, 0:128], col(eGd), None, op0=ALU.mult), [u_["kv"], eGd], [u_["kg"]])
                    A("pool", lambda: nc.gpsimd.tensor_scalar(u_["dg"][0:C, 0:C], ident[0:C, 0:C], col(Gt), None, op0=ALU.mult), [ident, Gt], [u_["dg"]])
                    yield
                    bk = k.bank()
                    k.mm(bk[0:C, 0:C], [(kT[:, cs], kT[:, cs])], reads=[kT], bank=bk)
                    k.mm(bk[0:C, 128:128 + C], [(kT[:, cs], qT[:, cs])], reads=[kT, qT], bank=bk, merge=True)
                    k.mm(bk[0:C, 256:256 + C], [(ones[0:C, 0:C], u_["dg"][0:C, 0:C])], reads=[ones, u_["dg"]], bank=bk, merge=True)
                    yield
                    A("dve", lambda: nc.vector.scalar_tensor_tensor(out=u_["z1"][0:C, 0:C], in0=bk[0:C, 256:256 + C], scalar=col(Gt),
                                                                     in1=CT["pos_le"][0:C, 0:C], op0=ALU.subtract, op1=ALU.add), [bk, Gt, CT["pos_le"]], [u_["z1"]])
                    A("dve", lambda: nc.vector.scalar_tensor_tensor(out=u_["z2"][0:C, 0:C], in0=bk[0:C, 256:256 + C], scalar=col(Gt),
                                                                     in1=CT["neg_gt"][0:C, 0:C], op0=ALU.subtract, op1=ALU.add), [bk, Gt, CT["neg_gt"]], [u_["z2"]])
                    yield
                    A("act", lambda: nc.scalar.activation(out=u_["z1"][0:C, 0:C], in_=u_["z1"][0:C, 0:C], func=AF.Exp, scale=-1.0), [u_["z1"]], [u_["z1"]])
                    A("act", lambda: nc.scalar.activation(out=u_["z2"][0:C, 0:C], in_=u_["z2"][0:C, 0:C], func=AF.Exp), [u_["z2"]], [u_["z2"]])
                    X, XT, R = u_["X"], u_["XT"], u_["R"]
                    yield
                    A("dve", lambda: nc.vector.scalar_tensor_tensor(out=XT[0][0:C, 0:C], in0=bk[0:C, 0:C], scalar=col(nbeta), in1=u_["z1"][0:C, 0:C],
                                                                     op0=ALU.mult, op1=ALU.mult), [bk, nbeta, u_["z1"]], [XT[0]])
                    A("dve", lambda: nc.vector.tensor_tensor(u_["attnT"][0:C, 0:C], bk[0:C, 128:128 + C], u_["z2"][0:C, 0:C], op=ALU.mult), [bk, u_["z2"]], [u_["attnT"]])
                    yield
                    b2 = k.bank()
                    k.tr(b2[0:C, 0:C], XT[0][0:C, 0:C], ident[0:C, 0:C], reads=[XT[0], ident], bank=b2, merge=False)
                    yield
                    k.copy("act", X[0][0:C, 0:C], b2[0:C, 0:C], reads=[b2], writes=[X[0]])
                    A("dve", lambda: nc.vector.tensor_tensor(R[0][0:C, 0:C], b2[0:C, 0:C], ident[0:C, 0:C], op=ALU.add), [b2, ident], [R[0]])
                    cur = 0
                    for kk in range(1, nsq + 1):
                        nx = 1 - cur
                        yield
                        b3 = k.bank()
                        last = (kk == nsq)
                        k.mm(b3[0:C, 0:C], [(X[cur][0:C, 0:C], XT[cur][0:C, 0:C])], reads=[X[cur], XT[cur]], bank=b3)
                        if not last:
                            k.mm(b3[0:C, 128:128 + C], [(XT[cur][0:C, 0:C], X[cur][0:C, 0:C])], reads=[X[cur], XT[cur]], bank=b3, merge=True)
                        yield
                        k.copy("act", XT[nx][0:C, 0:C], b3[0:C, 0:C], reads=[b3], writes=[XT[nx]])
                        if not last:
                            k.copy("dve", X[nx][0:C, 0:C], b3[0:C, 128:128 + C], reads=[b3], writes=[X[nx]])
                        yield
                        b4 = k.bank()
                        k.mm(b4[0:C, 0:C], [(XT[nx][0:C, 0:C], R[cur][0:C, 0:C])], reads=[XT[nx], R[cur]], bank=b4)
                        yield
                        A("dve", lambda: nc.vector.tensor_tensor(R[nx][0:C, 0:C], b4[0:C, 0:C], R[cur][0:C, 0:C], op=ALU.add), [b4, R[cur]], [R[nx]])
                        cur = nx
                    TT = R[cur]
                    yield
                    b5 = k.bank()
                    k.mm(b5[0:C, 0:128], [(TT[0:C, 0:C], u_["vb"][0:C, :])], reads=[TT, u_["vb"]], bank=b5)
                    k.mm(b5[:, 128:128 + C], [(u_["kbg"][0:C, :], TT[0:C, 0:C])], reads=[TT, u_["kbg"]], bank=b5, merge=True)
                    yield
                    k.copy("act", u_["u"][0:C, :], b5[0:C, 0:128], reads=[b5], writes=[u_["u"]])
                    k.copy("dve", u_["wT"][:, 0:C], b5[:, 128:128 + C], reads=[b5], writes=[u_["wT"]])
                    u_["TT"] = TT
                def seq(n, u_):
                    cs = slice(n * C, (n + 1) * C)
                    col = lambda tl: tl[0:C, n, h:h + 1]
                    yield
                    b6 = k.bank()
                    k.mm(b6[0:C, 0:128], [(u_["wT"][:, 0:C], S[:, :])], reads=[u_["wT"], S], bank=b6)
                    k.mm(b6[0:C, 128:256], [(qT[:, cs], S[:, :])], reads=[qT, S], bank=b6, merge=True)
                    yield
                    A("dve", lambda: nc.vector.tensor_tensor(u_["vnew"][0:C, :], u_["u"][0:C, :], b6[0:C, 0:128], op=ALU.subtract), [u_["u"], b6], [u_["vnew"]])
                    A("act", lambda: nc.scalar.activation(out=u_["o1"][0:C, :], in_=b6[0:C, 128:256], func=AF.Identity, scale=col(eG)), [b6, eG], [u_["o1"]])
                    yield
                    b7 = k.bank()
                    k.mm(b7[0:C, 0:128], [(u_["attnT"][0:C, 0:C], u_["vnew"][0:C, :])], reads=[u_["attnT"], u_["vnew"]], bank=b7)
                    k.mm(b7[:, 128:256], [(u_["kg"][0:C, :], u_["vnew"][0:C, :])], reads=[u_["kg"], u_["vnew"]], bank=b7, merge=True)
                    yield
                    A("dve", lambda: nc.vector.tensor_tensor(u_["o"][0:C, :], u_["o1"][0:C, :], b7[0:C, 0:128], op=ALU.add), [u_["o1"], b7], [u_["o"]])
                    A("dve", lambda: nc.vector.scalar_tensor_tensor(out=S[:, :], in0=S[:, :], scalar=eGl[:, n, h:h + 1], in1=b7[:, 128:256],
                                                                     op0=ALU.mult, op1=ALU.add), [S, eGl, b7], [S])
                    if prompt:
                        k.dma("act", zb_t[0:C, :], P_zb.t[cs, h * 128:(h + 1) * 128], reads=[P_zb], writes=[zb_t])
                    else:
                        c0 = SEG["zb"][0] + h * 128
                        k.dma("act", zb_t[0:C, :], sTM.t[bj * TS:(bj + 1) * TS, c0:c0 + 128], reads=[sTM], writes=[zb_t])
                    yield
                    A("act", lambda: nc.scalar.activation(out=u_["y"][0:C, :], in_=u_["o"][0:C, :], func=AF.Square, accum_out=u_["st"][0:C, 0:1]), [u_["o"]], [u_["y"], u_["st"]])
                    rstd_from_ss(u_["st"][0:C, 0:1], u_["st"][0:C, 1:2], 128, u_["st"], C)
                    yield
                    A("dve", lambda: nc.vector.scalar_tensor_tensor(out=u_["y"][0:C, :], in0=u_["o"][0:C, :], scalar=u_["st"][0:C, 1:2], in1=onb[0:C, :],
                                                                     op0=ALU.mult, op1=ALU.mult), [u_["o"], u_["st"], onb], [u_["y"]])
                    A("act", lambda: nc.scalar.activation(out=zb_t[0:C, :], in_=zb_t[0:C, :], func=AF.Silu), [zb_t], [zb_t])
                    A("dve", lambda: nc.vector.tensor_tensor(u_["y"][0:C, :], u_["y"][0:C, :], zb_t[0:C, :], op=ALU.mult), [u_["y"], zb_t], [u_["y"]])
                    yield
                    b8 = k.bank()
                    k.tr(b8[:, 0:C], u_["y"][0:C, :], ident[0:C, 0:C], reads=[u_["y"], ident], bank=b8, merge=False)
                    if prompt:
                        k.copy("act", u_["yT"][:, 0:C], b8[:, 0:C], reads=[b8], writes=[u_["yT"]])
                        k.dma("sp", mixT.t[1024 + h * 128:1024 + (h + 1) * 128, cs], u_["yT"][:, 0:C], reads=[u_["yT"]], writes=[mixT], merge=True)
                    else:
                        k.copy("act", mixTs[:, 8 + h, bj * TS:(bj + 1) * TS], b8[:, 0:C], reads=[b8], writes=[mixTs], merge=True)
                NU = len(U)
                G = NU // 2

                def run(gens):
                    gens = list(gens)
                    while gens:
                        for g_ in list(gens):
                            try:
                                next(g_)
                            except StopIteration:
                                gens.remove(g_)

                def seq_chain(chunks):
                    for n_ in chunks:
                        yield from seq(n_, U[n_ % NU])
                groups = [list(range(g0, min(nch, g0 + G))) for g0 in range(0, nch, G)]
                run([par(n_, U[n_ % NU]) for n_ in groups[0]])
                for gi, grp in enumerate(groups):
                    gens = [seq_chain(grp)]
                    if gi + 1 < len(groups):
                        gens += [par(n_, U[n_ % NU]) for n_ in groups[gi + 1]]
                    run(gens)
                dst = O["p_dn"] if prompt else O["s_dn"]
                k.dma("sp", dst.t[l, bj * 16 + h], S[:, :], reads=[S], writes=[dst], merge=True)

    def mixerC_prompt(l, b):
        with k.scope():
            kiT2 = k.sb([128, 2, T], F32, "kiT2")
            kT = k.sb([128, 2, T], F32, "kT")
            V = k.sb([128, NT, 256], F32, "V")
            A("pool", lambda: nc.gpsimd.memset(kiT2[:], 0.0), [], [kiT2])
            r0 = FM_ROWS["ki"]
            k.dma("sp", kiT2[0:64, 0, :], P_fm.t[r0:r0 + 64, :], reads=[P_fm], writes=[kiT2])
            k.dma("sp", kiT2[64:128, 1, :], P_fm.t[r0 + 64:r0 + 128, :], reads=[P_fm], writes=[kiT2], merge=True)
            r0 = FM_ROWS["kc"]
            k.dma("sp", kT[:], P_fm.t[r0:r0 + 256, :].rearrange("(h d) t -> d h t", d=128), reads=[P_fm], writes=[kT])
            k.dma("sp", V[:], O["p_v"].t[l, b * T:(b + 1) * T, :].rearrange("(n s) f -> s n f", s=128), reads=[O["p_v"]], writes=[V])
            qiT = [k.sb([128, 8, 128], F32, "qiT") for _ in range(2)]
            qT = [k.sb([128, 8, 128], F32, "qT") for _ in range(2)]
            zcT = [k.sb([128, 8, 128], F32, "zcT") for _ in range(2)]
            wi = [k.sb([128, 16], F32, "wi") for _ in range(2)]
            rr = [k.sb([128, 2, 256], F32, "rr") for _ in range(2)]
            Iacc = k.sb([128, T], F32, "Iacc")
            work = k.sb([128, T], F32, "work")
            M = k.sb([128, T], F32, "M")
            MT = k.sb([128, NT, 128], F32, "MT")
            m8 = k.sb([128, 8], F32, "m8")
            thr = k.sb([128, 1], F32, "thr")
            ee = [k.sb([128, 4, 128], F32, "ee") for _ in range(2)]
            pp = [k.sb([128, 4, 128], F32, "pp") for _ in range(2)]
            rden = k.sb([128, 4, 128], F32, "rden")
            oo = k.sb([128, 4, 128], F32, "oo")
            szc = k.sb([128, 8, 128], F32, "szc")
            yc = k.sb([128, 4, 128], BF16, "yc")
            for qi in range(NT):
                ts = slice(qi * 128, (qi + 1) * 128)
                Sk = (qi + 1) * 128
                q_i, q_, z_, w_ = qiT[qi % 2], qT[qi % 2], zcT[qi % 2], wi[qi % 2]
                for buf, nm in ((q_i, "qi"), (q_, "qc"), (z_, "zc")):
                    r0 = FM_ROWS[nm]
                    k.dma("act", buf[:], P_fm.t[r0:r0 + 1024, ts].rearrange("(h d) t -> d h t", d=128), reads=[P_fm], writes=[buf])
                k.dma("act", w_[:], P_wi.t[ts, :], reads=[P_wi], writes=[w_])
                A("dve", lambda: nc.vector.tensor_scalar(w_[:], w_[:], 1.0 / 32.0, None, op0=ALU.mult), [w_], [w_])
                for kb0 in range(0, Sk, 256):
                    wd = min(256, Sk - kb0)
                    for c in range(8):
                        bank = k.bank()
                        k.mm(bank[:, 0:2 * wd].rearrange("p (a s) -> p a s", a=2), [(q_i[:, c, :], kiT2[:, :, kb0:kb0 + wd])], reads=[q_i, kiT2], bank=bank)
                        r_ = rr[c % 2]
                        A("act", lambda: nc.scalar.activation(out=r_[:, :, 0:wd], in_=bank[:, 0:2 * wd].rearrange("p (a s) -> p a s", a=2), func=AF.Relu), [bank], [r_])
                        for h2 in range(2):
                            hh = 2 * c + h2
                            e = "dve"
                            E = nc.vector
                            if hh == 0:
                                A(e, lambda: E.tensor_scalar(Iacc[:, kb0:kb0 + wd], r_[:, h2, 0:wd], w_[:, hh:hh + 1], None, op0=ALU.mult), [r_, w_], [Iacc], True)
                            else:
                                A(e, lambda: E.scalar_tensor_tensor(out=Iacc[:, kb0:kb0 + wd], in0=r_[:, h2, 0:wd], scalar=w_[:, hh:hh + 1],
                                                                    in1=Iacc[:, kb0:kb0 + wd], op0=ALU.mult, op1=ALU.add), [r_, w_, Iacc], [Iacc])
                A("dve", lambda: nc.vector.tensor_tensor(Iacc[:, qi * 128:Sk], Iacc[:, qi * 128:Sk], CT["neg_lt"][:, :], op=ALU.add), [Iacc, CT["neg_lt"]], [Iacc])
                if qi < 2:
                    A("dve", lambda: nc.vector.memset(thr[:], -1.0e29), [], [thr])
                else:
                    src = Iacc
                    for rd in range(32):
                        A("dve", lambda: nc.vector.max(out=m8[:], in_=src[:, 0:Sk]), [src], [m8])
                        if rd < 31:
                            A("dve", lambda: nc.vector.match_replace(out=work[:, 0:Sk], in_to_replace=m8[:], in_values=src[:, 0:Sk], imm_value=NEG), [src, m8], [work])
                            src = work
                    A("dve", lambda: nc.vector.tensor_reduce(out=thr[:], in_=m8[:], axis=AX.X, op=ALU.min), [m8], [thr])
                A("dve", lambda: nc.vector.tensor_scalar(M[:, 0:Sk], Iacc[:, 0:Sk], thr[:, 0:1], None, op0=ALU.is_ge), [Iacc, thr], [M])
                for s4 in range(0, qi + 1, 4):
                    n4 = min(4, qi + 1 - s4)
                    bank = k.bank()
                    for j in range(n4):
                        k.tr(bank[:, j * 128:(j + 1) * 128], M[:, (s4 + j) * 128:(s4 + j + 1) * 128], ident[:, :], reads=[M, ident], bank=bank, merge=(j > 0))
                    k.copy(k.ev(), MT[:, s4:s4 + n4, :].rearrange("p a b -> p (a b)"), bank[:, 0:n4 * 128], reads=[bank], writes=[MT], merge=True)
                A("act", lambda: nc.scalar.activation(out=szc[:], in_=z_[:], func=AF.Silu), [z_], [szc])
                for hkv in range(2):
                    k.reserved = []
                    num = k.bank()
                    den = k.bank()
                    k.reserved = [num, den]
                    for sb_ in range(qi + 1):
                        ks = slice(sb_ * 128, (sb_ + 1) * 128)
                        sc = k.bank()
                        k.mm(sc[:, :].rearrange("p (g t) -> p g t", g=4), [(kT[:, hkv, ks], q_[:, 4 * hkv:4 * hkv + 4, :])], reads=[kT, q_], bank=sc)
                        e_, p_ = ee[sb_ % 2], pp[sb_ % 2]
                        A("act", lambda: nc.scalar.activation(out=e_[:].rearrange("p g t -> p (g t)"), in_=sc[:, :], func=AF.Exp, scale=128.0 ** -0.5), [sc], [e_])
                        for g in range(4):
                            e = "dve" if g % 2 == 0 else "pool"
                            E = nc.vector if g % 2 == 0 else nc.gpsimd
                            A(e, lambda: E.tensor_tensor(p_[:, g, :], e_[:, g, :], MT[:, sb_, :], op=ALU.mult), [e_, MT], [p_], g > 0)
                        k.op("pe", lambda: nc.tensor.matmul(num[:, :], lhsT=V[:, sb_, hkv * 128:(hkv + 1) * 128], rhs=p_[:].rearrange("p g t -> p (g t)"),
                                                            start=(sb_ == 0), stop=(sb_ == qi)), reads=[V, p_], writes=[num], merge=(sb_ > 0))
                        k.op("pe", lambda: nc.tensor.matmul(den[:, :], lhsT=ones[:, :], rhs=p_[:].rearrange("p g t -> p (g t)"),
                                                            start=(sb_ == 0), stop=(sb_ == qi)), reads=[ones, p_], writes=[den], merge=(sb_ > 0))
                    A("dve", lambda: nc.vector.reciprocal(rden[:].rearrange("p g t -> p (g t)"), den[:, :]), [den], [rden])
                    A("dve", lambda: nc.vector.tensor_tensor(oo[:].rearrange("p g t -> p (g t)"), num[:, :], rden[:].rearrange("p g t -> p (g t)"), op=ALU.mult), [num, rden], [oo])
                    A("dve", lambda: nc.vector.tensor_tensor(yc[:], oo[:], szc[:, 4 * hkv:4 * hkv + 4, :], op=ALU.mult), [oo, szc], [yc])
                    k.reserved = []
                    r0 = 3072 + hkv * 512
                    k.dma("sp", mixT.t[r0:r0 + 512, ts].rearrange("(g d) t -> d g t", d=128), yc[:], reads=[yc], writes=[mixT], merge=True)


    def mixerC_sample(l, j):
        NK = 16384
        j4 = j * TS
        ts = slice(j4, j4 + TS)
        with k.scope():
            pt = k.sb([128, 1], I32, "pt")
            k.dma("sp", pt[:], page_table.t[j * 128:(j + 1) * 128, :], reads=[page_table], writes=[pt])
            pt8 = k.sb([128, 1], I32, "pt8")
            A("dve", lambda: nc.vector.tensor_single_scalar(out=pt8[:], in_=pt[:], scalar=3, op=ALU.logical_shift_left), [pt], [pt8])
            MT = k.sb([128, 128, TS], F32, "MTs")
            MTn = k.sb([TS, TS], F32, "MTn")
            NK = 16384
            NKT = NK + TS
            with k.scope():
                I_s = k.sb([TS, NK + 16], F32, "I_s")
                qiT = k.sb([64, TS, 8, 2], F32, "qiT")
                wcol = k.sb([64, 1], F32, "wcol")
                Wsel = k.sb([64, TS], F32, "Wsel")
                gq = SG["qi"]
                A("dve", lambda: nc.vector.tensor_copy(qiT[:, :, :, 0], sFM[0:64, gq:gq + 8, ts].rearrange("p c t -> p t c")), [sFM], [qiT], True)
                bank = k.bank()
                k.mm(bank[0:64, 0:8 * TS], [(ident[:, 64:128], sFM[:, gq:gq + 8, ts])], reads=[ident, sFM], bank=bank)
                A("dve", lambda: nc.vector.tensor_copy(qiT[:, :, :, 1], bank[0:64, 0:8 * TS].rearrange("p (c t) -> p t c", t=TS)), [bank], [qiT], True)
                w0 = SEG["wi"][0]
                for t_ in range(TS):
                    k.dma("sp", wcol[t_ * 16:(t_ + 1) * 16, :], sTM.t[j4 + t_:j4 + t_ + 1, w0:w0 + 16].rearrange("o h -> h o"), reads=[sTM], writes=[wcol], merge=True,
                          allow_slow_non_contiguous=True)
                A("dve", lambda: nc.vector.tensor_scalar(Wsel[:, :], CT["bm16"][0:64, 0:TS], wcol[:, 0:1], 1.0 / 32.0, op0=ALU.mult, op1=ALU.mult), [CT["bm16"], wcol], [Wsel])
                qiT2 = qiT[:].rearrange("p t c h -> p (t c h)")

                def score_block(rhs_ap, rhs_buf, wd, col0, rl):
                    b1 = k.bank()
                    k.mm(b1[0:64, 0:wd], [(qiT2, rhs_ap)], reads=[qiT, rhs_buf], bank=b1)
                    A("act", lambda: nc.scalar.activation(out=rl[:, 0:wd], in_=b1[0:64, 0:wd], func=AF.Relu), [b1], [rl])
                    b2 = k.bank()
                    k.mm(b2[0:TS, 0:wd], [(Wsel[:, :], rl[:, 0:wd])], reads=[Wsel, rl], bank=b2)
                    k.copy("dve", I_s[:, col0:col0 + wd], b2[0:TS, 0:wd], reads=[b2], writes=[I_s], merge=True)
                with k.scope():
                    kx = k.sb([128, 128 * 64], F32, "kx")
                    k.dma("pool", kx[:, :], cache_kidx.t[:, :], reads=[cache_kidx, pt], writes=[kx], indirect=pt[:, 0:1], elem_off=l * NPOOL * 8192)
                    kxT = [k.sb([64, 512], F32, "kxT") for _ in range(2)]
                    rl = [k.sb([64, 512], F32, "rl") for _ in range(2)]
                    for blk in range(NK // 512):
                        bank = k.bank()
                        for q in range(4):
                            kk = blk * 4 + q
                            k.tr(bank[0:64, q * 128:(q + 1) * 128], kx[:, kk * 64:(kk + 1) * 64], ident[:, :], reads=[kx, ident], bank=bank, merge=(q > 0))
                        kt_ = kxT[blk % 2]
                        k.copy(k.ev(), kt_[:, :], bank[0:64, :], reads=[bank], writes=[kt_])
                        score_block(kt_[:, :], kt_, 512, blk * 512, rl[blk % 2])
                    score_block(sFM[0:64, SG["ki"], ts], sFM, TS, NK, rl[0])
                    A("dve", lambda: nc.vector.tensor_tensor(I_s[:, NK:NK + TS], I_s[:, NK:NK + TS], CT["neg_lt"][0:TS, 0:TS], op=ALU.add), [I_s, CT["neg_lt"]], [I_s])
                NKT = NK + TS
                M_s = k.sb([TS, NK + 16], F32, "M_s")
                m8 = k.sb([TS, 8], F32, "m8s")
                thr = k.sb([TS, 1], F32, "thrs")
                cand = k.sb([TS, 16], F32, "cand")
                A("dve", lambda: nc.vector.memset(cand[:, :], NEG), [], [cand])
                A("dve", lambda: nc.vector.tensor_copy(cand[:, 8:8 + TS], I_s[:, NK:NK + TS]), [I_s], [cand])
                src = I_s
                for rd in range(32):
                    A("dve", lambda: nc.vector.max(out=cand[:, 0:8], in_=src[:, 0:NK]), [src, cand], [cand])
                    A("dve", lambda: nc.vector.max(out=m8[:], in_=cand[:, 0:16]), [cand], [m8])
                    if rd < 31:
                        A("dve", lambda: nc.vector.match_replace(out=M_s[:, 0:NK], in_to_replace=m8[:], in_values=src[:, 0:NK], imm_value=NEG), [src, m8], [M_s])
                        A("dve", lambda: nc.vector.match_replace(out=cand[:, 8:16], in_to_replace=m8[:], in_values=cand[:, 8:16], imm_value=NEG), [cand, m8], [cand])
                        src = M_s
                A("dve", lambda: nc.vector.tensor_reduce(out=thr[:], in_=m8[:], axis=AX.X, op=ALU.min), [m8], [thr])
                A("dve", lambda: nc.vector.tensor_scalar(M_s[:, 0:NKT], I_s[:, 0:NKT], thr[:, 0:1], None, op0=ALU.is_ge), [I_s, thr], [M_s])
                bank = k.bank()
                for kk in range(128):
                    k.tr(bank[:, kk * TS:(kk + 1) * TS], M_s[:, kk * 128:(kk + 1) * 128], ident[0:TS, 0:TS], reads=[M_s, ident], bank=bank, merge=(kk > 0))
                k.copy("dve", MT[:].rearrange("p a b -> p (a b)"), bank[:, 0:128 * TS], reads=[bank], writes=[MT])
                bank = k.bank()
                k.tr(bank[0:TS, 0:TS], M_s[:, NK:NK + TS], ident[0:TS, 0:TS], reads=[M_s, ident], bank=bank, merge=False)
                k.copy("dve", MTn[:, :], bank[0:TS, 0:TS], reads=[bank], writes=[MTn])
            qTs = k.sb([128, 2, TS, 4], F32, "qTs")
            gc = SG["qc"]
            for hkv in range(2):
                A("dve", lambda: nc.vector.tensor_copy(qTs[:, hkv, :, :], sFM[:, gc + hkv * 4:gc + hkv * 4 + 4, ts].rearrange("p g t -> p t g")), [sFM], [qTs], True)
            Ks = [k.sb([128, 16 * 256], F32, "Ks") for _ in range(1)]
            Vs = [k.sb([128, 16 * 256], F32, "Vs") for _ in range(1)]
            KT = k.sb([128, 32, 128], F32, "KTs")
            E = k.sb([128, 16, 2, TS, 4], F32, "Es")
            PT = k.sb([128, 16, 2, TS, 4], F32, "PTs")
            vnew = k.sb([TS, 256], F32, "vnews")
            v0 = SEG["vc"][0]
            k.dma("sp", vnew[:, :], sTM.t[ts, v0:v0 + 256], reads=[sTM], writes=[vnew])
            k.reserved = []
            acc = [k.bank() for _ in range(4)]
            k.reserved = list(acc)
            first = [True, True, True, True]

            def accum(bi, out_ap, lhsT, rhs, rd_bufs, last):
                k.op("pe", lambda: nc.tensor.matmul(out_ap, lhsT=lhsT, rhs=rhs, start=first[bi], stop=last), reads=rd_bufs, writes=[acc[bi]], merge=(not first[bi]))
                first[bi] = False
            for grp in range(8):
                K_, V_ = Ks[0], Vs[0]
                c0 = grp * 16 * 256
                k.dma("pool", K_[:, :], cache_k.t[:, :], reads=[cache_k, pt8], writes=[K_], indirect=pt8[:, 0:1], elem_off=(l * NPOOL * 8 + grp) * 4096)
                k.dma("pool", V_[:, :], cache_v.t[:, :], reads=[cache_v, pt8], writes=[V_], indirect=pt8[:, 0:1], elem_off=(l * NPOOL * 8 + grp) * 4096)
                for b8 in range(8):
                    bank = k.bank()
                    for q in range(4):
                        idx = b8 * 4 + q
                        k.tr(bank[:, q * 128:(q + 1) * 128], K_[:, idx * 128:(idx + 1) * 128], ident[:, :], reads=[K_, ident], bank=bank, merge=(q > 0))
                    k.copy(k.ev(), KT[:, b8 * 4:(b8 + 1) * 4, :].rearrange("p a b -> p (a b)"), bank[:, :], reads=[bank], writes=[KT], merge=True)
                sbank = k.bank()
                for idx in range(32):
                    hkv = idx % 2
                    k.mm(sbank[:, idx * 16:(idx + 1) * 16], [(KT[:, idx, :], qTs[:, hkv, :, :].rearrange("p t g -> p (t g)"))], reads=[KT, qTs], bank=sbank, merge=(idx > 0))
                A("act", lambda: nc.scalar.activation(out=E[:].rearrange("p a b c d -> p (a b c d)"), in_=sbank[:, :], func=AF.Exp, scale=128.0 ** -0.5), [sbank], [E])
                for hkv in range(2):
                    for g in range(4):
                        e = "dve" if g % 2 == 0 else "pool"
                        En = nc.vector if g % 2 == 0 else nc.gpsimd
                        A(e, lambda: En.tensor_tensor(PT[:, :, hkv, :, g], E[:, :, hkv, :, g], MT[:, grp * 16:(grp + 1) * 16, :], op=ALU.mult), [E, MT], [PT], True)
                for kk in range(16):
                    for hkv in range(2):
                        lhsT = PT[:, kk, hkv, :, :].rearrange("p t g -> p (t g)")
                        accum(hkv, acc[hkv][0:16, 0:128], lhsT, V_[:, (kk * 2 + hkv) * 128:(kk * 2 + hkv + 1) * 128], [PT, V_], False)
                        accum(2 + hkv, acc[2 + hkv][0:16, 0:1], lhsT, ones[:, 0:1], [PT, ones], False)
            En_ = k.sb([TS, 2, TS, 4], F32, "Enew")
            Pn_ = k.sb([TS, 2, TS, 4], F32, "Pnew")
            sbank = k.bank()
            gk = SG["kc"]
            for hkv in range(2):
                k.mm(sbank[0:TS, hkv * 16:(hkv + 1) * 16], [(sFM[:, gk + hkv, ts], qTs[:, hkv, :, :].rearrange("p t g -> p (t g)"))], reads=[sFM, qTs], bank=sbank, merge=(hkv > 0))
            A("act", lambda: nc.scalar.activation(out=En_[:].rearrange("p b c d -> p (b c d)"), in_=sbank[0:TS, 0:32], func=AF.Exp, scale=128.0 ** -0.5), [sbank], [En_])
            for hkv in range(2):
                for g in range(4):
                    A("dve", lambda: nc.vector.tensor_tensor(Pn_[:, hkv, :, g], En_[:, hkv, :, g], MTn[:, :], op=ALU.mult), [En_, MTn], [Pn_], True)
            for hkv in range(2):
                lhsT = Pn_[:, hkv, :, :].rearrange("p t g -> p (t g)")
                accum(hkv, acc[hkv][0:16, 0:128], lhsT, vnew[:, hkv * 128:(hkv + 1) * 128], [Pn_, vnew], True)
                accum(2 + hkv, acc[2 + hkv][0:16, 0:1], lhsT, ones[0:TS, 0:1], [Pn_, ones], True)
            rd_ = k.sb([16, 2], F32, "rdens")
            oo = k.sb([16, 128], F32, "oos")
            zc = k.sb([16, 128], F32, "zcs")
            z0 = SEG["zc"][0]
            for hkv in range(2):
                A("dve", lambda: nc.vector.reciprocal(rd_[:, hkv:hkv + 1], acc[2 + hkv][0:16, 0:1]), [acc[2 + hkv]], [rd_], True)
                k.dma("sp", zc[:, :], sTM.t[ts, z0 + hkv * 512:z0 + (hkv + 1) * 512].rearrange("t (g d) -> t g d", d=128), reads=[sTM], writes=[zc])
                A("act", lambda: nc.scalar.activation(out=zc[:, :], in_=zc[:, :], func=AF.Silu), [zc], [zc])
                A("dve", lambda: nc.vector.scalar_tensor_tensor(out=oo[:, :], in0=acc[hkv][0:16, 0:128], scalar=rd_[:, hkv:hkv + 1], in1=zc[:, :],
                                                                 op0=ALU.mult, op1=ALU.mult), [acc[hkv], rd_, zc], [oo])
                bank = k.bank()
                k.tr(bank[:, 0:16], oo[:, :], ident[0:16, 0:16], reads=[oo, ident], bank=bank, merge=False)
                A("dve", lambda: nc.vector.tensor_copy(mixTs[:, 24 + hkv * 4:24 + hkv * 4 + 4, ts], bank[:, 0:16].rearrange("p (t g) -> p g t", g=4)), [bank], [mixTs], True)
            k.reserved = []

    def outproj(l, prompt, b, xin, xout):
        ntok = T if prompt else NS
        with k.scope():
            HT = T // 2
            if prompt:
                mT = k.sb([128, 32, HT], BF16, "mT")
            else:
                mT = mixTs
            gate = k.sb([128, D], F32, "gate")
            gb = k.sb([128, 1024], F32, "gb")
            P = 128 if prompt else NS
            if prompt:
                k.dma("sp", gate[:], m_scr.t[b:b + 1, 2 * D:3 * D].to_broadcast([128, D]), reads=[m_scr], writes=[gate])
            else:
                for j in range(sbn):
                    k.dma("sp", gate[j * TS:(j + 1) * TS, :], m_scr.t[pb + j:pb + j + 1, 2 * D:3 * D].to_broadcast([TS, D]), reads=[m_scr], writes=[gate], merge=True)
            for q4 in range(4):
                k.dma("sp", gb[0:P, :], b_ada.t[l:l + 1, 2 * D + q4 * 1024:2 * D + (q4 + 1) * 1024].to_broadcast([P, 1024]), reads=[b_ada], writes=[gb])
                A("dve", lambda: nc.vector.tensor_tensor(gate[0:P, q4 * 1024:(q4 + 1) * 1024], gate[0:P, q4 * 1024:(q4 + 1) * 1024], gb[0:P, :], op=ALU.add), [gate, gb], [gate])
            xt = [k.sb([128, 256], F32, "xo") for _ in range(3)]
            xi = 0
            wv = w_out.t[l].rearrange("(c p) n -> p c n", p=128)
            row_base = b * T if prompt else 0
            for half, blk in [(hf, bl) for hf in range(2 if prompt else 1) for bl in range(D // 256)]:
                if prompt and blk == 0:
                    k.dma("sp", mT[:], mixT.t[:, half * HT:(half + 1) * HT].rearrange("(c p) t -> p c t", p=128), reads=[mixT], writes=[mT])
                wb = next_w()
                cs = slice(blk * 256, (blk + 1) * 256)
                k.dma("pool", wb[:], wv[:, :, cs], reads=[w_out], writes=[wb])
                ntl = (HT // P) if prompt else 1
                for ti in range(ntl):
                    r_off = half * HT + ti * P
                    rs = slice(row_base + r_off, row_base + r_off + P)
                    x_ = xt[xi % 3]
                    xi += 1
                    k.dma("act", x_[0:P, :], xin.t[rs, cs], reads=[xin], writes=[x_])
                    bank = k.bank()
                    k.mm(bank[0:P, 0:256], [(mT[:, c, ti * P:(ti + 1) * P], wb[:, c, :]) for c in range(32)], reads=[mT, wb], bank=bank)
                    st = next_stage()
                    A("dve", lambda: nc.vector.tensor_tensor(st[0:P, 0:256], bank[0:P, 0:256], gate[0:P, cs], op=ALU.mult), [bank, gate], [st])
                    A("pool", lambda: nc.gpsimd.tensor_tensor(st[0:P, 0:256], st[0:P, 0:256], x_[0:P, :], op=ALU.add), [st, x_], [st])
                    k.dma("sp", xout.t[rs, cs], st[0:P, 0:256], reads=[st], writes=[xout], merge=True)

    def final_norm(xin, yout, ntok):
        with k.scope():
            gf = k.sb([128, D], F32, "gf")
            k.dma("sp", gf[:], g_final.t[0:1, :].to_broadcast([128, D]), reads=[g_final], writes=[gf])
            xt = [k.sb([128, D], F32, "xf") for _ in range(2)]
            junk = k.sb([128, 2048], BF16, "junkf")
            ssq = k.sb([128, 4], F32, "ssqf")
            P = min(128, ntok)
            for ti in range(ntok // P):
                x_ = xt[ti % 2]
                rs = slice(ti * P, (ti + 1) * P)
                k.dma("act", x_[0:P, :], xin.t[rs, :], reads=[xin], writes=[x_])
                for hh in range(2):
                    A("act", lambda: nc.scalar.activation(out=junk[0:P, :], in_=x_[0:P, hh * 2048:(hh + 1) * 2048], func=AF.Square,
                                                          accum_out=ssq[0:P, hh:hh + 1]), [x_], [junk, ssq])
                A("dve", lambda: nc.vector.tensor_tensor(ssq[0:P, 2:3], ssq[0:P, 0:1], ssq[0:P, 1:2], op=ALU.add), [ssq], [ssq])
                rstd_from_ss(ssq[0:P, 2:3], ssq[0:P, 3:4], D, ssq, P)
                A("dve", lambda: nc.vector.scalar_tensor_tensor(out=x_[0:P, :], in0=x_[0:P, :], scalar=ssq[0:P, 3:4], in1=gf[0:P, :],
                                                                 op0=ALU.mult, op1=ALU.mult), [x_, ssq, gf], [x_])
                k.dma("sp", yout.t[rs, :], x_[0:P, :], reads=[x_], writes=[yout], merge=True)

    for l in range(nl):
        xin = [x_p, x_s] if l == 0 else xa
        xout = xa if l == 0 else xb
        adaln(l)
        if stage == "ada":
            break
        with k.scope():
            Acol = k.sb([128, NR, 32], F32, "Acol")
            Bcol = k.sb([128, NR, 32], F32, "Bcol")
            mod_cols(l, Acol, Bcol)
            for b in range(pb):
                with k.scope():
                    hT = k.sb([128, 32, T], BF16, "hT")
                    h_phase(xin[0], b * T, NT, 128, hT, Acol, Bcol, lambda ti, P, b=b: [(0, P, b)])
                    if stage != "h":
                        inproj(l, hT, T, True, b)
                if stage in ("inproj", "h"):
                    continue
                if stage in ("all", "A"):
                    mixerA_prompt(l)
                if stage in ("all", "B"):
                    mixerB(l, True, b)
                if stage in ("all", "C"):
                    mixerC_prompt(l, b)
                if stage == "all":
                    outproj(l, True, b, xin[0], xout[0])
            with k.scope():
                hTs = k.sb([128, 32, NS], BF16, "hTs")
                h_phase(xin[1], 0, 1, NS, hTs, Acol, Bcol, lambda ti, P: [(j * TS, (j + 1) * TS, pb + j) for j in range(sbn)])
                if stage != "h":
                    inproj(l, hTs, NS, False, 0)
            if stage in ("inproj", "h"):
                continue
            if stage in ("all", "A"):
                mixerA_sample(l)
            if stage in ("all", "B"):
                for j in range(sbn):
                    mixerB(l, False, j)
            if stage == "Cs":
                for j in range(sbn):
                    mixerC_sample(l, j)
            if stage == "all":
                if with_cache:
                    for j in range(sbn):
                        mixerC_sample(l, j)
                else:
                    A("dve", lambda: nc.vector.memset(mixTs[:, 24:32, :], 0.0), [], [mixTs], True)
                outproj(l, False, 0, xin[1], xout[1])
    if stage == "all":
        last = xa if nl == 1 else xb
        final_norm(last[0], O["y_p"], pb * T)
        final_norm(last[1], O["y_s"], NS)
    k.finish()
    return k


_BUILT = {}
NCORES = 4


def kernel(_build_args=None, **inputs):
    inp = {kk: np.asarray(v) for kk, v in inputs.items()}
    ba = dict(nl=DEPTH, pb=4 // NCORES, sbn=8 // NCORES, stage="all", with_cache=True)
    if _build_args:
        ba.update(_build_args)
    ncores = ba.pop("ncores", NCORES)
    key = tuple(sorted(ba.items()))
    if key not in _BUILT:
        _BUILT[key] = build(**ba)
    k = _BUILT[key]
    nl, pb, sbn = ba["nl"], ba["pb"], ba["sbn"]
    f = np.float32
    in_maps = []
    for c in range(ncores):
        ps = slice(c * pb, (c + 1) * pb)
        ss = slice(c * sbn, (c + 1) * sbn)
        m = {
            "x_p": np.ascontiguousarray(inp["x_prompt"][ps], f).reshape(pb * T, D),
            "x_s": np.ascontiguousarray(inp["x_sample"][ss], f).reshape(sbn * TS, D),
            "c_ps": np.ascontiguousarray(np.concatenate([inp["c_prompt"][ps], inp["c_sample"][ss]], 0), f),
            "state_dn": np.ascontiguousarray(inp["state_dn"][:nl, ss], f).reshape(nl, sbn * 16, 128, 128),
            "state_conv": np.ascontiguousarray(inp["state_conv"][:nl, ss], f).reshape(nl, sbn * 3, 6144),
            "w_ada": np.ascontiguousarray(inp["w_ada"][:nl], f),
            "b_ada": np.ascontiguousarray(inp["b_ada"][:nl], f),
            "g_norm": np.ascontiguousarray(inp["g_norm"][:nl], f),
            "w_in": np.ascontiguousarray(inp["w_in"][:nl], f),
            "a_vnorm": np.ascontiguousarray(inp["a_vnorm"][:nl], f),
            "a_ws": np.ascontiguousarray(inp["a_ws"][:nl], f),
            "a_bs": np.ascontiguousarray(inp["a_bs"][:nl], f).reshape(nl, 1024),
            "dn_conv_w": np.ascontiguousarray(inp["dn_conv_w"][:nl], f),
            "dn_a_log": np.ascontiguousarray(inp["dn_a_log"][:nl], f),
            "dn_dt_bias": np.ascontiguousarray(inp["dn_dt_bias"][:nl], f),
            "dn_onorm": np.ascontiguousarray(inp["dn_onorm"][:nl], f),
            "w_out": np.ascontiguousarray(inp["w_out"][:nl], f),
            "g_final": np.ascontiguousarray(inp["g_final"].reshape(1, D), f),
            "consts": CONST_ARR,
        }
        if ba["with_cache"]:
            m["cache_k"] = np.ascontiguousarray(inp["cache_k"][:nl], f).reshape(nl * NPOOL * 8, 16 * 256)
            m["cache_v"] = np.ascontiguousarray(inp["cache_v"][:nl], f).reshape(nl * NPOOL * 8, 16 * 256)
            m["cache_kidx"] = np.ascontiguousarray(inp["cache_kidx"][:nl], f).reshape(nl * NPOOL, 128 * 64)
            m["page_table"] = np.ascontiguousarray(inp["page_table"][ss].reshape(sbn * 128, 1), np.int32)
        in_maps.append(m)
    res = run_bass_kernel_spmd(k.nc, in_maps, core_ids=list(range(ncores)))
    R = res.results

    def cat(name, tail, per):
        parts = [R[c][name].reshape((nl, per) + tail) for c in range(ncores)]
        return np.concatenate(parts, 1)

    y_p = np.concatenate([R[c]["y_p"].reshape(pb, T, D) for c in range(ncores)], 0)
    y_s = np.concatenate([R[c]["y_s"].reshape(sbn, TS, D) for c in range(ncores)], 0)
    outs = (y_p, y_s,
            cat("p_k", (T, 2, 128), pb), cat("p_v", (T, 2, 128), pb), cat("p_kidx", (T, 64), pb),
            cat("p_dn", (16, 128, 128), pb), cat("p_conv", (3, 6144), pb),
            cat("s_k", (TS, 2, 128), sbn), cat("s_v", (TS, 2, 128), sbn), cat("s_kidx", (TS, 64), sbn),
            cat("s_dn", (16, 128, 128), sbn), cat("s_conv", (3, 6144), sbn), cat("s_amlp_v", (TS, 1024), sbn))
    return tuple(np.ascontiguousarray(o, dtype=np.float32) for o in outs)
```

```python
import numpy as np
from contextlib import ExitStack, contextmanager
import concourse.bass as bass
import concourse.mybir as mybir
from concourse.bass_utils import run_bass_kernel_spmd

F32 = mybir.dt.float32
BF16 = mybir.dt.bfloat16
I32 = mybir.dt.int32
AF = mybir.ActivationFunctionType
ALU = mybir.AluOpType
AX = mybir.AxisListType

SAME_ENG_SYNC = True
N_DSEM = 20

D = 4096
T = 2048
NT = 16
TS = 4
DEPTH = 2
D_IN = 14960
NPOOL = 1280
EPS = 1e-6
NEG = -1.0e30

SEGS = [("u", 0, 1024), ("v", 1024, 1024), ("za", 2048, 1024), ("qkv", 3072, 6144), ("zb", 9216, 2048),
        ("ab", 11264, 32), ("qc", 11296, 1024), ("kc", 12320, 256), ("vc", 12576, 256), ("zc", 12832, 1024),
        ("qi", 13856, 1024), ("ki", 14880, 64), ("wi", 14944, 16)]
SEG = {n: (s, w) for n, s, w in SEGS}
FMSEG = ["u", "za", "qkv", "qc", "kc", "zc", "qi", "ki"]
TMSEG = ["v", "zb", "ab", "kc", "vc", "ki", "wi"]
FM_ROWS = {}
_r = 0
for _n in FMSEG:
    FM_ROWS[_n] = _r
    _r += max(SEG[_n][1], 128)
FM_TOT = _r
SG = {}
_g = 0
for _n in FMSEG:
    SG[_n] = _g
    _g += max(SEG[_n][1], 128) // 128
NSG = _g


class Buf:
    __slots__ = ("t", "wr", "rd", "name", "excl")

    def __init__(self, t, name="", excl=False):
        self.t = t
        self.wr = {}
        self.rd = {}
        self.name = name
        self.excl = excl

    def __getitem__(self, idx):
        return self.t[idx]


class K:
    def __init__(self):
        self.nc = bass.Bass("TRN2", target_bir_lowering=False)
        self.stack = [ExitStack()]
        nc = self.nc
        self.eng = {"pe": nc.tensor, "act": nc.scalar, "dve": nc.vector, "pool": nc.gpsimd, "sp": nc.sync}
        self.sem = {}
        self.cnt = {}
        es = self.stack[0]
        for e in ("pe", "act", "dve", "pool"):
            self.sem[e] = es.enter_context(nc.semaphore("s_" + e))
            self.cnt[e] = 0
        self.dq = {}
        for q in ("sp", "pool", "act"):
            sems = []
            for i in range(N_DSEM):
                key = "d_%s_%d" % (q, i)
                self.sem[key] = es.enter_context(nc.semaphore(key))
                self.cnt[key] = 0
                sems.append(key)
            self.dq[q] = [sems, 0]
        self.waited = {e: {} for e in ("pe", "act", "dve", "pool", "sp")}
        self.nbuf = 0
        self.banks = []
        self.bank_i = 0
        self.reserved = []
        self.ev_i = 0

    @contextmanager
    def scope(self):
        es = ExitStack()
        self.stack.append(es)
        try:
            yield
        finally:
            self.barrier()
            self.stack.pop()
            es.close()

    def sb(self, shape, dt=F32, name=None):
        self.nbuf += 1
        name = (name or "sb") + "_%d" % self.nbuf
        t = self.stack[-1].enter_context(self.nc.sbuf_tensor(name, list(shape), dt))
        return Buf(t, name)

    def ps(self, shape, dt=F32, name=None):
        self.nbuf += 1
        name = (name or "ps") + "_%d" % self.nbuf
        t = self.stack[-1].enter_context(self.nc.psum_tensor(name, list(shape), dt))
        return Buf(t, name, excl=True)

    def dram(self, name, shape, dt=F32, kind="Internal"):
        t = self.nc.dram_tensor(name, list(shape), dt, kind=kind)
        return Buf(t.ap(), name)

    def bank(self):
        while True:
            b = self.banks[self.bank_i % len(self.banks)]
            self.bank_i += 1
            if b not in self.reserved:
                return b

    def ev(self):
        self.ev_i += 1
        return ("act", "dve")[self.ev_i % 2]

    def _deps(self, reads, writes, merge, e=None):
        deps = {}

        def add(d):
            for kk, c in d.items():
                if deps.get(kk, 0) < c:
                    deps[kk] = c
        for b in reads:
            add(b.wr)
            if b.excl:
                add({kk: c for kk, c in b.rd.items() if kk != e})
        for b in writes:
            add(b.rd)
            if not merge:
                add(b.wr)
        return deps

    def _emit_waits(self, e, deps):
        w = self.waited[e]
        for kk, c in deps.items():
            if kk == e and (e == "pe" or not SAME_ENG_SYNC):
                continue
            if w.get(kk, 0) >= c:
                continue
            self.eng[e].wait_ge(self.sem[kk], c)
            w[kk] = c

    def _mark(self, key, c, reads, writes, merge):
        for b in reads:
            if b.rd.get(key, 0) < c:
                b.rd[key] = c
        for b in writes:
            if merge:
                b.wr[key] = c
            else:
                b.wr = {key: c}
            b.rd = {}

    def op(self, e, fn, reads=(), writes=(), merge=False):
        deps = self._deps(reads, writes, merge, e)
        self._emit_waits(e, deps)
        ins = fn()
        self.cnt[e] += 1
        ins.then_inc(self.sem[e], 1)
        self._mark(e, self.cnt[e], reads, writes, merge)
        return ins

    def dma(self, q, out, in_, reads=(), writes=(), merge=False, indirect=None, elem_off=0, **kw):
        sems, i = self.dq[q]
        key = sems[i % len(sems)]
        self.dq[q][1] = i + 1
        deps = self._deps(reads, writes, merge)
        if self.cnt[key] > 0:
            deps[key] = max(deps.get(key, 0), self.cnt[key])
        self._emit_waits(q, deps)
        if indirect is not None:
            ins = self.nc.gpsimd.indirect_dma_start(out=out, out_offset=None, in_=in_,
                                                    in_offset=bass.IndirectOffsetOnAxis(ap=indirect, axis=0), element_offset=elem_off)
        else:
            ins = self.eng[q].dma_start(out=out, in_=in_, **kw)
        self.cnt[key] += 16
        ins.then_inc(self.sem[key], 16)
        self._mark(key, self.cnt[key], reads, writes, merge)

    def finish(self):
        deps = {kk: c for kk, c in self.cnt.items() if c > 0}
        self._emit_waits("sp", deps)

    def barrier(self):
        snap = {kk: c for kk, c in self.cnt.items() if c > 0}
        for e in ("pe", "act", "dve", "pool", "sp"):
            self._emit_waits(e, {kk: c for kk, c in snap.items() if kk != e})

    def mm(self, out_ap, pairs, reads, bank, merge=False):
        nc = self.nc
        n = len(pairs)

        def fn():
            ins = None
            for i, (l, r) in enumerate(pairs):
                ins = nc.tensor.matmul(out_ap, lhsT=l, rhs=r, start=(i == 0), stop=(i == n - 1))
            return ins
        self.op("pe", fn, reads=reads, writes=[bank], merge=merge)

    def tr(self, out_ap, in_ap, ident_ap, reads, bank, merge=True):
        nc = self.nc
        self.op("pe", lambda: nc.tensor.transpose(out_ap, in_ap, ident_ap), reads=reads, writes=[bank], merge=merge)

    def copy(self, e, out_ap, in_ap, reads, writes, merge=False):
        nc = self.nc
        if e == "act":
            self.op("act", lambda: nc.scalar.copy(out=out_ap, in_=in_ap), reads=reads, writes=writes, merge=merge)
        elif e == "dve":
            self.op("dve", lambda: nc.vector.tensor_copy(out_ap, in_ap), reads=reads, writes=writes, merge=merge)
        else:
            self.op("pool", lambda: nc.gpsimd.tensor_copy(out_ap, in_ap), reads=reads, writes=writes, merge=merge)


def make_consts():
    c = {}
    c["ident"] = np.eye(128, dtype=np.float32)
    i = np.arange(128)
    c["tri_le"] = (i[:, None] <= i[None, :]).astype(np.float32)
    c["neg_lt"] = np.where(i[:, None] < i[None, :], NEG, 0.0).astype(np.float32)
    c["neg_gt"] = np.where(i[:, None] > i[None, :], NEG, 0.0).astype(np.float32)
    c["pos_le"] = np.where(i[:, None] <= i[None, :], -NEG, 0.0).astype(np.float32)
    c["ones"] = np.ones((128, 128), np.float32)
    c["bm16"] = (i[:, None] // 16 == i[None, :]).astype(np.float32)
    for nm, r in (("sel127", 127), ("sel3", 3)):
        sel = np.zeros((128, 128), np.float32)
        sel[r, :] = 1.0
        c[nm] = sel
    names = list(c.keys())
    arr = np.stack([c[n] for n in names], 0)
    return names, arr


CONST_NAMES, CONST_ARR = make_consts()


def build(nl=DEPTH, pb=4, sbn=8, stage="all", with_cache=True):
    k = K()
    nc = k.nc
    NS = sbn * TS
    NR = pb + sbn
    x_p = k.dram("x_p", [pb * T, D], kind="ExternalInput")
    x_s = k.dram("x_s", [NS, D], kind="ExternalInput")
    c_ps = k.dram("c_ps", [NR, D], kind="ExternalInput")
    if with_cache:
        cache_k = k.dram("cache_k", [nl * NPOOL * 8, 16 * 256], kind="ExternalInput")
        cache_v = k.dram("cache_v", [nl * NPOOL * 8, 16 * 256], kind="ExternalInput")
        cache_kidx = k.dram("cache_kidx", [nl * NPOOL, 128 * 64], kind="ExternalInput")
        page_table = k.dram("page_table", [sbn * 128, 1], I32, kind="ExternalInput")
    state_dn = k.dram("state_dn", [nl, sbn * 16, 128, 128], kind="ExternalInput")
    state_conv = k.dram("state_conv", [nl, sbn * 3, 6144], kind="ExternalInput")
    w_ada = k.dram("w_ada", [nl, D, 3 * D], kind="ExternalInput")
    b_ada = k.dram("b_ada", [nl, 3 * D], kind="ExternalInput")
    g_norm = k.dram("g_norm", [nl, D], kind="ExternalInput")
    w_in = k.dram("w_in", [nl, D, D_IN], kind="ExternalInput")
    a_vnorm = k.dram("a_vnorm", [nl, 1024], kind="ExternalInput")
    a_ws = k.dram("a_ws", [nl, 8, 128, 128], kind="ExternalInput")
    a_bs = k.dram("a_bs", [nl, 8 * 128], kind="ExternalInput")
    dn_conv_w = k.dram("dn_conv_w", [nl, 4, 6144], kind="ExternalInput")
    dn_a_log = k.dram("dn_a_log", [nl, 16], kind="ExternalInput")
    dn_dt_bias = k.dram("dn_dt_bias", [nl, 16], kind="ExternalInput")
    dn_onorm = k.dram("dn_onorm", [nl, 128], kind="ExternalInput")
    w_out = k.dram("w_out", [nl, D, D], kind="ExternalInput")
    g_final = k.dram("g_final", [1, D], kind="ExternalInput")
    consts = k.dram("consts", list(CONST_ARR.shape), kind="ExternalInput")

    O = {}
    for name, shape in [("y_p", [pb * T, D]), ("y_s", [NS, D]), ("p_k", [nl, pb * T, 256]), ("p_v", [nl, pb * T, 256]),
                        ("p_kidx", [nl, pb * T, 64]), ("p_dn", [nl, pb * 16, 128, 128]), ("p_conv", [nl, pb * 3, 6144]),
                        ("s_k", [nl, NS, 256]), ("s_v", [nl, NS, 256]), ("s_kidx", [nl, NS, 64]),
                        ("s_dn", [nl, sbn * 16, 128, 128]), ("s_conv", [nl, sbn * 3, 6144]), ("s_amlp_v", [nl, NS, 1024])]:
        O[name] = k.dram(name, shape, kind="ExternalOutput")

    m_scr = k.dram("m_scr", [NR, 3 * D])
    P_fm = k.dram("P_fm", [FM_TOT, T])
    P_v = k.dram("P_v", [T, 1024])
    P_zb = k.dram("P_zb", [T, 2048])
    P_ab = k.dram("P_ab", [T, 32])
    P_wi = k.dram("P_wi", [T, 16])
    sTM = k.dram("sTM", [NS, D_IN])
    xa = [k.dram("xa_p", [pb * T, D]), k.dram("xa_s", [NS, D])]
    xb = [k.dram("xb_p", [pb * T, D]), k.dram("xb_s", [NS, D])]
    mixT = k.dram("mixT", [D, T], BF16)

    CT = {}
    for i, n in enumerate(CONST_NAMES):
        CT[n] = k.sb([128, 128], F32, "c_" + n)
        k.dma("sp", CT[n][:], consts.t[i], reads=[consts], writes=[CT[n]])
    ident = CT["ident"]
    ones = CT["ones"]
    k.banks = [k.ps([128, 512], F32, "bank") for _ in range(8)]
    wbuf = [k.sb([128, 32, 256], BF16, "wbuf") for _ in range(2)]
    wb_i = [0]
    stage_bufs = [k.sb([128, 512], F32, "stage") for _ in range(4)]
    st_i = [0]

    def next_w():
        b = wbuf[wb_i[0] % 2]
        wb_i[0] += 1
        return b

    def next_stage():
        b = stage_bufs[st_i[0] % len(stage_bufs)]
        st_i[0] += 1
        return b

    def evac_to_dram(bank, P, W, dst_ap, dst_buf, extra=()):
        st = next_stage()
        k.copy(k.ev(), st[0:P, 0:W], bank[0:P, 0:W], reads=[bank], writes=[st])
        k.dma("sp", dst_ap, st[0:P, 0:W], reads=[st], writes=[dst_buf], merge=True)
        for ap, buf in extra:
            k.dma("sp", ap, st[0:P, 0:W], reads=[st], writes=[buf], merge=True)

    def A(e, f, reads, writes, merge=False):
        return k.op(e, f, reads=reads, writes=writes, merge=merge)

    def rstd_from_ss(ss_ap, out_ap, n, buf, P):
        A("act", lambda: nc.scalar.activation(out=out_ap, in_=ss_ap, func=AF.Sqrt, bias=EPS, scale=1.0 / n), [buf], [buf])
        A("dve", lambda: nc.vector.reciprocal(out_ap, out_ap), [buf], [buf])

    mixTs = k.sb([128, 32, NS], BF16, "mixTs")
    sFM = k.sb([128, NSG, NS], F32, "sFM")

    def adaln(l):
        with k.scope():
            craw = k.sb([128, NR, 32], F32, "craw")
            cbf = k.sb([128, NR, 32], BF16, "cbf")
            k.dma("sp", craw[:], c_ps.t.rearrange("r (p c) -> p r c", c=32), reads=[c_ps], writes=[craw])
            A("act", lambda: nc.scalar.activation(out=cbf[:], in_=craw[:], func=AF.Silu), [craw], [cbf])
            wv = w_ada.t[l].rearrange("(p c) n -> p c n", c=32)
            for blk in range(3 * D // 256):
                wb = next_w()
                k.dma("pool", wb[:], wv[:, :, blk * 256:(blk + 1) * 256], reads=[w_ada], writes=[wb])
                bank = k.bank()
                k.mm(bank[0:NR, 0:256], [(cbf[:, :, c], wb[:, c, :]) for c in range(32)], reads=[cbf, wb], bank=bank)
                evac_to_dram(bank, NR, 256, m_scr.t[:, blk * 256:(blk + 1) * 256], m_scr)

    def mod_cols(l, Acol, Bcol):
        with k.scope():
            nrow = 2 * NR + 3
            rows = k.sb([32, nrow, 128], F32, "rows")
            for r in range(NR):
                for j in range(2):
                    k.dma("sp", rows[:, r * 2 + j, :], m_scr.t[r, j * D:(j + 1) * D].rearrange("(c p) -> c p", p=128),
                          reads=[m_scr], writes=[rows], merge=True)
            for j in range(2):
                k.dma("sp", rows[:, 2 * NR + j, :], b_ada.t[l, j * D:(j + 1) * D].rearrange("(c p) -> c p", p=128),
                      reads=[b_ada], writes=[rows], merge=True)
            k.dma("sp", rows[:, 2 * NR + 2, :], g_norm.t[l].rearrange("(c p) -> c p", p=128), reads=[g_norm], writes=[rows], merge=True)
            cols = k.sb([128, nrow, 32], F32, "cols")
            for j0 in range(0, nrow, 16):
                bank = k.bank()
                n = min(16, nrow - j0)
                for j in range(n):
                    k.tr(bank[:, j * 32:(j + 1) * 32], rows[:, j0 + j, :], ident[0:32, 0:32], reads=[rows, ident], bank=bank, merge=(j > 0))
                k.copy("dve", cols[:, j0:j0 + n, :].rearrange("p a b -> p (a b)"), bank[:, 0:n * 32], reads=[bank], writes=[cols], merge=True)
            for r in range(NR):
                A("dve", lambda: nc.vector.tensor_tensor(Bcol[:, r, :], cols[:, r * 2 + 0, :], cols[:, 2 * NR, :], op=ALU.add), [cols], [Bcol], True)
                A("dve", lambda: nc.vector.scalar_tensor_tensor(out=Acol[:, r, :], in0=cols[:, r * 2 + 1, :], scalar=1.0,
                                                                 in1=cols[:, 2 * NR + 1, :], op0=ALU.add, op1=ALU.add), [cols], [Acol], True)
                A("dve", lambda: nc.vector.tensor_tensor(Acol[:, r, :], Acol[:, r, :], cols[:, 2 * NR + 2, :], op=ALU.mult), [cols, Acol], [Acol], True)

    def h_phase(xin, row0, ntile, P, hT, Acol, Bcol, rows_of):
        with k.scope():
            xt = k.sb([128, D], F32, "xt")
            junk = k.sb([128, 1024], BF16, "junk")
            ssq = k.sb([128, 8], F32, "ssq")
            for ti in range(ntile):
                k.dma("act", xt[0:P, :], xin.t[row0 + ti * P:row0 + (ti + 1) * P, :], reads=[xin], writes=[xt])
                for hh in range(4):
                    A("act", lambda: nc.scalar.activation(out=junk[0:P, :], in_=xt[0:P, hh * 1024:(hh + 1) * 1024],
                                                          func=AF.Square, accum_out=ssq[0:P, hh:hh + 1]), [xt], [junk, ssq])
                A("dve", lambda: nc.vector.tensor_reduce(out=ssq[0:P, 4:5], in_=ssq[0:P, 0:4], axis=AX.X, op=ALU.add), [ssq], [ssq])
                rstd_from_ss(ssq[0:P, 4:5], ssq[0:P, 5:6], D, ssq, P)
                A("dve", lambda: nc.vector.tensor_scalar(xt[0:P, :], xt[0:P, :], ssq[0:P, 5:6], None, op0=ALU.mult), [xt, ssq], [xt])
                for c4 in range(8):
                    bank = k.bank()
                    for j in range(4):
                        c = c4 * 4 + j
                        k.tr(bank[:, j * 128:j * 128 + P], xt[0:P, c * 128:(c + 1) * 128], ident[0:P, 0:P],
                             reads=[xt, ident], bank=bank, merge=(j > 0))
                    eng_ = k.ev()
                    for j in range(4):
                        c = c4 * 4 + j
                        for (p0, p1, r) in rows_of(ti, P):
                            dst = hT[:, c, ti * P + p0:ti * P + p1]
                            src = bank[:, j * 128 + p0:j * 128 + p1]
                            if eng_ == "act":
                                A("act", lambda: nc.scalar.activation(out=dst, in_=src, func=AF.Identity,
                                                                      bias=Bcol[:, r, c:c + 1], scale=Acol[:, r, c:c + 1]),
                                  [bank, Acol, Bcol], [hT], True)
                            else:
                                A("dve", lambda: nc.vector.tensor_scalar(dst, src, Acol[:, r, c:c + 1], Bcol[:, r, c:c + 1],
                                                                         op0=ALU.mult, op1=ALU.add), [bank, Acol, Bcol], [hT], True)

    def inproj_sample(l, name, off, w, col0, wb, hTs):
        bank = k.bank()
        k.mm(bank[0:NS, 0:w], [(hTs[:, c, :], wb[:, c, 0:w]) for c in range(32)], reads=[hTs, wb], bank=bank)
        extra = []
        if name == "kc":
            extra.append((O["s_k"].t[l, :, :], O["s_k"]))
        if name == "vc":
            extra.append((O["s_v"].t[l, :, :], O["s_v"]))
        if name == "ki":
            extra.append((O["s_kidx"].t[l, :, :], O["s_kidx"]))
        evac_to_dram(bank, NS, w, sTM.t[:, col0:col0 + w], sTM, extra)
        if name == "qkv":
            for j in range(sbn):
                k.dma("sp", O["s_conv"].t[l, j * 3:(j + 1) * 3, off:off + w], sTM.t[j * 4 + 1:j * 4 + 4, col0:col0 + w],
                      reads=[sTM], writes=[O["s_conv"]], merge=True)
        if name in FMSEG:
            wfm = 128 if name == "ki" else w
            for g0 in range(0, wfm, 128):
                bank = k.bank()
                k.mm(bank[:, 0:NS], [(wb[:, c, g0:g0 + 128], hTs[:, c, :]) for c in range(32)], reads=[hTs, wb], bank=bank)
                sg = SG[name] + (off + g0) // 128
                k.copy(k.ev(), sFM[:, sg, :], bank[:, 0:NS], reads=[bank], writes=[sFM], merge=True)

    def inproj(l, hT, ntok, prompt, b, hTs=None):
        wv = w_in.t[l].rearrange("(c p) n -> p c n", p=128)
        for name, s0, sw in SEGS:
            for off in range(0, sw, 256):
                w = min(256, sw - off)
                col0 = s0 + off
                wb = next_w()
                k.dma("pool", wb[:, :, 0:w], wv[:, :, col0:col0 + w], reads=[w_in], writes=[wb])
                if name == "ki":
                    k.dma("pool", wb[:, :, 64:128], wv[:, :, col0:col0 + w], reads=[w_in], writes=[wb], merge=True)
                if hTs is not None:
                    inproj_sample(l, name, off, w, col0, wb, hTs)
                if not prompt:
                    continue
                if name == "qkv":
                    bank = k.bank()
                    k.mm(bank[0:3, 0:w], [(hT[:, c, T - 3:T], wb[:, c, 0:w]) for c in range(32)], reads=[hT, wb], bank=bank)
                    evac_to_dram(bank, 3, w, O["p_conv"].t[l, b * 3:(b + 1) * 3, off:off + w], O["p_conv"])
                if name in TMSEG:
                    for ti in range(NT):
                        bank = k.bank()
                        k.mm(bank[:, 0:w], [(hT[:, c, ti * 128:(ti + 1) * 128], wb[:, c, 0:w]) for c in range(32)],
                             reads=[hT, wb], bank=bank)
                        rs = slice(ti * 128, (ti + 1) * 128)
                        ors = slice(b * T + ti * 128, b * T + (ti + 1) * 128)
                        if name == "v":
                            dst, dbuf = P_v.t[rs, off:off + w], P_v
                        elif name == "zb":
                            dst, dbuf = P_zb.t[rs, off:off + w], P_zb
                        elif name == "ab":
                            dst, dbuf = P_ab.t[rs, :], P_ab
                        elif name == "wi":
                            dst, dbuf = P_wi.t[rs, :], P_wi
                        elif name == "kc":
                            dst, dbuf = O["p_k"].t[l, ors, :], O["p_k"]
                        elif name == "vc":
                            dst, dbuf = O["p_v"].t[l, ors, :], O["p_v"]
                        else:
                            dst, dbuf = O["p_kidx"].t[l, ors, :], O["p_kidx"]
                        evac_to_dram(bank, 128, w, dst, dbuf)
                if name in FMSEG:
                    wfm = 128 if name == "ki" else w
                    for g0 in range(0, wfm, 128):
                        row0 = FM_ROWS[name] + off + g0
                        for tb in range(4):
                            bank = k.bank()
                            k.mm(bank[:, :], [(wb[:, c, g0:g0 + 128], hT[:, c, tb * 512:(tb + 1) * 512]) for c in range(32)],
                                 reads=[hT, wb], bank=bank)
                            evac_to_dram(bank, 128, 512, P_fm.t[row0:row0 + 128, tb * 512:(tb + 1) * 512], P_fm)

    def mixerA_setup(l, wmT, absr, avn):
        with k.scope():
            wnat = k.sb([128, 8, 128], F32, "wnat")
            k.dma("sp", wnat[:], a_ws.t[l].rearrange("h t s -> t h s"), reads=[a_ws], writes=[wnat])
            for hg in range(2):
                bank = k.bank()
                for hh in range(4):
                    k.tr(bank[:, hh * 128:(hh + 1) * 128], wnat[:, hg * 4 + hh, :], ident[:, :], reads=[wnat, ident], bank=bank, merge=(hh > 0))
                for hh in range(4):
                    A("dve", lambda: nc.vector.tensor_tensor(wmT[:, hg * 4 + hh, :], bank[:, hh * 128:(hh + 1) * 128], CT["tri_le"][:, :], op=ALU.mult),
                      [bank, CT["tri_le"]], [wmT], True)
        k.dma("sp", absr[0:1, :], a_bs.t[l:l + 1, :], reads=[a_bs], writes=[absr])
        k.dma("sp", avn[:], a_vnorm.t[l:l + 1, :].to_broadcast([128, 1024]), reads=[a_vnorm], writes=[avn])

    def mixerA_chunk(l, C, vt, uT, zaT, wmT, absr, avn, tmp, out_cb, vout_cb=None):
        (uT_ap, uT_buf), (zaT_ap, zaT_buf) = uT, zaT
        g1, dd, st4, gu, sz, yb = tmp
        A("act", lambda: nc.scalar.activation(out=g1[0:C, :], in_=vt[0:C, :], func=AF.Gelu_apprx_tanh, accum_out=st4[0:C, 0:1]), [vt], [g1, st4])
        A("dve", lambda: nc.vector.tensor_scalar(st4[0:C, 1:2], st4[0:C, 0:1], -1.0 / 1024, None, op0=ALU.mult), [st4], [st4])
        A("dve", lambda: nc.vector.tensor_scalar(dd[0:C, :], g1[0:C, :], st4[0:C, 1:2], None, op0=ALU.add), [g1, st4], [dd])
        A("act", lambda: nc.scalar.activation(out=g1[0:C, :], in_=dd[0:C, :], func=AF.Square, accum_out=st4[0:C, 2:3]), [dd], [g1, st4])
        rstd_from_ss(st4[0:C, 2:3], st4[0:C, 3:4], 1024, st4, C)
        A("dve", lambda: nc.vector.scalar_tensor_tensor(out=dd[0:C, :], in0=dd[0:C, :], scalar=st4[0:C, 3:4], in1=avn[0:C, :],
                                                         op0=ALU.mult, op1=ALU.mult), [dd, st4, avn], [dd])
        if vout_cb is not None:
            vout_cb(dd)
        A("act", lambda: nc.scalar.activation(out=gu[:, :, 0:C], in_=uT_ap, func=AF.Gelu_apprx_tanh), [uT_buf], [gu])
        A("act", lambda: nc.scalar.activation(out=sz[:, :, 0:C], in_=zaT_ap, func=AF.Silu), [zaT_buf], [sz])
        for hg in range(2):
            bank = k.bank()
            for hh in range(4):
                h = hg * 4 + hh
                k.mm(bank[:, hh * C:(hh + 1) * C], [(dd[0:C, h * 128:(h + 1) * 128], wmT[0:C, h, 0:C]),
                                                   (ones[0:1, :], absr[0:1, h * 128:h * 128 + C])],
                     reads=[dd, wmT, ones, absr], bank=bank, merge=(hh > 0))
            A("dve", lambda: nc.vector.tensor_tensor(gu[:, hg * 4:(hg + 1) * 4, 0:C], gu[:, hg * 4:(hg + 1) * 4, 0:C],
                                                     bank[:, 0:4 * C].rearrange("p (h c) -> p h c", c=C), op=ALU.mult), [gu, bank], [gu])
        A("dve", lambda: nc.vector.tensor_tensor(yb[:, :, 0:C], gu[:, :, 0:C], sz[:, :, 0:C], op=ALU.mult), [gu, sz], [yb])
        out_cb(yb)

    def mixerA_prompt(l):
        with k.scope():
            wmT = k.sb([128, 8, 128], F32, "wmT")
            absr = k.sb([1, 1024], F32, "absr")
            avn = k.sb([128, 1024], F32, "avn")
            mixerA_setup(l, wmT, absr, avn)
            vt = [k.sb([128, 1024], F32, "vt") for _ in range(2)]
            uT = [k.sb([128, 8, 128], F32, "uT") for _ in range(2)]
            zaT = [k.sb([128, 8, 128], F32, "zaT") for _ in range(2)]
            tmp = (k.sb([128, 1024], F32, "g1"), k.sb([128, 1024], F32, "dd"), k.sb([128, 4], F32, "st4"),
                   k.sb([128, 8, 128], F32, "gu"), k.sb([128, 8, 128], F32, "sz"), k.sb([128, 8, 128], BF16, "yb"))
            for ci in range(NT):
                ts = slice(ci * 128, (ci + 1) * 128)
                v_, u_, z_ = vt[ci % 2], uT[ci % 2], zaT[ci % 2]
                k.dma("sp", v_[:], P_v.t[ts, :], reads=[P_v], writes=[v_])
                r0 = FM_ROWS["u"]
                k.dma("act", u_[:], P_fm.t[r0:r0 + 1024, ts].rearrange("(h d) t -> d h t", d=128), reads=[P_fm], writes=[u_])
                r0 = FM_ROWS["za"]
                k.dma("act", z_[:], P_fm.t[r0:r0 + 1024, ts].rearrange("(h d) t -> d h t", d=128), reads=[P_fm], writes=[z_])

                def out_cb(yb, ts=ts):
                    k.dma("sp", mixT.t[0:1024, ts].rearrange("(h d) t -> d h t", d=128), yb[:], reads=[yb], writes=[mixT], merge=True)
                mixerA_chunk(l, 128, v_, (u_[:], u_), (z_[:], z_), wmT, absr, avn, tmp, out_cb)

    def mixerA_sample(l):
        with k.scope():
            wmT = k.sb([128, 8, 128], F32, "wmT")
            absr = k.sb([1, 1024], F32, "absr")
            avn = k.sb([128, 1024], F32, "avn")
            mixerA_setup(l, wmT, absr, avn)
            vt = k.sb([TS, 1024], F32, "vt")
            tmp = (k.sb([TS, 1024], F32, "g1"), k.sb([TS, 1024], F32, "dd"), k.sb([TS, 4], F32, "st4"),
                   k.sb([128, 8, TS], F32, "gu"), k.sb([128, 8, TS], F32, "sz"), k.sb([128, 8, TS], BF16, "yb"))
            for j in range(sbn):
                ts = slice(j * TS, (j + 1) * TS)
                k.dma("sp", vt[:], sTM.t[ts, 1024:2048], reads=[sTM], writes=[vt])

                def out_cb(yb, ts=ts):
                    A("dve", lambda: nc.vector.tensor_copy(mixTs[:, 0:8, ts], yb[:]), [yb], [mixTs], True)

                def vout_cb(dd, ts=ts):
                    k.dma("sp", O["s_amlp_v"].t[l, ts, :], dd[0:TS, :], reads=[dd], writes=[O["s_amlp_v"]], merge=True)
                mixerA_chunk(l, TS, vt, (sFM[:, SG["u"]:SG["u"] + 8, ts], sFM), (sFM[:, SG["za"]:SG["za"] + 8, ts], sFM),
                             wmT, absr, avn, tmp, out_cb, vout_cb)

    def mixerB(l, prompt, bj):
        L = T if prompt else TS
        C = 128 if prompt else TS
        nch = L // C
        sel = CT["sel127"] if prompt else CT["sel3"]
        nsq = 6 if prompt else 1
        with k.scope():
            convw = k.sb([128, 48, 4], F32, "convw")
            with k.scope():
                cw_nat = k.sb([4, 6144], F32, "cw_nat")
                k.dma("sp", cw_nat[:], dn_conv_w.t[l], reads=[dn_conv_w], writes=[cw_nat])
                bank = k.bank()
                for g in range(48):
                    k.tr(bank[:, g * 4:(g + 1) * 4], cw_nat[0:4, g * 128:(g + 1) * 128], ident[0:4, 0:4], reads=[cw_nat, ident], bank=bank, merge=(g > 0))
                k.copy("dve", convw[:].rearrange("p g j -> p (g j)"), bank[:, 0:192], reads=[bank], writes=[convw])
            hp = k.sb([128, 3, 16], F32, "hp")
            k.dma("sp", hp[:, 0, :], dn_a_log.t[l:l + 1, :].to_broadcast([128, 16]), reads=[dn_a_log], writes=[hp], merge=True)
            k.dma("sp", hp[:, 1, :], dn_dt_bias.t[l:l + 1, :].to_broadcast([128, 16]), reads=[dn_dt_bias], writes=[hp], merge=True)
            A("act", lambda: nc.scalar.activation(out=hp[:, 2, :], in_=hp[:, 0, :], func=AF.Exp), [hp], [hp])
            A("dve", lambda: nc.vector.tensor_scalar(hp[:, 2, :], hp[:, 2, :], -1.0, None, op0=ALU.mult), [hp], [hp])
            onb = k.sb([128, 128], F32, "onb")
            k.dma("sp", onb[:], dn_onorm.t[l:l + 1, :].to_broadcast([128, 128]), reads=[dn_onorm], writes=[onb])
            abt = k.sb([128, nch, 32], F32, "abt")
            if prompt:
                k.dma("sp", abt[:], P_ab.t.rearrange("(n c) f -> c n f", c=128), reads=[P_ab], writes=[abt])
            else:
                k.dma("sp", abt[0:C, 0, :], sTM.t[bj * TS:(bj + 1) * TS, SEG["ab"][0]:SEG["ab"][0] + 32], reads=[sTM], writes=[abt])
            gt = k.sb([128, nch, 16], F32, "gt")
            beta = k.sb([128, nch, 16], F32, "beta")
            Gt = k.sb([128, nch, 16], F32, "Gt")
            eG = k.sb([128, nch, 16], F32, "eG")
            eGd = k.sb([128, nch, 16], F32, "eGd")
            eGl = k.sb([128, nch, 16], F32, "eGl")
            bE = k.sb([128, nch, 16], F32, "bE")
            nbeta = k.sb([128, nch, 16], F32, "nbeta")
            for n in range(nch):
                A("dve", lambda: nc.vector.tensor_tensor(gt[0:C, n, :], abt[0:C, n, 0:16], hp[0:C, 1, :], op=ALU.add), [abt, hp], [gt], True)
            A("act", lambda: nc.scalar.activation(out=gt[0:C], in_=gt[0:C], func=AF.Exp), [gt], [gt])
            A("act", lambda: nc.scalar.activation(out=gt[0:C], in_=gt[0:C], func=AF.Ln, bias=1.0, scale=1.0), [gt], [gt])
            for n in range(nch):
                A("dve", lambda: nc.vector.tensor_tensor(gt[0:C, n, :], gt[0:C, n, :], hp[0:C, 2, :], op=ALU.mult), [gt, hp], [gt], True)
            A("act", lambda: nc.scalar.activation(out=beta[0:C], in_=abt[0:C, :, 16:32], func=AF.Sigmoid), [abt], [beta])
            A("dve", lambda: nc.vector.tensor_scalar(nbeta[0:C], beta[0:C], -1.0, None, op0=ALU.mult), [beta], [nbeta])
            bank = k.bank()
            k.mm(bank[0:C, 0:nch * 16], [(CT["tri_le"][0:C, 0:C], gt[0:C].rearrange("p n h -> p (n h)"))], reads=[CT["tri_le"], gt], bank=bank)
            k.copy("dve", Gt[0:C].rearrange("p n h -> p (n h)"), bank[0:C, 0:nch * 16], reads=[bank], writes=[Gt])
            A("act", lambda: nc.scalar.activation(out=eG[0:C], in_=Gt[0:C], func=AF.Exp), [Gt], [eG])
            A("dve", lambda: nc.vector.tensor_tensor(bE[0:C], eG[0:C], beta[0:C], op=ALU.mult), [eG, beta], [bE])
            bank = k.bank()
            k.mm(bank[:, 0:nch * 16], [(sel[0:C, :], Gt[0:C].rearrange("p n h -> p (n h)"))], reads=[sel, Gt], bank=bank)
            A("act", lambda: nc.scalar.activation(out=eGl[:].rearrange("p n h -> p (n h)"), in_=bank[:, 0:nch * 16], func=AF.Exp), [bank], [eGl])
            A("dve", lambda: nc.vector.tensor_tensor(eGd[0:C].rearrange("p n h -> p (n h)"), bank[0:C, 0:nch * 16],
                                                     Gt[0:C].rearrange("p n h -> p (n h)"), op=ALU.subtract), [bank, Gt], [eGd])
            A("act", lambda: nc.scalar.activation(out=eGd[0:C], in_=eGd[0:C], func=AF.Exp), [eGd], [eGd])

            xp = [k.sb([128, 3 + L], F32, "xp") for _ in range(3)]
            qkv = [k.sb([128, L], F32, "qkvc") for _ in range(3)]
            sq = k.sb([128, min(L, 512)], F32, "sq")
            rs = k.sb([128, min(L, 512)], F32, "rs")
            S = k.sb([128, 128], F32, "S")
            zb_t = k.sb([128, 128], F32, "zb_t")
            U = [dict(kv=k.sb([128, 256], F32, "kv"), vb=k.sb([128, 128], F32, "vb"), kbg=k.sb([128, 128], F32, "kbg"),
                      kg=k.sb([128, 128], F32, "kg"), dg=k.sb([128, 128], F32, "dg"), z1=k.sb([128, 128], F32, "z1"),
                      z2=k.sb([128, 128], F32, "z2"), X=[k.sb([128, 128], F32, "X") for _ in range(2)],
                      XT=[k.sb([128, 128], F32, "XT") for _ in range(2)], R=[k.sb([128, 128], F32, "R") for _ in range(2)],
                      attnT=k.sb([128, 128], F32, "attnT"), u=k.sb([128, 128], F32, "u"), wT=k.sb([128, 128], F32, "wT"),
                      vnew=k.sb([128, 128], F32, "vnew"), o1=k.sb([128, 128], F32, "o1"), o=k.sb([128, 128], F32, "o"),
                      st=k.sb([128, 4], F32, "ost"), y=k.sb([128, 128], F32, "y"), yT=k.sb([128, 128], BF16, "yT"))
                 for _ in range(8 if prompt else 2)]
            ui = 0
            qrow = FM_ROWS["qkv"]
            for h in range(16):
                for i3 in range(3):
                    ch0 = i3 * 2048 + h * 128
                    if prompt:
                        A("pool", lambda: nc.gpsimd.memset(xp[i3][:, 0:3], 0.0), [], [xp[i3]])
                        k.dma("sp", xp[i3][:, 3:3 + L], P_fm.t[qrow + ch0:qrow + ch0 + 128, :], reads=[P_fm], writes=[xp[i3]], merge=True)
                    else:
                        sg = SG["qkv"] + ch0 // 128
                        k.dma("sp", xp[i3][:, 0:3], state_conv.t[l, bj * 3:(bj + 1) * 3, ch0:ch0 + 128].rearrange("j d -> d j"),
                              reads=[state_conv], writes=[xp[i3]], allow_slow_non_contiguous=True)
                        A("dve", lambda: nc.vector.tensor_copy(xp[i3][:, 3:3 + L], sFM[:, sg, bj * TS:(bj + 1) * TS]), [sFM], [xp[i3]], True)
                    g = ch0 // 128
                    o_ = qkv[i3]
                    A("dve", lambda: nc.vector.tensor_scalar(o_[:, :], xp[i3][:, 0:L], convw[:, g, 0:1], None, op0=ALU.mult), [xp[i3], convw], [o_])
                    for j in range(1, 4):
                        A("dve",
                          lambda: nc.vector.scalar_tensor_tensor(out=o_[:, :], in0=xp[i3][:, j:j + L], scalar=convw[:, g, j:j + 1],
                                                                                              in1=o_[:, :], op0=ALU.mult, op1=ALU.add),
                          [xp[i3], convw, o_], [o_])
                    A("act", lambda: nc.scalar.activation(out=o_[:, :], in_=o_[:, :], func=AF.Silu), [o_], [o_])
                for i3 in range(2):
                    o_ = qkv[i3]
                    for t0 in range(0, L, 512):
                        wd = min(512, L - t0)
                        A("act", lambda: nc.scalar.activation(out=sq[:, 0:wd], in_=o_[:, t0:t0 + wd], func=AF.Square), [o_], [sq])
                        bank = k.bank()
                        k.mm(bank[:, 0:wd], [(ones[:, :], sq[:, 0:wd])], reads=[ones, sq], bank=bank)
                        A("act", lambda: nc.scalar.activation(out=rs[:, 0:wd], in_=bank[:, 0:wd], func=AF.Sqrt, bias=EPS, scale=1.0), [bank], [rs])
                        A("dve", lambda: nc.vector.reciprocal(rs[:, 0:wd], rs[:, 0:wd]), [rs], [rs])
                        if i3 == 0:
                            A("dve", lambda: nc.vector.scalar_tensor_tensor(out=o_[:, t0:t0 + wd], in0=o_[:, t0:t0 + wd], scalar=128.0 ** -0.5,
                                                                             in1=rs[:, 0:wd], op0=ALU.mult, op1=ALU.mult), [o_, rs], [o_])
                        else:
                            A("dve", lambda: nc.vector.tensor_tensor(o_[:, t0:t0 + wd], o_[:, t0:t0 + wd], rs[:, 0:wd], op=ALU.mult), [o_, rs], [o_])
                qT, kT, vT = qkv
                if prompt:
                    A("pool", lambda: nc.gpsimd.memset(S[:], 0.0), [], [S])
                else:
                    k.dma("sp", S[:], state_dn.t[l, bj * 16 + h], reads=[state_dn], writes=[S])
                def par(n, u_):
                    cs = slice(n * C, (n + 1) * C)
                    col = lambda tl: tl[0:C, n, h:h + 1]
                    X, XT, R = u_["X"], u_["XT"], u_["R"]
                    bank = k.bank()
                    k.tr(bank[0:C, 0:128], kT[:, cs], ident[:, :], reads=[kT, ident], bank=bank, merge=False)
                    k.tr(bank[0:C, 128:256], vT[:, cs], ident[:, :], reads=[vT, ident], bank=bank, merge=True)
                    yield
                    k.copy("act", u_["kv"][0:C, :], bank[0:C, 0:256], reads=[bank], writes=[u_["kv"]])
                    A("pool", lambda: nc.gpsimd.tensor_scalar(u_["vb"][0:C, :], u_["kv"][0:C, 128:256], col(beta), None, op0=ALU.mult), [u_["kv"], beta], [u_["vb"]])
                    A("pool", lambda: nc.gpsimd.tensor_scalar(u_["kbg"][0:C, :], u_["kv"][0:C, 0:128], col(bE), None, op0=ALU.mult), [u_["kv"], bE], [u_["kbg"]])
                    A("pool", lambda: nc.gpsimd.tensor_scalar(u_["kg"][0:C, :], u_["kv"][0:C, 0:128], col(eGd), None, op0=ALU.mult), [u_["kv"], eGd], [u_["kg"]])
                    A("pool", lambda: nc.gpsimd.tensor_scalar(u_["dg"][0:C, 0:C], ident[0:C, 0:C], col(Gt), None, op0=ALU.mult), [ident, Gt], [u_["dg"]])
                    yield
                    bk = k.bank()
                    k.mm(bk[0:C, 0:C], [(kT[:, cs], kT[:, cs])], reads=[kT], bank=bk)
                    k.mm(bk[0:C, 128:128 + C], [(kT[:, cs], qT[:, cs])], reads=[kT, qT], bank=bk, merge=True)
                    k.mm(bk[0:C, 256:256 + C], [(ones[0:C, 0:C], u_["dg"][0:C, 0:C])], reads=[ones, u_["dg"]], bank=bk, merge=True)
                    yield
                    A("dve", lambda: nc.vector.scalar_tensor_tensor(out=u_["z1"][0:C, 0:C], in0=bk[0:C, 256:256 + C], scalar=col(Gt),
                                                                     in1=CT["pos_le"][0:C, 0:C], op0=ALU.subtract, op1=ALU.add), [bk, Gt, CT["pos_le"]], [u_["z1"]])
                    A("dve", lambda: nc.vector.scalar_tensor_tensor(out=u_["z2"][0:C, 0:C], in0=bk[0:C, 256:256 + C], scalar=col(Gt),
                                                                     in1=CT["neg_gt"][0:C, 0:C], op0=ALU.subtract, op1=ALU.add), [bk, Gt, CT["neg_gt"]], [u_["z2"]])
                    yield
                    A("act", lambda: nc.scalar.activation(out=u_["z1"][0:C, 0:C], in_=u_["z1"][0:C, 0:C], func=AF.Exp, scale=-1.0), [u_["z1"]], [u_["z1"]])
                    A("act", lambda: nc.scalar.activation(out=u_["z2"][0:C, 0:C], in_=u_["z2"][0:C, 0:C], func=AF.Exp), [u_["z2"]], [u_["z2"]])
                    X, XT, R = u_["X"], u_["XT"], u_["R"]
                    yield
                    A("dve", lambda: nc.vector.scalar_tensor_tensor(out=XT[0][0:C, 0:C], in0=bk[0:C, 0:C], scalar=col(nbeta), in1=u_["z1"][0:C, 0:C],
                                                                     op0=ALU.mult, op1=ALU.mult), [bk, nbeta, u_["z1"]], [XT[0]])
                    A("dve", lambda: nc.vector.tensor_tensor(u_["attnT"][0:C, 0:C], bk[0:C, 128:128 + C], u_["z2"][0:C, 0:C], op=ALU.mult), [bk, u_["z2"]], [u_["attnT"]])
                    yield
                    b2 = k.bank()
                    k.tr(b2[0:C, 0:C], XT[0][0:C, 0:C], ident[0:C, 0:C], reads=[XT[0], ident], bank=b2, merge=False)
                    yield
                    k.copy("act", X[0][0:C, 0:C], b2[0:C, 0:C], reads=[b2], writes=[X[0]])
                    A("dve", lambda: nc.vector.tensor_tensor(R[0][0:C, 0:C], b2[0:C, 0:C], ident[0:C, 0:C], op=ALU.add), [b2, ident], [R[0]])
                    cur = 0
                    for kk in range(1, nsq + 1):
                        nx = 1 - cur
                        yield
                        b3 = k.bank()
                        last = (kk == nsq)
                        k.mm(b3[0:C, 0:C], [(X[cur][0:C, 0:C], XT[cur][0:C, 0:C])], reads=[X[cur], XT[cur]], bank=b3)
                        if not last:
                            k.mm(b3[0:C, 128:128 + C], [(XT[cur][0:C, 0:C], X[cur][0:C, 0:C])], reads=[X[cur], XT[cur]], bank=b3, merge=True)
                        yield
                        k.copy("act", XT[nx][0:C, 0:C], b3[0:C, 0:C], reads=[b3], writes=[XT[nx]])
                        if not last:
                            k.copy("dve", X[nx][0:C, 0:C], b3[0:C, 128:128 + C], reads=[b3], writes=[X[nx]])
                        yield
                        b4 = k.bank()
                        k.mm(b4[0:C, 0:C], [(XT[nx][0:C, 0:C], R[cur][0:C, 0:C])], reads=[XT[nx], R[cur]], bank=b4)
                        yield
                        A("dve", lambda: nc.vector.tensor_tensor(R[nx][0:C, 0:C], b4[0:C, 0:C], R[cur][0:C, 0:C], op=ALU.add), [b4, R[cur]], [R[nx]])
                        cur = nx
                    TT = R[cur]
                    yield
                    b5 = k.bank()
                    k.mm(b5[0:C, 0:128], [(TT[0:C, 0:C], u_["vb"][0:C, :])], reads=[TT, u_["vb"]], bank=b5)
                    k.mm(b5[:, 128:128 + C], [(u_["kbg"][0:C, :], TT[0:C, 0:C])], reads=[TT, u_["kbg"]], bank=b5, merge=True)
                    yield
                    k.copy("act", u_["u"][0:C, :], b5[0:C, 0:128], reads=[b5], writes=[u_["u"]])
                    k.copy("dve", u_["wT"][:, 0:C], b5[:, 128:128 + C], reads=[b5], writes=[u_["wT"]])
                    u_["TT"] = TT
                def seq(n, u_):
                    cs = slice(n * C, (n + 1) * C)
                    col = lambda tl: tl[0:C, n, h:h + 1]
                    yield
                    b6 = k.bank()
                    k.mm(b6[0:C, 0:128], [(u_["wT"][:, 0:C], S[:, :])], reads=[u_["wT"], S], bank=b6)
                    k.mm(b6[0:C, 128:256], [(qT[:, cs], S[:, :])], reads=[qT, S], bank=b6, merge=True)
                    yield
                    A("dve", lambda: nc.vector.tensor_tensor(u_["vnew"][0:C, :], u_["u"][0:C, :], b6[0:C, 0:128], op=ALU.subtract), [u_["u"], b6], [u_["vnew"]])
                    A("act", lambda: nc.scalar.activation(out=u_["o1"][0:C, :], in_=b6[0:C, 128:256], func=AF.Identity, scale=col(eG)), [b6, eG], [u_["o1"]])
                    yield
                    b7 = k.bank()
                    k.mm(b7[0:C, 0:128], [(u_["attnT"][0:C, 0:C], u_["vnew"][0:C, :])], reads=[u_["attnT"], u_["vnew"]], bank=b7)
                    k.mm(b7[:, 128:256], [(u_["kg"][0:C, :], u_["vnew"][0:C, :])], reads=[u_["kg"], u_["vnew"]], bank=b7, merge=True)
                    yield
                    A("dve", lambda: nc.vector.tensor_tensor(u_["o"][0:C, :], u_["o1"][0:C, :], b7[0:C, 0:128], op=ALU.add), [u_["o1"], b7], [u_["o"]])
                    A("dve", lambda: nc.vector.scalar_tensor_tensor(out=S[:, :], in0=S[:, :], scalar=eGl[:, n, h:h + 1], in1=b7[:, 128:256],
                                                                     op0=ALU.mult, op1=ALU.add), [S, eGl, b7], [S])
                    if prompt:
                        k.dma("act", zb_t[0:C, :], P_zb.t[cs, h * 128:(h + 1) * 128], reads=[P_zb], writes=[zb_t])
                    else:
                        c0 = SEG["zb"][0] + h * 128
                        k.dma("act", zb_t[0:C, :], sTM.t[bj * TS:(bj + 1) * TS, c0:c0 + 128], reads=[sTM], writes=[zb_t])
                    yield
                    A("act", lambda: nc.scalar.activation(out=u_["y"][0:C, :], in_=u_["o"][0:C, :], func=AF.Square, accum_out=u_["st"][0:C, 0:1]), [u_["o"]], [u_["y"], u_["st"]])
                    rstd_from_ss(u_["st"][0:C, 0:1], u_["st"][0:C, 1:2], 128, u_["st"], C)
                    yield
                    A("dve", lambda: nc.vector.scalar_tensor_tensor(out=u_["y"][0:C, :], in0=u_["o"][0:C, :], scalar=u_["st"][0:C, 1:2], in1=onb[0:C, :],
                                                                     op0=ALU.mult, op1=ALU.mult), [u_["o"], u_["st"], onb], [u_["y"]])
                    A("act", lambda: nc.scalar.activation(out=zb_t[0:C, :], in_=zb_t[0:C, :], func=AF.Silu), [zb_t], [zb_t])
                    A("dve", lambda: nc.vector.tensor_tensor(u_["y"][0:C, :], u_["y"][0:C, :], zb_t[0:C, :], op=ALU.mult), [u_["y"], zb_t], [u_["y"]])
                    yield
                    b8 = k.bank()
                    k.tr(b8[:, 0:C], u_["y"][0:C, :], ident[0:C, 0:C], reads=[u_["y"], ident], bank=b8, merge=False)
                    if prompt:
                        k.copy("act", u_["yT"][:, 0:C], b8[:, 0:C], reads=[b8], writes=[u_["yT"]])
                        k.dma("sp", mixT.t[1024 + h * 128:1024 + (h + 1) * 128, cs], u_["yT"][:, 0:C], reads=[u_["yT"]], writes=[mixT], merge=True)
                    else:
                        k.copy("act", mixTs[:, 8 + h, bj * TS:(bj + 1) * TS], b8[:, 0:C], reads=[b8], writes=[mixTs], merge=True)
                NU = len(U)
                G = NU // 2

                def run(gens):
                    gens = list(gens)
                    while gens:
                        for g_ in list(gens):
                            try:
                                next(g_)
                            except StopIteration:
                                gens.remove(g_)

                def seq_chain(chunks):
                    for n_ in chunks:
                        yield from seq(n_, U[n_ % NU])
                groups = [list(range(g0, min(nch, g0 + G))) for g0 in range(0, nch, G)]
                run([par(n_, U[n_ % NU]) for n_ in groups[0]])
                for gi, grp in enumerate(groups):
                    gens = [seq_chain(grp)]
                    if gi + 1 < len(groups):
                        gens += [par(n_, U[n_ % NU]) for n_ in groups[gi + 1]]
                    run(gens)
                dst = O["p_dn"] if prompt else O["s_dn"]
                k.dma("sp", dst.t[l, bj * 16 + h], S[:, :], reads=[S], writes=[dst], merge=True)

    def mixerC_prompt(l, b):
        with k.scope():
            kiT2 = k.sb([128, 2, T], F32, "kiT2")
            kT = k.sb([128, 2, T], F32, "kT")
            V = k.sb([128, NT, 256], F32, "V")
            A("pool", lambda: nc.gpsimd.memset(kiT2[:], 0.0), [], [kiT2])
            r0 = FM_ROWS["ki"]
            k.dma("sp", kiT2[0:64, 0, :], P_fm.t[r0:r0 + 64, :], reads=[P_fm], writes=[kiT2])
            k.dma("sp", kiT2[64:128, 1, :], P_fm.t[r0 + 64:r0 + 128, :], reads=[P_fm], writes=[kiT2], merge=True)
            r0 = FM_ROWS["kc"]
            k.dma("sp", kT[:], P_fm.t[r0:r0 + 256, :].rearrange("(h d) t -> d h t", d=128), reads=[P_fm], writes=[kT])
            k.dma("sp", V[:], O["p_v"].t[l, b * T:(b + 1) * T, :].rearrange("(n s) f -> s n f", s=128), reads=[O["p_v"]], writes=[V])
            qiT = [k.sb([128, 8, 128], F32, "qiT") for _ in range(2)]
            qT = [k.sb([128, 8, 128], F32, "qT") for _ in range(2)]
            zcT = [k.sb([128, 8, 128], F32, "zcT") for _ in range(2)]
            wi = [k.sb([128, 16], F32, "wi") for _ in range(2)]
            rr = [k.sb([128, 2, 256], F32, "rr") for _ in range(2)]
            Iacc = k.sb([128, T], F32, "Iacc")
            work = k.sb([128, T], F32, "work")
            M = k.sb([128, T], F32, "M")
            MT = k.sb([128, NT, 128], F32, "MT")
            m8 = k.sb([128, 8], F32, "m8")
            thr = k.sb([128, 1], F32, "thr")
            ee = [k.sb([128, 4, 128], F32, "ee") for _ in range(2)]
            pp = [k.sb([128, 4, 128], F32, "pp") for _ in range(2)]
            rden = k.sb([128, 4, 128], F32, "rden")
            oo = k.sb([128, 4, 128], F32, "oo")
            szc = k.sb([128, 8, 128], F32, "szc")
            yc = k.sb([128, 4, 128], BF16, "yc")
            for qi in range(NT):
                ts = slice(qi * 128, (qi + 1) * 128)
                Sk = (qi + 1) * 128
                q_i, q_, z_, w_ = qiT[qi % 2], qT[qi % 2], zcT[qi % 2], wi[qi % 2]
                for buf, nm in ((q_i, "qi"), (q_, "qc"), (z_, "zc")):
                    r0 = FM_ROWS[nm]
                    k.dma("act", buf[:], P_fm.t[r0:r0 + 1024, ts].rearrange("(h d) t -> d h t", d=128), reads=[P_fm], writes=[buf])
                k.dma("act", w_[:], P_wi.t[ts, :], reads=[P_wi], writes=[w_])
                A("dve", lambda: nc.vector.tensor_scalar(w_[:], w_[:], 1.0 / 32.0, None, op0=ALU.mult), [w_], [w_])
                for kb0 in range(0, Sk, 256):
                    wd = min(256, Sk - kb0)
                    for c in range(8):
                        bank = k.bank()
                        k.mm(bank[:, 0:2 * wd].rearrange("p (a s) -> p a s", a=2), [(q_i[:, c, :], kiT2[:, :, kb0:kb0 + wd])], reads=[q_i, kiT2], bank=bank)
                        r_ = rr[c % 2]
                        A("act", lambda: nc.scalar.activation(out=r_[:, :, 0:wd], in_=bank[:, 0:2 * wd].rearrange("p (a s) -> p a s", a=2), func=AF.Relu), [bank], [r_])
                        for h2 in range(2):
                            hh = 2 * c + h2
                            e = "dve"
                            E = nc.vector
                            if hh == 0:
                                A(e, lambda: E.tensor_scalar(Iacc[:, kb0:kb0 + wd], r_[:, h2, 0:wd], w_[:, hh:hh + 1], None, op0=ALU.mult), [r_, w_], [Iacc], True)
                            else:
                                A(e, lambda: E.scalar_tensor_tensor(out=Iacc[:, kb0:kb0 + wd], in0=r_[:, h2, 0:wd], scalar=w_[:, hh:hh + 1],
                                                                    in1=Iacc[:, kb0:kb0 + wd], op0=ALU.mult, op1=ALU.add), [r_, w_, Iacc], [Iacc])
                A("dve", lambda: nc.vector.tensor_tensor(Iacc[:, qi * 128:Sk], Iacc[:, qi * 128:Sk], CT["neg_lt"][:, :], op=ALU.add), [Iacc, CT["neg_lt"]], [Iacc])
                if qi < 2:
                    A("dve", lambda: nc.vector.memset(thr[:], -1.0e29), [], [thr])
                else:
                    src = Iacc
                    for rd in range(32):
                        A("dve", lambda: nc.vector.max(out=m8[:], in_=src[:, 0:Sk]), [src], [m8])
                        if rd < 31:
                            A("dve", lambda: nc.vector.match_replace(out=work[:, 0:Sk], in_to_replace=m8[:], in_values=src[:, 0:Sk], imm_value=NEG), [src, m8], [work])
                            src = work
                    A("dve", lambda: nc.vector.tensor_reduce(out=thr[:], in_=m8[:], axis=AX.X, op=ALU.min), [m8], [thr])
                A("dve", lambda: nc.vector.tensor_scalar(M[:, 0:Sk], Iacc[:, 0:Sk], thr[:, 0:1], None, op0=ALU.is_ge), [Iacc, thr], [M])
                for s4 in range(0, qi + 1, 4):
                    n4 = min(4, qi + 1 - s4)
                    bank = k.bank()
                    for j in range(n4):
                        k.tr(bank[:, j * 128:(j + 1) * 128], M[:, (s4 + j) * 128:(s4 + j + 1) * 128], ident[:, :], reads=[M, ident], bank=bank, merge=(j > 0))
                    k.copy(k.ev(), MT[:, s4:s4 + n4, :].rearrange("p a b -> p (a b)"), bank[:, 0:n4 * 128], reads=[bank], writes=[MT], merge=True)
                A("act", lambda: nc.scalar.activation(out=szc[:], in_=z_[:], func=AF.Silu), [z_], [szc])
                for hkv in range(2):
                    k.reserved = []
                    num = k.bank()
                    den = k.bank()
                    k.reserved = [num, den]
                    for sb_ in range(qi + 1):
                        ks = slice(sb_ * 128, (sb_ + 1) * 128)
                        sc = k.bank()
                        k.mm(sc[:, :].rearrange("p (g t) -> p g t", g=4), [(kT[:, hkv, ks], q_[:, 4 * hkv:4 * hkv + 4, :])], reads=[kT, q_], bank=sc)
                        e_, p_ = ee[sb_ % 2], pp[sb_ % 2]
                        A("act", lambda: nc.scalar.activation(out=e_[:].rearrange("p g t -> p (g t)"), in_=sc[:, :], func=AF.Exp, scale=128.0 ** -0.5), [sc], [e_])
                        for g in range(4):
                            e = "dve" if g % 2 == 0 else "pool"
                            E = nc.vector if g % 2 == 0 else nc.gpsimd
                            A(e, lambda: E.tensor_tensor(p_[:, g, :], e_[:, g, :], MT[:, sb_, :], op=ALU.mult), [e_, MT], [p_], g > 0)
                        k.op("pe", lambda: nc.tensor.matmul(num[:, :], lhsT=V[:, sb_, hkv * 128:(hkv + 1) * 128], rhs=p_[:].rearrange("p g t -> p (g t)"),
                                                            start=(sb_ == 0), stop=(sb_ == qi)), reads=[V, p_], writes=[num], merge=(sb_ > 0))
                        k.op("pe", lambda: nc.tensor.matmul(den[:, :], lhsT=ones[:, :], rhs=p_[:].rearrange("p g t -> p (g t)"),
                                                            start=(sb_ == 0), stop=(sb_ == qi)), reads=[ones, p_], writes=[den], merge=(sb_ > 0))
                    A("dve", lambda: nc.vector.reciprocal(rden[:].rearrange("p g t -> p (g t)"), den[:, :]), [den], [rden])
                    A("dve", lambda: nc.vector.tensor_tensor(oo[:].rearrange("p g t -> p (g t)"), num[:, :], rden[:].rearrange("p g t -> p (g t)"), op=ALU.mult), [num, rden], [oo])
                    A("dve", lambda: nc.vector.tensor_tensor(yc[:], oo[:], szc[:, 4 * hkv:4 * hkv + 4, :], op=ALU.mult), [oo, szc], [yc])
                    k.reserved = []
                    r0 = 3072 + hkv * 512
                    k.dma("sp", mixT.t[r0:r0 + 512, ts].rearrange("(g d) t -> d g t", d=128), yc[:], reads=[yc], writes=[mixT], merge=True)


    def mixerC_sample(l, j):
        NK = 16384
        j4 = j * TS
        ts = slice(j4, j4 + TS)
        with k.scope():
            pt = k.sb([128, 1], I32, "pt")
            k.dma("sp", pt[:], page_table.t[j * 128:(j + 1) * 128, :], reads=[page_table], writes=[pt])
            pt8 = k.sb([128, 1], I32, "pt8")
            A("dve", lambda: nc.vector.tensor_single_scalar(out=pt8[:], in_=pt[:], scalar=3, op=ALU.logical_shift_left), [pt], [pt8])
            MT = k.sb([128, 128, TS], F32, "MTs")
            MTn = k.sb([TS, TS], F32, "MTn")
            NK = 16384
            NKT = NK + TS
            with k.scope():
                I_s = k.sb([TS, NK + 16], F32, "I_s")
                qiT = k.sb([64, TS, 8, 2], F32, "qiT")
                wcol = k.sb([64, 1], F32, "wcol")
                Wsel = k.sb([64, TS], F32, "Wsel")
                gq = SG["qi"]
                A("dve", lambda: nc.vector.tensor_copy(qiT[:, :, :, 0], sFM[0:64, gq:gq + 8, ts].rearrange("p c t -> p t c")), [sFM], [qiT], True)
                bank = k.bank()
                k.mm(bank[0:64, 0:8 * TS], [(ident[:, 64:128], sFM[:, gq:gq + 8, ts])], reads=[ident, sFM], bank=bank)
                A("dve", lambda: nc.vector.tensor_copy(qiT[:, :, :, 1], bank[0:64, 0:8 * TS].rearrange("p (c t) -> p t c", t=TS)), [bank], [qiT], True)
                w0 = SEG["wi"][0]
                for t_ in range(TS):
                    k.dma("sp", wcol[t_ * 16:(t_ + 1) * 16, :], sTM.t[j4 + t_:j4 + t_ + 1, w0:w0 + 16].rearrange("o h -> h o"), reads=[sTM], writes=[wcol], merge=True,
                          allow_slow_non_contiguous=True)
                A("dve", lambda: nc.vector.tensor_scalar(Wsel[:, :], CT["bm16"][0:64, 0:TS], wcol[:, 0:1], 1.0 / 32.0, op0=ALU.mult, op1=ALU.mult), [CT["bm16"], wcol], [Wsel])
                qiT2 = qiT[:].rearrange("p t c h -> p (t c h)")

                def score_block(rhs_ap, rhs_buf, wd, col0, rl):
                    b1 = k.bank()
                    k.mm(b1[0:64, 0:wd], [(qiT2, rhs_ap)], reads=[qiT, rhs_buf], bank=b1)
                    A("act", lambda: nc.scalar.activation(out=rl[:, 0:wd], in_=b1[0:64, 0:wd], func=AF.Relu), [b1], [rl])
                    b2 = k.bank()
                    k.mm(b2[0:TS, 0:wd], [(Wsel[:, :], rl[:, 0:wd])], reads=[Wsel, rl], bank=b2)
                    k.copy("dve", I_s[:, col0:col0 + wd], b2[0:TS, 0:wd], reads=[b2], writes=[I_s], merge=True)
                with k.scope():
                    kx = k.sb([128, 128 * 64], F32, "kx")
                    k.dma("pool", kx[:, :], cache_kidx.t[:, :], reads=[cache_kidx, pt], writes=[kx], indirect=pt[:, 0:1], elem_off=l * NPOOL * 8192)
                    kxT = [k.sb([64, 512], F32, "kxT") for _ in range(2)]
                    rl = [k.sb([64, 512], F32, "rl") for _ in range(2)]
                    for blk in range(NK // 512):
                        bank = k.bank()
                        for q in range(4):
                            kk = blk * 4 + q
                            k.tr(bank[0:64, q * 128:(q + 1) * 128], kx[:, kk * 64:(kk + 1) * 64], ident[:, :], reads=[kx, ident], bank=bank, merge=(q > 0))
                        kt_ = kxT[blk % 2]
                        k.copy(k.ev(), kt_[:, :], bank[0:64, :], reads=[bank], writes=[kt_])
                        score_block(kt_[:, :], kt_, 512, blk * 512, rl[blk % 2])
                    score_block(sFM[0:64, SG["ki"], ts], sFM, TS, NK, rl[0])
                    A("dve", lambda: nc.vector.tensor_tensor(I_s[:, NK:NK + TS], I_s[:, NK:NK + TS], CT["neg_lt"][0:TS, 0:TS], op=ALU.add), [I_s, CT["neg_lt"]], [I_s])
                NKT = NK + TS
                M_s = k.sb([TS, NK + 16], F32, "M_s")
                m8 = k.sb([TS, 8], F32, "m8s")
                thr = k.sb([TS, 1], F32, "thrs")
                cand = k.sb([TS, 16], F32, "cand")
                A("dve", lambda: nc.vector.memset(cand[:, :], NEG), [], [cand])
                A("dve", lambda: nc.vector.tensor_copy(cand[:, 8:8 + TS], I_s[:, NK:NK + TS]), [I_s], [cand])
                src = I_s
                for rd in range(32):
                    A("dve", lambda: nc.vector.max(out=cand[:, 0:8], in_=src[:, 0:NK]), [src, cand], [cand])
                    A("dve", lambda: nc.vector.max(out=m8[:], in_=cand[:, 0:16]), [cand], [m8])
                    if rd < 31:
                        A("dve", lambda: nc.vector.match_replace(out=M_s[:, 0:NK], in_to_replace=m8[:], in_values=src[:, 0:NK], imm_value=NEG), [src, m8], [M_s])
                        A("dve", lambda: nc.vector.match_replace(out=cand[:, 8:16], in_to_replace=m8[:], in_values=cand[:, 8:16], imm_value=NEG), [cand, m8], [cand])
                        src = M_s
                A("dve", lambda: nc.vector.tensor_reduce(out=thr[:], in_=m8[:], axis=AX.X, op=ALU.min), [m8], [thr])
                A("dve", lambda: nc.vector.tensor_scalar(M_s[:, 0:NKT], I_s[:, 0:NKT], thr[:, 0:1], None, op0=ALU.is_ge), [I_s, thr], [M_s])
                bank = k.bank()
                for kk in range(128):
                    k.tr(bank[:, kk * TS:(kk + 1) * TS], M_s[:, kk * 128:(kk + 1) * 128], ident[0:TS, 0:TS], reads=[M_s, ident], bank=bank, merge=(kk > 0))
                k.copy("dve", MT[:].rearrange("p a b -> p (a b)"), bank[:, 0:128 * TS], reads=[bank], writes=[MT])
                bank = k.bank()
                k.tr(bank[0:TS, 0:TS], M_s[:, NK:NK + TS], ident[0:TS, 0:TS], reads=[M_s, ident], bank=bank, merge=False)
                k.copy("dve", MTn[:, :], bank[0:TS, 0:TS], reads=[bank], writes=[MTn])
            qTs = k.sb([128, 2, TS, 4], F32, "qTs")
            gc = SG["qc"]
            for hkv in range(2):
                A("dve", lambda: nc.vector.tensor_copy(qTs[:, hkv, :, :], sFM[:, gc + hkv * 4:gc + hkv * 4 + 4, ts].rearrange("p g t -> p t g")), [sFM], [qTs], True)
            Ks = [k.sb([128, 16 * 256], F32, "Ks") for _ in range(1)]
            Vs = [k.sb([128, 16 * 256], F32, "Vs") for _ in range(1)]
            KT = k.sb([128, 32, 128], F32, "KTs")
            E = k.sb([128, 16, 2, TS, 4], F32, "Es")
            PT = k.sb([128, 16, 2, TS, 4], F32, "PTs")
            vnew = k.sb([TS, 256], F32, "vnews")
            v0 = SEG["vc"][0]
            k.dma("sp", vnew[:, :], sTM.t[ts, v0:v0 + 256], reads=[sTM], writes=[vnew])
            k.reserved = []
            acc = [k.bank() for _ in range(4)]
            k.reserved = list(acc)
            first = [True, True, True, True]

            def accum(bi, out_ap, lhsT, rhs, rd_bufs, last):
                k.op("pe", lambda: nc.tensor.matmul(out_ap, lhsT=lhsT, rhs=rhs, start=first[bi], stop=last), reads=rd_bufs, writes=[acc[bi]], merge=(not first[bi]))
                first[bi] = False
            for grp in range(8):
                K_, V_ = Ks[0], Vs[0]
                c0 = grp * 16 * 256
                k.dma("pool", K_[:, :], cache_k.t[:, :], reads=[cache_k, pt8], writes=[K_], indirect=pt8[:, 0:1], elem_off=(l * NPOOL * 8 + grp) * 4096)
                k.dma("pool", V_[:, :], cache_v.t[:, :], reads=[cache_v, pt8], writes=[V_], indirect=pt8[:, 0:1], elem_off=(l * NPOOL * 8 + grp) * 4096)
                for b8 in range(8):
                    bank = k.bank()
                    for q in range(4):
                        idx = b8 * 4 + q
                        k.tr(bank[:, q * 128:(q + 1) * 128], K_[:, idx * 128:(idx + 1) * 128], ident[:, :], reads=[K_, ident], bank=bank, merge=(q > 0))
                    k.copy(k.ev(), KT[:, b8 * 4:(b8 + 1) * 4, :].rearrange("p a b -> p (a b)"), bank[:, :], reads=[bank], writes=[KT], merge=True)
                sbank = k.bank()
                for idx in range(32):
                    hkv = idx % 2
                    k.mm(sbank[:, idx * 16:(idx + 1) * 16], [(KT[:, idx, :], qTs[:, hkv, :, :].rearrange("p t g -> p (t g)"))], reads=[KT, qTs], bank=sbank, merge=(idx > 0))
                A("act", lambda: nc.scalar.activation(out=E[:].rearrange("p a b c d -> p (a b c d)"), in_=sbank[:, :], func=AF.Exp, scale=128.0 ** -0.5), [sbank], [E])
                for hkv in range(2):
                    for g in range(4):
                        e = "dve" if g % 2 == 0 else "pool"
                        En = nc.vector if g % 2 == 0 else nc.gpsimd
                        A(e, lambda: En.tensor_tensor(PT[:, :, hkv, :, g], E[:, :, hkv, :, g], MT[:, grp * 16:(grp + 1) * 16, :], op=ALU.mult), [E, MT], [PT], True)
                for kk in range(16):
                    for hkv in range(2):
                        lhsT = PT[:, kk, hkv, :, :].rearrange("p t g -> p (t g)")
                        accum(hkv, acc[hkv][0:16, 0:128], lhsT, V_[:, (kk * 2 + hkv) * 128:(kk * 2 + hkv + 1) * 128], [PT, V_], False)
                        accum(2 + hkv, acc[2 + hkv][0:16, 0:1], lhsT, ones[:, 0:1], [PT, ones], False)
            En_ = k.sb([TS, 2, TS, 4], F32, "Enew")
            Pn_ = k.sb([TS, 2, TS, 4], F32, "Pnew")
            sbank = k.bank()
            gk = SG["kc"]
            for hkv in range(2):
                k.mm(sbank[0:TS, hkv * 16:(hkv + 1) * 16], [(sFM[:, gk + hkv, ts], qTs[:, hkv, :, :].rearrange("p t g -> p (t g)"))], reads=[sFM, qTs], bank=sbank, merge=(hkv > 0))
            A("act", lambda: nc.scalar.activation(out=En_[:].rearrange("p b c d -> p (b c d)"), in_=sbank[0:TS, 0:32], func=AF.Exp, scale=128.0 ** -0.5), [sbank], [En_])
            for hkv in range(2):
                for g in range(4):
                    A("dve", lambda: nc.vector.tensor_tensor(Pn_[:, hkv, :, g], En_[:, hkv, :, g], MTn[:, :], op=ALU.mult), [En_, MTn], [Pn_], True)
            for hkv in range(2):
                lhsT = Pn_[:, hkv, :, :].rearrange("p t g -> p (t g)")
                accum(hkv, acc[hkv][0:16, 0:128], lhsT, vnew[:, hkv * 128:(hkv + 1) * 128], [Pn_, vnew], True)
                accum(2 + hkv, acc[2 + hkv][0:16, 0:1], lhsT, ones[0:TS, 0:1], [Pn_, ones], True)
            rd_ = k.sb([16, 2], F32, "rdens")
            oo = k.sb([16, 128], F32, "oos")
            zc = k.sb([16, 128], F32, "zcs")
            z0 = SEG["zc"][0]
            for hkv in range(2):
                A("dve", lambda: nc.vector.reciprocal(rd_[:, hkv:hkv + 1], acc[2 + hkv][0:16, 0:1]), [acc[2 + hkv]], [rd_], True)
                k.dma("sp", zc[:, :], sTM.t[ts, z0 + hkv * 512:z0 + (hkv + 1) * 512].rearrange("t (g d) -> t g d", d=128), reads=[sTM], writes=[zc])
                A("act", lambda: nc.scalar.activation(out=zc[:, :], in_=zc[:, :], func=AF.Silu), [zc], [zc])
                A("dve", lambda: nc.vector.scalar_tensor_tensor(out=oo[:, :], in0=acc[hkv][0:16, 0:128], scalar=rd_[:, hkv:hkv + 1], in1=zc[:, :],
                                                                 op0=ALU.mult, op1=ALU.mult), [acc[hkv], rd_, zc], [oo])
                bank = k.bank()
                k.tr(bank[:, 0:16], oo[:, :], ident[0:16, 0:16], reads=[oo, ident], bank=bank, merge=False)
                A("dve", lambda: nc.vector.tensor_copy(mixTs[:, 24 + hkv * 4:24 + hkv * 4 + 4, ts], bank[:, 0:16].rearrange("p (t g) -> p g t", g=4)), [bank], [mixTs], True)
            k.reserved = []

    def outproj(l, prompt, b, xin, xout):
        ntok = T if prompt else NS
        with k.scope():
            HT = T
            if prompt:
                mT = k.sb([128, 32, HT], BF16, "mT")
            else:
                mT = mixTs
            gate = k.sb([128, D], F32, "gate")
            gb = k.sb([128, 1024], F32, "gb")
            P = 128 if prompt else NS
            if prompt:
                k.dma("sp", gate[:], m_scr.t[b:b + 1, 2 * D:3 * D].to_broadcast([128, D]), reads=[m_scr], writes=[gate])
            else:
                for j in range(sbn):
                    k.dma("sp", gate[j * TS:(j + 1) * TS, :], m_scr.t[pb + j:pb + j + 1, 2 * D:3 * D].to_broadcast([TS, D]), reads=[m_scr], writes=[gate], merge=True)
            for q4 in range(4):
                k.dma("sp", gb[0:P, :], b_ada.t[l:l + 1, 2 * D + q4 * 1024:2 * D + (q4 + 1) * 1024].to_broadcast([P, 1024]), reads=[b_ada], writes=[gb])
                A("dve", lambda: nc.vector.tensor_tensor(gate[0:P, q4 * 1024:(q4 + 1) * 1024], gate[0:P, q4 * 1024:(q4 + 1) * 1024], gb[0:P, :], op=ALU.add), [gate, gb], [gate])
            xt = [k.sb([128, 256], F32, "xo") for _ in range(3)]
            xi = 0
            wv = w_out.t[l].rearrange("(c p) n -> p c n", p=128)
            row_base = b * T if prompt else 0
            for half, blk in [(hf, bl) for hf in range(1) for bl in range(D // 256)]:
                if prompt and blk == 0:
                    k.dma("sp", mT[:], mixT.t[:, half * HT:(half + 1) * HT].rearrange("(c p) t -> p c t", p=128), reads=[mixT], writes=[mT])
                wb = next_w()
                cs = slice(blk * 256, (blk + 1) * 256)
                k.dma("pool", wb[:], wv[:, :, cs], reads=[w_out], writes=[wb])
                ntl = (HT // P) if prompt else 1
                for ti in range(ntl):
                    r_off = half * HT + ti * P
                    rs = slice(row_base + r_off, row_base + r_off + P)
                    x_ = xt[xi % 3]
                    xi += 1
                    k.dma("act", x_[0:P, :], xin.t[rs, cs], reads=[xin], writes=[x_])
                    bank = k.bank()
                    k.mm(bank[0:P, 0:256], [(mT[:, c, ti * P:(ti + 1) * P], wb[:, c, :]) for c in range(32)], reads=[mT, wb], bank=bank)
                    st = next_stage()
                    A("dve", lambda: nc.vector.tensor_tensor(st[0:P, 0:256], bank[0:P, 0:256], gate[0:P, cs], op=ALU.mult), [bank, gate], [st])
                    A("pool", lambda: nc.gpsimd.tensor_tensor(st[0:P, 0:256], st[0:P, 0:256], x_[0:P, :], op=ALU.add), [st, x_], [st])
                    k.dma("sp", xout.t[rs, cs], st[0:P, 0:256], reads=[st], writes=[xout], merge=True)

    def final_norm(xin, yout, ntok):
        with k.scope():
            gf = k.sb([128, D], F32, "gf")
            k.dma("sp", gf[:], g_final.t[0:1, :].to_broadcast([128, D]), reads=[g_final], writes=[gf])
            xt = [k.sb([128, D], F32, "xf") for _ in range(2)]
            junk = k.sb([128, 2048], BF16, "junkf")
            ssq = k.sb([128, 4], F32, "ssqf")
            P = min(128, ntok)
            for ti in range(ntok // P):
                x_ = xt[ti % 2]
                rs = slice(ti * P, (ti + 1) * P)
                k.dma("act", x_[0:P, :], xin.t[rs, :], reads=[xin], writes=[x_])
                for hh in range(2):
                    A("act", lambda: nc.scalar.activation(out=junk[0:P, :], in_=x_[0:P, hh * 2048:(hh + 1) * 2048], func=AF.Square,
                                                          accum_out=ssq[0:P, hh:hh + 1]), [x_], [junk, ssq])
                A("dve", lambda: nc.vector.tensor_tensor(ssq[0:P, 2:3], ssq[0:P, 0:1], ssq[0:P, 1:2], op=ALU.add), [ssq], [ssq])
                rstd_from_ss(ssq[0:P, 2:3], ssq[0:P, 3:4], D, ssq, P)
                A("dve", lambda: nc.vector.scalar_tensor_tensor(out=x_[0:P, :], in0=x_[0:P, :], scalar=ssq[0:P, 3:4], in1=gf[0:P, :],
                                                                 op0=ALU.mult, op1=ALU.mult), [x_, ssq, gf], [x_])
                k.dma("sp", yout.t[rs, :], x_[0:P, :], reads=[x_], writes=[yout], merge=True)

    for l in range(nl):
        xin = [x_p, x_s] if l == 0 else xa
        xout = xa if l == 0 else xb
        adaln(l)
        if stage == "ada":
            break
        with k.scope():
            Acol = k.sb([128, NR, 32], F32, "Acol")
            Bcol = k.sb([128, NR, 32], F32, "Bcol")
            mod_cols(l, Acol, Bcol)
            hTs = k.sb([128, 32, NS], BF16, "hTs")
            h_phase(xin[1], 0, 1, NS, hTs, Acol, Bcol, lambda ti, P: [(j * TS, (j + 1) * TS, pb + j) for j in range(sbn)])
            for b in range(pb):
                with k.scope():
                    hT = k.sb([128, 32, T], BF16, "hT")
                    h_phase(xin[0], b * T, NT, 128, hT, Acol, Bcol, lambda ti, P, b=b: [(0, P, b)])
                    if stage != "h":
                        inproj(l, hT, T, True, b, hTs if b == 0 else None)
                if stage in ("inproj", "h"):
                    continue
                if stage in ("all", "A"):
                    mixerA_prompt(l)
                if stage in ("all", "B"):
                    mixerB(l, True, b)
                if stage in ("all", "C"):
                    mixerC_prompt(l, b)
                if stage == "all":
                    outproj(l, True, b, xin[0], xout[0])
            if stage in ("inproj", "h"):
                continue
            if stage in ("all", "A"):
                mixerA_sample(l)
            if stage in ("all", "B"):
                for j in range(sbn):
                    mixerB(l, False, j)
            if stage == "Cs":
                for j in range(sbn):
                    mixerC_sample(l, j)
            if stage == "all":
                if with_cache:
                    for j in range(sbn):
                        mixerC_sample(l, j)
                else:
                    A("dve", lambda: nc.vector.memset(mixTs[:, 24:32, :], 0.0), [], [mixTs], True)
                outproj(l, False, 0, xin[1], xout[1])
    if stage == "all":
        last = xa if nl == 1 else xb
        final_norm(last[0], O["y_p"], pb * T)
        final_norm(last[1], O["y_s"], NS)
    k.finish()
    return k


_BUILT = {}
NCORES = 4


def kernel(_build_args=None, **inputs):
    inp = {kk: np.asarray(v) for kk, v in inputs.items()}
    ba = dict(nl=DEPTH, pb=4 // NCORES, sbn=8 // NCORES, stage="all", with_cache=True)
    if _build_args:
        ba.update(_build_args)
    ncores = ba.pop("ncores", NCORES)
    key = tuple(sorted(ba.items()))
    if key not in _BUILT:
        _BUILT[key] = build(**ba)
    k = _BUILT[key]
    nl, pb, sbn = ba["nl"], ba["pb"], ba["sbn"]
    f = np.float32
    in_maps = []
    for c in range(ncores):
        ps = slice(c * pb, (c + 1) * pb)
        ss = slice(c * sbn, (c + 1) * sbn)
        m = {
            "x_p": np.ascontiguousarray(inp["x_prompt"][ps], f).reshape(pb * T, D),
            "x_s": np.ascontiguousarray(inp["x_sample"][ss], f).reshape(sbn * TS, D),
            "c_ps": np.ascontiguousarray(np.concatenate([inp["c_prompt"][ps], inp["c_sample"][ss]], 0), f),
            "state_dn": np.ascontiguousarray(inp["state_dn"][:nl, ss], f).reshape(nl, sbn * 16, 128, 128),
            "state_conv": np.ascontiguousarray(inp["state_conv"][:nl, ss], f).reshape(nl, sbn * 3, 6144),
            "w_ada": np.ascontiguousarray(inp["w_ada"][:nl], f),
            "b_ada": np.ascontiguousarray(inp["b_ada"][:nl], f),
            "g_norm": np.ascontiguousarray(inp["g_norm"][:nl], f),
            "w_in": np.ascontiguousarray(inp["w_in"][:nl], f),
            "a_vnorm": np.ascontiguousarray(inp["a_vnorm"][:nl], f),
            "a_ws": np.ascontiguousarray(inp["a_ws"][:nl], f),
            "a_bs": np.ascontiguousarray(inp["a_bs"][:nl], f).reshape(nl, 1024),
            "dn_conv_w": np.ascontiguousarray(inp["dn_conv_w"][:nl], f),
            "dn_a_log": np.ascontiguousarray(inp["dn_a_log"][:nl], f),
            "dn_dt_bias": np.ascontiguousarray(inp["dn_dt_bias"][:nl], f),
            "dn_onorm": np.ascontiguousarray(inp["dn_onorm"][:nl], f),
            "w_out": np.ascontiguousarray(inp["w_out"][:nl], f),
            "g_final": np.ascontiguousarray(inp["g_final"].reshape(1, D), f),
            "consts": CONST_ARR,
        }
        if ba["with_cache"]:
            m["cache_k"] = np.ascontiguousarray(inp["cache_k"][:nl], f).reshape(nl * NPOOL * 8, 16 * 256)
            m["cache_v"] = np.ascontiguousarray(inp["cache_v"][:nl], f).reshape(nl * NPOOL * 8, 16 * 256)
            m["cache_kidx"] = np.ascontiguousarray(inp["cache_kidx"][:nl], f).reshape(nl * NPOOL, 128 * 64)
            m["page_table"] = np.ascontiguousarray(inp["page_table"][ss].reshape(sbn * 128, 1), np.int32)
        in_maps.append(m)
    res = run_bass_kernel_spmd(k.nc, in_maps, core_ids=list(range(ncores)))
    R = res.results

    def cat(name, tail, per):
        parts = [R[c][name].reshape((nl, per) + tail) for c in range(ncores)]
        return np.concatenate(parts, 1)

    y_p = np.concatenate([R[c]["y_p"].reshape(pb, T, D) for c in range(ncores)], 0)
    y_s = np.concatenate([R[c]["y_s"].reshape(sbn, TS, D) for c in range(ncores)], 0)
    outs = (y_p, y_s,
            cat("p_k", (T, 2, 128), pb), cat("p_v", (T, 2, 128), pb), cat("p_kidx", (T, 64), pb),
            cat("p_dn", (16, 128, 128), pb), cat("p_conv", (3, 6144), pb),
            cat("s_k", (TS, 2, 128), sbn), cat("s_v", (TS, 2, 128), sbn), cat("s_kidx", (TS, 64), sbn),
            cat("s_dn", (16, 128, 128), sbn), cat("s_conv", (3, 6144), sbn), cat("s_amlp_v", (TS, 1024), sbn))
    return tuple(np.ascontiguousarray(o, dtype=np.float32) for o in outs)
```

```python
import numpy as np
from contextlib import ExitStack, contextmanager
import concourse.bass as bass
import concourse.mybir as mybir
from concourse.bass_utils import run_bass_kernel_spmd

F32 = mybir.dt.float32
BF16 = mybir.dt.bfloat16
I32 = mybir.dt.int32
AF = mybir.ActivationFunctionType
ALU = mybir.AluOpType
AX = mybir.AxisListType

SAME_ENG_SYNC = True
N_DSEM = 20

D = 4096
T = 2048
NT = 16
TS = 4
DEPTH = 2
D_IN = 14960
NPOOL = 1280
EPS = 1e-6
NEG = -1.0e30

SEGS = [("u", 0, 1024), ("v", 1024, 1024), ("za", 2048, 1024), ("qkv", 3072, 6144), ("zb", 9216, 2048),
        ("ab", 11264, 32), ("qc", 11296, 1024), ("kc", 12320, 256), ("vc", 12576, 256), ("zc", 12832, 1024),
        ("qi", 13856, 1024), ("ki", 14880, 64), ("wi", 14944, 16)]
SEG = {n: (s, w) for n, s, w in SEGS}
FMSEG = ["u", "za", "qkv", "qc", "kc", "zc", "qi", "ki"]
TMSEG = ["v", "zb", "ab", "kc", "vc", "ki", "wi"]
FM_ROWS = {}
_r = 0
for _n in FMSEG:
    FM_ROWS[_n] = _r
    _r += max(SEG[_n][1], 128)
FM_TOT = _r
SG = {}
_g = 0
for _n in FMSEG:
    SG[_n] = _g
    _g += max(SEG[_n][1], 128) // 128
NSG = _g


class Buf:
    __slots__ = ("t", "wr", "rd", "name", "excl")

    def __init__(self, t, name="", excl=False):
        self.t = t
        self.wr = {}
        self.rd = {}
        self.name = name
        self.excl = excl

    def __getitem__(self, idx):
        return self.t[idx]


class K:
    def __init__(self):
        self.nc = bass.Bass("TRN2", target_bir_lowering=False)
        self.stack = [ExitStack()]
        nc = self.nc
        self.eng = {"pe": nc.tensor, "act": nc.scalar, "dve": nc.vector, "pool": nc.gpsimd, "sp": nc.sync}
        self.sem = {}
        self.cnt = {}
        es = self.stack[0]
        for e in ("pe", "act", "dve", "pool"):
            self.sem[e] = es.enter_context(nc.semaphore("s_" + e))
            self.cnt[e] = 0
        self.dq = {}
        for q in ("sp", "pool", "act"):
            sems = []
            for i in range(N_DSEM):
                key = "d_%s_%d" % (q, i)
                self.sem[key] = es.enter_context(nc.semaphore(key))
                self.cnt[key] = 0
                sems.append(key)
            self.dq[q] = [sems, 0]
        self.waited = {e: {} for e in ("pe", "act", "dve", "pool", "sp")}
        self.nbuf = 0
        self.banks = []
        self.bank_i = 0
        self.reserved = []
        self.ev_i = 0

    @contextmanager
    def scope(self):
        es = ExitStack()
        self.stack.append(es)
        try:
            yield
        finally:
            self.barrier()
            self.stack.pop()
            es.close()

    def sb(self, shape, dt=F32, name=None):
        self.nbuf += 1
        name = (name or "sb") + "_%d" % self.nbuf
        t = self.stack[-1].enter_context(self.nc.sbuf_tensor(name, list(shape), dt))
        return Buf(t, name)

    def ps(self, shape, dt=F32, name=None):
        self.nbuf += 1
        name = (name or "ps") + "_%d" % self.nbuf
        t = self.stack[-1].enter_context(self.nc.psum_tensor(name, list(shape), dt))
        return Buf(t, name, excl=True)

    def dram(self, name, shape, dt=F32, kind="Internal"):
        t = self.nc.dram_tensor(name, list(shape), dt, kind=kind)
        return Buf(t.ap(), name)

    def bank(self):
        while True:
            b = self.banks[self.bank_i % len(self.banks)]
            self.bank_i += 1
            if b not in self.reserved:
                return b

    def ev(self):
        self.ev_i += 1
        return ("act", "dve")[self.ev_i % 2]

    def _deps(self, reads, writes, merge, e=None):
        deps = {}

        def add(d):
            for kk, c in d.items():
                if deps.get(kk, 0) < c:
                    deps[kk] = c
        for b in reads:
            add(b.wr)
            if b.excl:
                add({kk: c for kk, c in b.rd.items() if kk != e})
        for b in writes:
            add(b.rd)
            if not merge:
                add(b.wr)
        return deps

    def _emit_waits(self, e, deps):
        w = self.waited[e]
        for kk, c in deps.items():
            if kk == e and (e == "pe" or not SAME_ENG_SYNC):
                continue
            if w.get(kk, 0) >= c:
                continue
            self.eng[e].wait_ge(self.sem[kk], c)
            w[kk] = c

    def _mark(self, key, c, reads, writes, merge):
        for b in reads:
            if b.rd.get(key, 0) < c:
                b.rd[key] = c
        for b in writes:
            if merge:
                b.wr[key] = c
            else:
                b.wr = {key: c}
            b.rd = {}

    def op(self, e, fn, reads=(), writes=(), merge=False):
        deps = self._deps(reads, writes, merge, e)
        self._emit_waits(e, deps)
        ins = fn()
        self.cnt[e] += 1
        ins.then_inc(self.sem[e], 1)
        self._mark(e, self.cnt[e], reads, writes, merge)
        return ins

    def dma(self, q, out, in_, reads=(), writes=(), merge=False, indirect=None, elem_off=0, **kw):
        sems, i = self.dq[q]
        key = sems[i % len(sems)]
        self.dq[q][1] = i + 1
        deps = self._deps(reads, writes, merge)
        if self.cnt[key] > 0:
            deps[key] = max(deps.get(key, 0), self.cnt[key])
        self._emit_waits(q, deps)
        if indirect is not None:
            ins = self.nc.gpsimd.indirect_dma_start(out=out, out_offset=None, in_=in_,
                                                    in_offset=bass.IndirectOffsetOnAxis(ap=indirect, axis=0), element_offset=elem_off)
        else:
            ins = self.eng[q].dma_start(out=out, in_=in_, **kw)
        self.cnt[key] += 16
        ins.then_inc(self.sem[key], 16)
        self._mark(key, self.cnt[key], reads, writes, merge)

    def finish(self):
        deps = {kk: c for kk, c in self.cnt.items() if c > 0}
        self._emit_waits("sp", deps)

    def barrier(self):
        snap = {kk: c for kk, c in self.cnt.items() if c > 0}
        for e in ("pe", "act", "dve", "pool", "sp"):
            self._emit_waits(e, {kk: c for kk, c in snap.items() if kk != e})

    def mm(self, out_ap, pairs, reads, bank, merge=False):
        nc = self.nc
        n = len(pairs)

        def fn():
            ins = None
            for i, (l, r) in enumerate(pairs):
                ins = nc.tensor.matmul(out_ap, lhsT=l, rhs=r, start=(i == 0), stop=(i == n - 1))
            return ins
        self.op("pe", fn, reads=reads, writes=[bank], merge=merge)

    def tr(self, out_ap, in_ap, ident_ap, reads, bank, merge=True):
        nc = self.nc
        self.op("pe", lambda: nc.tensor.transpose(out_ap, in_ap, ident_ap), reads=reads, writes=[bank], merge=merge)

    def copy(self, e, out_ap, in_ap, reads, writes, merge=False):
        nc = self.nc
        if e == "act":
            self.op("act", lambda: nc.scalar.copy(out=out_ap, in_=in_ap), reads=reads, writes=writes, merge=merge)
        elif e == "dve":
            self.op("dve", lambda: nc.vector.tensor_copy(out_ap, in_ap), reads=reads, writes=writes, merge=merge)
        else:
            self.op("pool", lambda: nc.gpsimd.tensor_copy(out_ap, in_ap), reads=reads, writes=writes, merge=merge)


def make_consts():
    c = {}
    c["ident"] = np.eye(128, dtype=np.float32)
    i = np.arange(128)
    c["tri_le"] = (i[:, None] <= i[None, :]).astype(np.float32)
    c["neg_lt"] = np.where(i[:, None] < i[None, :], NEG, 0.0).astype(np.float32)
    c["neg_gt"] = np.where(i[:, None] > i[None, :], NEG, 0.0).astype(np.float32)
    c["pos_le"] = np.where(i[:, None] <= i[None, :], -NEG, 0.0).astype(np.float32)
    c["ones"] = np.ones((128, 128), np.float32)
    c["bm16"] = (i[:, None] // 16 == i[None, :]).astype(np.float32)
    for nm, r in (("sel127", 127), ("sel3", 3)):
        sel = np.zeros((128, 128), np.float32)
        sel[r, :] = 1.0
        c[nm] = sel
    names = list(c.keys())
    arr = np.stack([c[n] for n in names], 0)
    return names, arr


CONST_NAMES, CONST_ARR = make_consts()


def build(nl=DEPTH, pb=4, sbn=8, stage="all", with_cache=True):
    k = K()
    nc = k.nc
    NS = sbn * TS
    NR = pb + sbn
    x_p = k.dram("x_p", [pb * T, D], kind="ExternalInput")
    x_s = k.dram("x_s", [NS, D], kind="ExternalInput")
    c_ps = k.dram("c_ps", [NR, D], kind="ExternalInput")
    if with_cache:
        cache_k = k.dram("cache_k", [nl * NPOOL * 8, 16 * 256], kind="ExternalInput")
        cache_v = k.dram("cache_v", [nl * NPOOL * 8, 16 * 256], kind="ExternalInput")
        cache_kidx = k.dram("cache_kidx", [nl * NPOOL, 128 * 64], kind="ExternalInput")
        page_table = k.dram("page_table", [sbn * 128, 1], I32, kind="ExternalInput")
    state_dn = k.dram("state_dn", [nl, sbn * 16, 128, 128], kind="ExternalInput")
    state_conv = k.dram("state_conv", [nl, sbn * 3, 6144], kind="ExternalInput")
    w_ada = k.dram("w_ada", [nl, D, 3 * D], kind="ExternalInput")
    b_ada = k.dram("b_ada", [nl, 3 * D], kind="ExternalInput")
    g_norm = k.dram("g_norm", [nl, D], kind="ExternalInput")
    w_in = k.dram("w_in", [nl, D, D_IN], kind="ExternalInput")
    a_vnorm = k.dram("a_vnorm", [nl, 1024], kind="ExternalInput")
    a_ws = k.dram("a_ws", [nl, 8, 128, 128], kind="ExternalInput")
    a_bs = k.dram("a_bs", [nl, 8 * 128], kind="ExternalInput")
    dn_conv_w = k.dram("dn_conv_w", [nl, 4, 6144], kind="ExternalInput")
    dn_a_log = k.dram("dn_a_log", [nl, 16], kind="ExternalInput")
    dn_dt_bias = k.dram("dn_dt_bias", [nl, 16], kind="ExternalInput")
    dn_onorm = k.dram("dn_onorm", [nl, 128], kind="ExternalInput")
    w_out = k.dram("w_out", [nl, D, D], kind="ExternalInput")
    g_final = k.dram("g_final", [1, D], kind="ExternalInput")
    consts = k.dram("consts", list(CONST_ARR.shape), kind="ExternalInput")

    O = {}
    for name, shape in [("y_p", [pb * T, D]), ("y_s", [NS, D]), ("p_k", [nl, pb * T, 256]), ("p_v", [nl, pb * T, 256]),
                        ("p_kidx", [nl, pb * T, 64]), ("p_dn", [nl, pb * 16, 128, 128]), ("p_conv", [nl, pb * 3, 6144]),
                        ("s_k", [nl, NS, 256]), ("s_v", [nl, NS, 256]), ("s_kidx", [nl, NS, 64]),
                        ("s_dn", [nl, sbn * 16, 128, 128]), ("s_conv", [nl, sbn * 3, 6144]), ("s_amlp_v", [nl, NS, 1024])]:
        O[name] = k.dram(name, shape, kind="ExternalOutput")

    m_scr = k.dram("m_scr", [NR, 3 * D])
    P_fm = k.dram("P_fm", [FM_TOT, T])
    P_v = k.dram("P_v", [T, 1024])
    P_zb = k.dram("P_zb", [T, 2048])
    P_ab = k.dram("P_ab", [T, 32])
    P_wi = k.dram("P_wi", [T, 16])
    sTM = k.dram("sTM", [NS, D_IN])
    xa = [k.dram("xa_p", [pb * T, D]), k.dram("xa_s", [NS, D])]
    xb = [k.dram("xb_p", [pb * T, D]), k.dram("xb_s", [NS, D])]
    mixT = k.dram("mixT", [D, T], BF16)

    CT = {}
    for i, n in enumerate(CONST_NAMES):
        CT[n] = k.sb([128, 128], F32, "c_" + n)
        k.dma("sp", CT[n][:], consts.t[i], reads=[consts], writes=[CT[n]])
    ident = CT["ident"]
    ones = CT["ones"]
    k.banks = [k.ps([128, 512], F32, "bank") for _ in range(8)]
    wbuf = [k.sb([128, 32, 256], BF16, "wbuf") for _ in range(2)]
    wb_i = [0]
    stage_bufs = [k.sb([128, 512], F32, "stage") for _ in range(4)]
    st_i = [0]

    def next_w():
        b = wbuf[wb_i[0] % 2]
        wb_i[0] += 1
        return b

    def next_stage():
        b = stage_bufs[st_i[0] % len(stage_bufs)]
        st_i[0] += 1
        return b

    def evac_to_dram(bank, P, W, dst_ap, dst_buf, extra=()):
        st = next_stage()
        k.copy(k.ev(), st[0:P, 0:W], bank[0:P, 0:W], reads=[bank], writes=[st])
        k.dma("sp", dst_ap, st[0:P, 0:W], reads=[st], writes=[dst_buf], merge=True)
        for ap, buf in extra:
            k.dma("sp", ap, st[0:P, 0:W], reads=[st], writes=[buf], merge=True)

    def A(e, f, reads, writes, merge=False):
        return k.op(e, f, reads=reads, writes=writes, merge=merge)

    def rstd_from_ss(ss_ap, out_ap, n, buf, P):
        A("act", lambda: nc.scalar.activation(out=out_ap, in_=ss_ap, func=AF.Sqrt, bias=EPS, scale=1.0 / n), [buf], [buf])
        A("dve", lambda: nc.vector.reciprocal(out_ap, out_ap), [buf], [buf])

    mixTs = k.sb([128, 32, NS], BF16, "mixTs")
    sFM = k.sb([128, NSG, NS], F32, "sFM")

    def adaln(l):
        with k.scope():
            craw = k.sb([128, NR, 32], F32, "craw")
            cbf = k.sb([128, NR, 32], BF16, "cbf")
            k.dma("sp", craw[:], c_ps.t.rearrange("r (p c) -> p r c", c=32), reads=[c_ps], writes=[craw])
            A("act", lambda: nc.scalar.activation(out=cbf[:], in_=craw[:], func=AF.Silu), [craw], [cbf])
            wv = w_ada.t[l].rearrange("(p c) n -> p c n", c=32)
            for blk in range(3 * D // 256):
                wb = next_w()
                k.dma("pool", wb[:], wv[:, :, blk * 256:(blk + 1) * 256], reads=[w_ada], writes=[wb])
                bank = k.bank()
                k.mm(bank[0:NR, 0:256], [(cbf[:, :, c], wb[:, c, :]) for c in range(32)], reads=[cbf, wb], bank=bank)
                evac_to_dram(bank, NR, 256, m_scr.t[:, blk * 256:(blk + 1) * 256], m_scr)

    def mod_cols(l, Acol, Bcol):
        with k.scope():
            nrow = 2 * NR + 3
            rows = k.sb([32, nrow, 128], F32, "rows")
            for r in range(NR):
                for j in range(2):
                    k.dma("sp", rows[:, r * 2 + j, :], m_scr.t[r, j * D:(j + 1) * D].rearrange("(c p) -> c p", p=128),
                          reads=[m_scr], writes=[rows], merge=True)
            for j in range(2):
                k.dma("sp", rows[:, 2 * NR + j, :], b_ada.t[l, j * D:(j + 1) * D].rearrange("(c p) -> c p", p=128),
                      reads=[b_ada], writes=[rows], merge=True)
            k.dma("sp", rows[:, 2 * NR + 2, :], g_norm.t[l].rearrange("(c p) -> c p", p=128), reads=[g_norm], writes=[rows], merge=True)
            cols = k.sb([128, nrow, 32], F32, "cols")
            for j0 in range(0, nrow, 16):
                bank = k.bank()
                n = min(16, nrow - j0)
                for j in range(n):
                    k.tr(bank[:, j * 32:(j + 1) * 32], rows[:, j0 + j, :], ident[0:32, 0:32], reads=[rows, ident], bank=bank, merge=(j > 0))
                k.copy("dve", cols[:, j0:j0 + n, :].rearrange("p a b -> p (a b)"), bank[:, 0:n * 32], reads=[bank], writes=[cols], merge=True)
            for r in range(NR):
                A("dve", lambda: nc.vector.tensor_tensor(Bcol[:, r, :], cols[:, r * 2 + 0, :], cols[:, 2 * NR, :], op=ALU.add), [cols], [Bcol], True)
                A("dve", lambda: nc.vector.scalar_tensor_tensor(out=Acol[:, r, :], in0=cols[:, r * 2 + 1, :], scalar=1.0,
                                                                 in1=cols[:, 2 * NR + 1, :], op0=ALU.add, op1=ALU.add), [cols], [Acol], True)
                A("dve", lambda: nc.vector.tensor_tensor(Acol[:, r, :], Acol[:, r, :], cols[:, 2 * NR + 2, :], op=ALU.mult), [cols, Acol], [Acol], True)

    def h_phase(xin, row0, ntile, P, hT, Acol, Bcol, rows_of):
        with k.scope():
            xt = k.sb([128, D], F32, "xt")
            junk = k.sb([128, 1024], BF16, "junk")
            ssq = k.sb([128, 8], F32, "ssq")
            for ti in range(ntile):
                k.dma("act", xt[0:P, :], xin.t[row0 + ti * P:row0 + (ti + 1) * P, :], reads=[xin], writes=[xt])
                for hh in range(4):
                    A("act", lambda: nc.scalar.activation(out=junk[0:P, :], in_=xt[0:P, hh * 1024:(hh + 1) * 1024],
                                                          func=AF.Square, accum_out=ssq[0:P, hh:hh + 1]), [xt], [junk, ssq])
                A("dve", lambda: nc.vector.tensor_reduce(out=ssq[0:P, 4:5], in_=ssq[0:P, 0:4], axis=AX.X, op=ALU.add), [ssq], [ssq])
                rstd_from_ss(ssq[0:P, 4:5], ssq[0:P, 5:6], D, ssq, P)
                A("dve", lambda: nc.vector.tensor_scalar(xt[0:P, :], xt[0:P, :], ssq[0:P, 5:6], None, op0=ALU.mult), [xt, ssq], [xt])
                for c4 in range(8):
                    bank = k.bank()
                    for j in range(4):
                        c = c4 * 4 + j
                        k.tr(bank[:, j * 128:j * 128 + P], xt[0:P, c * 128:(c + 1) * 128], ident[0:P, 0:P],
                             reads=[xt, ident], bank=bank, merge=(j > 0))
                    eng_ = k.ev()
                    for j in range(4):
                        c = c4 * 4 + j
                        for (p0, p1, r) in rows_of(ti, P):
                            dst = hT[:, c, ti * P + p0:ti * P + p1]
                            src = bank[:, j * 128 + p0:j * 128 + p1]
                            if eng_ == "act":
                                A("act", lambda: nc.scalar.activation(out=dst, in_=src, func=AF.Identity,
                                                                      bias=Bcol[:, r, c:c + 1], scale=Acol[:, r, c:c + 1]),
                                  [bank, Acol, Bcol], [hT], True)
                            else:
                                A("dve", lambda: nc.vector.tensor_scalar(dst, src, Acol[:, r, c:c + 1], Bcol[:, r, c:c + 1],
                                                                         op0=ALU.mult, op1=ALU.add), [bank, Acol, Bcol], [hT], True)

    def inproj_sample(l, name, off, w, col0, wb, hTs):
        bank = k.bank()
        k.mm(bank[0:NS, 0:w], [(hTs[:, c, :], wb[:, c, 0:w]) for c in range(32)], reads=[hTs, wb], bank=bank)
        extra = []
        if name == "kc":
            extra.append((O["s_k"].t[l, :, :], O["s_k"]))
        if name == "vc":
            extra.append((O["s_v"].t[l, :, :], O["s_v"]))
        if name == "ki":
            extra.append((O["s_kidx"].t[l, :, :], O["s_kidx"]))
        evac_to_dram(bank, NS, w, sTM.t[:, col0:col0 + w], sTM, extra)
        if name == "qkv":
            for j in range(sbn):
                k.dma("sp", O["s_conv"].t[l, j * 3:(j + 1) * 3, off:off + w], sTM.t[j * 4 + 1:j * 4 + 4, col0:col0 + w],
                      reads=[sTM], writes=[O["s_conv"]], merge=True)
        if name in FMSEG:
            wfm = 128 if name == "ki" else w
            for g0 in range(0, wfm, 128):
                bank = k.bank()
                k.mm(bank[:, 0:NS], [(wb[:, c, g0:g0 + 128], hTs[:, c, :]) for c in range(32)], reads=[hTs, wb], bank=bank)
                sg = SG[name] + (off + g0) // 128
                k.copy(k.ev(), sFM[:, sg, :], bank[:, 0:NS], reads=[bank], writes=[sFM], merge=True)

    def inproj(l, hT, ntok, prompt, b, hTs=None):
        wv = w_in.t[l].rearrange("(c p) n -> p c n", p=128)
        for name, s0, sw in SEGS:
            for off in range(0, sw, 256):
                w = min(256, sw - off)
                col0 = s0 + off
                wb = next_w()
                k.dma("pool", wb[:, :, 0:w], wv[:, :, col0:col0 + w], reads=[w_in], writes=[wb])
                if name == "ki":
                    k.dma("pool", wb[:, :, 64:128], wv[:, :, col0:col0 + w], reads=[w_in], writes=[wb], merge=True)
                if hTs is not None:
                    inproj_sample(l, name, off, w, col0, wb, hTs)
                if not prompt:
                    continue
                if name == "qkv":
                    bank = k.bank()
                    k.mm(bank[0:3, 0:w], [(hT[:, c, T - 3:T], wb[:, c, 0:w]) for c in range(32)], reads=[hT, wb], bank=bank)
                    evac_to_dram(bank, 3, w, O["p_conv"].t[l, b * 3:(b + 1) * 3, off:off + w], O["p_conv"])
                if name in TMSEG:
                    for ti in range(NT):
                        bank = k.bank()
                        k.mm(bank[:, 0:w], [(hT[:, c, ti * 128:(ti + 1) * 128], wb[:, c, 0:w]) for c in range(32)],
                             reads=[hT, wb], bank=bank)
                        rs = slice(ti * 128, (ti + 1) * 128)
                        ors = slice(b * T + ti * 128, b * T + (ti + 1) * 128)
                        if name == "v":
                            dst, dbuf = P_v.t[rs, off:off + w], P_v
                        elif name == "zb":
                            dst, dbuf = P_zb.t[rs, off:off + w], P_zb
                        elif name == "ab":
                            dst, dbuf = P_ab.t[rs, :], P_ab
                        elif name == "wi":
                            dst, dbuf = P_wi.t[rs, :], P_wi
                        elif name == "kc":
                            dst, dbuf = O["p_k"].t[l, ors, :], O["p_k"]
                        elif name == "vc":
                            dst, dbuf = O["p_v"].t[l, ors, :], O["p_v"]
                        else:
                            dst, dbuf = O["p_kidx"].t[l, ors, :], O["p_kidx"]
                        evac_to_dram(bank, 128, w, dst, dbuf)
                if name in FMSEG:
                    wfm = 128 if name == "ki" else w
                    for g0 in range(0, wfm, 128):
                        row0 = FM_ROWS[name] + off + g0
                        for tb in range(4):
                            bank = k.bank()
                            k.mm(bank[:, :], [(wb[:, c, g0:g0 + 128], hT[:, c, tb * 512:(tb + 1) * 512]) for c in range(32)],
                                 reads=[hT, wb], bank=bank)
                            evac_to_dram(bank, 128, 512, P_fm.t[row0:row0 + 128, tb * 512:(tb + 1) * 512], P_fm)

    def mixerA_setup(l, wmT, absr, avn):
        with k.scope():
            wnat = k.sb([128, 8, 128], F32, "wnat")
            k.dma("sp", wnat[:], a_ws.t[l].rearrange("h t s -> t h s"), reads=[a_ws], writes=[wnat])
            for hg in range(2):
                bank = k.bank()
                for hh in range(4):
                    k.tr(bank[:, hh * 128:(hh + 1) * 128], wnat[:, hg * 4 + hh, :], ident[:, :], reads=[wnat, ident], bank=bank, merge=(hh > 0))
                for hh in range(4):
                    A("dve", lambda: nc.vector.tensor_tensor(wmT[:, hg * 4 + hh, :], bank[:, hh * 128:(hh + 1) * 128], CT["tri_le"][:, :], op=ALU.mult),
                      [bank, CT["tri_le"]], [wmT], True)
        k.dma("sp", absr[0:1, :], a_bs.t[l:l + 1, :], reads=[a_bs], writes=[absr])
        k.dma("sp", avn[:], a_vnorm.t[l:l + 1, :].to_broadcast([128, 1024]), reads=[a_vnorm], writes=[avn])

    def mixerA_chunk(l, C, vt, uT, zaT, wmT, absr, avn, tmp, out_cb, vout_cb=None):
        (uT_ap, uT_buf), (zaT_ap, zaT_buf) = uT, zaT
        g1, dd, st4, gu, sz, yb = tmp
        A("act", lambda: nc.scalar.activation(out=g1[0:C, :], in_=vt[0:C, :], func=AF.Gelu_apprx_tanh, accum_out=st4[0:C, 0:1]), [vt], [g1, st4])
        A("dve", lambda: nc.vector.tensor_scalar(st4[0:C, 1:2], st4[0:C, 0:1], -1.0 / 1024, None, op0=ALU.mult), [st4], [st4])
        A("dve", lambda: nc.vector.tensor_scalar(dd[0:C, :], g1[0:C, :], st4[0:C, 1:2], None, op0=ALU.add), [g1, st4], [dd])
        A("act", lambda: nc.scalar.activation(out=g1[0:C, :], in_=dd[0:C, :], func=AF.Square, accum_out=st4[0:C, 2:3]), [dd], [g1, st4])
        rstd_from_ss(st4[0:C, 2:3], st4[0:C, 3:4], 1024, st4, C)
        A("dve", lambda: nc.vector.scalar_tensor_tensor(out=dd[0:C, :], in0=dd[0:C, :], scalar=st4[0:C, 3:4], in1=avn[0:C, :],
                                                         op0=ALU.mult, op1=ALU.mult), [dd, st4, avn], [dd])
        if vout_cb is not None:
            vout_cb(dd)
        A("act", lambda: nc.scalar.activation(out=gu[:, :, 0:C], in_=uT_ap, func=AF.Gelu_apprx_tanh), [uT_buf], [gu])
        A("act", lambda: nc.scalar.activation(out=sz[:, :, 0:C], in_=zaT_ap, func=AF.Silu), [zaT_buf], [sz])
        for hg in range(2):
            bank = k.bank()
            for hh in range(4):
                h = hg * 4 + hh
                k.mm(bank[:, hh * C:(hh + 1) * C], [(dd[0:C, h * 128:(h + 1) * 128], wmT[0:C, h, 0:C]),
                                                   (ones[0:1, :], absr[0:1, h * 128:h * 128 + C])],
                     reads=[dd, wmT, ones, absr], bank=bank, merge=(hh > 0))
            A("dve", lambda: nc.vector.tensor_tensor(gu[:, hg * 4:(hg + 1) * 4, 0:C], gu[:, hg * 4:(hg + 1) * 4, 0:C],
                                                     bank[:, 0:4 * C].rearrange("p (h c) -> p h c", c=C), op=ALU.mult), [gu, bank], [gu])
        A("dve", lambda: nc.vector.tensor_tensor(yb[:, :, 0:C], gu[:, :, 0:C], sz[:, :, 0:C], op=ALU.mult), [gu, sz], [yb])
        out_cb(yb)

    def mixerA_prompt(l):
        with k.scope():
            wmT = k.sb([128, 8, 128], F32, "wmT")
            absr = k.sb([1, 1024], F32, "absr")
            avn = k.sb([128, 1024], F32, "avn")
            mixerA_setup(l, wmT, absr, avn)
            vt = [k.sb([128, 1024], F32, "vt") for _ in range(2)]
            uT = [k.sb([128, 8, 128], F32, "uT") for _ in range(2)]
            zaT = [k.sb([128, 8, 128], F32, "zaT") for _ in range(2)]
            tmp = (k.sb([128, 1024], F32, "g1"), k.sb([128, 1024], F32, "dd"), k.sb([128, 4], F32, "st4"),
                   k.sb([128, 8, 128], F32, "gu"), k.sb([128, 8, 128], F32, "sz"), k.sb([128, 8, 128], BF16, "yb"))
            for ci in range(NT):
                ts = slice(ci * 128, (ci + 1) * 128)
                v_, u_, z_ = vt[ci % 2], uT[ci % 2], zaT[ci % 2]
                k.dma("sp", v_[:], P_v.t[ts, :], reads=[P_v], writes=[v_])
                r0 = FM_ROWS["u"]
                k.dma("act", u_[:], P_fm.t[r0:r0 + 1024, ts].rearrange("(h d) t -> d h t", d=128), reads=[P_fm], writes=[u_])
                r0 = FM_ROWS["za"]
                k.dma("act", z_[:], P_fm.t[r0:r0 + 1024, ts].rearrange("(h d) t -> d h t", d=128), reads=[P_fm], writes=[z_])

                def out_cb(yb, ts=ts):
                    k.dma("sp", mixT.t[0:1024, ts].rearrange("(h d) t -> d h t", d=128), yb[:], reads=[yb], writes=[mixT], merge=True)
                mixerA_chunk(l, 128, v_, (u_[:], u_), (z_[:], z_), wmT, absr, avn, tmp, out_cb)

    def mixerA_sample(l):
        with k.scope():
            wmT = k.sb([128, 8, 128], F32, "wmT")
            absr = k.sb([1, 1024], F32, "absr")
            avn = k.sb([128, 1024], F32, "avn")
            mixerA_setup(l, wmT, absr, avn)
            vt = k.sb([TS, 1024], F32, "vt")
            tmp = (k.sb([TS, 1024], F32, "g1"), k.sb([TS, 1024], F32, "dd"), k.sb([TS, 4], F32, "st4"),
                   k.sb([128, 8, TS], F32, "gu"), k.sb([128, 8, TS], F32, "sz"), k.sb([128, 8, TS], BF16, "yb"))
            for j in range(sbn):
                ts = slice(j * TS, (j + 1) * TS)
                k.dma("sp", vt[:], sTM.t[ts, 1024:2048], reads=[sTM], writes=[vt])

                def out_cb(yb, ts=ts):
                    A("dve", lambda: nc.vector.tensor_copy(mixTs[:, 0:8, ts], yb[:]), [yb], [mixTs], True)

                def vout_cb(dd, ts=ts):
                    k.dma("sp", O["s_amlp_v"].t[l, ts, :], dd[0:TS, :], reads=[dd], writes=[O["s_amlp_v"]], merge=True)
                mixerA_chunk(l, TS, vt, (sFM[:, SG["u"]:SG["u"] + 8, ts], sFM), (sFM[:, SG["za"]:SG["za"] + 8, ts], sFM),
                             wmT, absr, avn, tmp, out_cb, vout_cb)

    def mixerB(l, prompt, bj):
        L = T if prompt else TS
        C = 128 if prompt else TS
        nch = L // C
        sel = CT["sel127"] if prompt else CT["sel3"]
        nsq = 6 if prompt else 1
        with k.scope():
            convw = k.sb([128, 48, 4], F32, "convw")
            with k.scope():
                cw_nat = k.sb([4, 6144], F32, "cw_nat")
                k.dma("sp", cw_nat[:], dn_conv_w.t[l], reads=[dn_conv_w], writes=[cw_nat])
                bank = k.bank()
                for g in range(48):
                    k.tr(bank[:, g * 4:(g + 1) * 4], cw_nat[0:4, g * 128:(g + 1) * 128], ident[0:4, 0:4], reads=[cw_nat, ident], bank=bank, merge=(g > 0))
                k.copy("dve", convw[:].rearrange("p g j -> p (g j)"), bank[:, 0:192], reads=[bank], writes=[convw])
            hp = k.sb([128, 3, 16], F32, "hp")
            k.dma("sp", hp[:, 0, :], dn_a_log.t[l:l + 1, :].to_broadcast([128, 16]), reads=[dn_a_log], writes=[hp], merge=True)
            k.dma("sp", hp[:, 1, :], dn_dt_bias.t[l:l + 1, :].to_broadcast([128, 16]), reads=[dn_dt_bias], writes=[hp], merge=True)
            A("act", lambda: nc.scalar.activation(out=hp[:, 2, :], in_=hp[:, 0, :], func=AF.Exp), [hp], [hp])
            A("dve", lambda: nc.vector.tensor_scalar(hp[:, 2, :], hp[:, 2, :], -1.0, None, op0=ALU.mult), [hp], [hp])
            onb = k.sb([128, 128], F32, "onb")
            k.dma("sp", onb[:], dn_onorm.t[l:l + 1, :].to_broadcast([128, 128]), reads=[dn_onorm], writes=[onb])
            abt = k.sb([128, nch, 32], F32, "abt")
            if prompt:
                k.dma("sp", abt[:], P_ab.t.rearrange("(n c) f -> c n f", c=128), reads=[P_ab], writes=[abt])
            else:
                k.dma("sp", abt[0:C, 0, :], sTM.t[bj * TS:(bj + 1) * TS, SEG["ab"][0]:SEG["ab"][0] + 32], reads=[sTM], writes=[abt])
            gt = k.sb([128, nch, 16], F32, "gt")
            beta = k.sb([128, nch, 16], F32, "beta")
            Gt = k.sb([128, nch, 16], F32, "Gt")
            eG = k.sb([128, nch, 16], F32, "eG")
            eGd = k.sb([128, nch, 16], F32, "eGd")
            eGl = k.sb([128, nch, 16], F32, "eGl")
            bE = k.sb([128, nch, 16], F32, "bE")
            nbeta = k.sb([128, nch, 16], F32, "nbeta")
            for n in range(nch):
                A("dve", lambda: nc.vector.tensor_tensor(gt[0:C, n, :], abt[0:C, n, 0:16], hp[0:C, 1, :], op=ALU.add), [abt, hp], [gt], True)
            A("act", lambda: nc.scalar.activation(out=gt[0:C], in_=gt[0:C], func=AF.Exp), [gt], [gt])
            A("act", lambda: nc.scalar.activation(out=gt[0:C], in_=gt[0:C], func=AF.Ln, bias=1.0, scale=1.0), [gt], [gt])
            for n in range(nch):
                A("dve", lambda: nc.vector.tensor_tensor(gt[0:C, n, :], gt[0:C, n, :], hp[0:C, 2, :], op=ALU.mult), [gt, hp], [gt], True)
            A("act", lambda: nc.scalar.activation(out=beta[0:C], in_=abt[0:C, :, 16:32], func=AF.Sigmoid), [abt], [beta])
            A("dve", lambda: nc.vector.tensor_scalar(nbeta[0:C], beta[0:C], -1.0, None, op0=ALU.mult), [beta], [nbeta])
            bank = k.bank()
            k.mm(bank[0:C, 0:nch * 16], [(CT["tri_le"][0:C, 0:C], gt[0:C].rearrange("p n h -> p (n h)"))], reads=[CT["tri_le"], gt], bank=bank)
            k.copy("dve", Gt[0:C].rearrange("p n h -> p (n h)"), bank[0:C, 0:nch * 16], reads=[bank], writes=[Gt])
            A("act", lambda: nc.scalar.activation(out=eG[0:C], in_=Gt[0:C], func=AF.Exp), [Gt], [eG])
            A("dve", lambda: nc.vector.tensor_tensor(bE[0:C], eG[0:C], beta[0:C], op=ALU.mult), [eG, beta], [bE])
            bank = k.bank()
            k.mm(bank[:, 0:nch * 16], [(sel[0:C, :], Gt[0:C].rearrange("p n h -> p (n h)"))], reads=[sel, Gt], bank=bank)
            A("act", lambda: nc.scalar.activation(out=eGl[:].rearrange("p n h -> p (n h)"), in_=bank[:, 0:nch * 16], func=AF.Exp), [bank], [eGl])
            A("dve", lambda: nc.vector.tensor_tensor(eGd[0:C].rearrange("p n h -> p (n h)"), bank[0:C, 0:nch * 16],
                                                     Gt[0:C].rearrange("p n h -> p (n h)"), op=ALU.subtract), [bank, Gt], [eGd])
            A("act", lambda: nc.scalar.activation(out=eGd[0:C], in_=eGd[0:C], func=AF.Exp), [eGd], [eGd])

            xp = [k.sb([128, 3 + L], F32, "xp") for _ in range(3)]
            qkv = [k.sb([128, L], F32, "qkvc") for _ in range(3)]
            sq = k.sb([128, min(L, 512)], F32, "sq")
            rs = k.sb([128, min(L, 512)], F32, "rs")
            S = k.sb([128, 128], F32, "S")
            zb_t = k.sb([128, 128], F32, "zb_t")
            U = [dict(kv=k.sb([128, 256], F32, "kv"), vb=k.sb([128, 128], F32, "vb"), kbg=k.sb([128, 128], F32, "kbg"),
                      kg=k.sb([128, 128], F32, "kg"), dg=k.sb([128, 128], F32, "dg"), z1=k.sb([128, 128], F32, "z1"),
                      z2=k.sb([128, 128], F32, "z2"), X=[k.sb([128, 128], F32, "X") for _ in range(2)],
                      XT=[k.sb([128, 128], F32, "XT") for _ in range(2)], R=[k.sb([128, 128], F32, "R") for _ in range(2)],
                      attnT=k.sb([128, 128], F32, "attnT"), u=k.sb([128, 128], F32, "u"), wT=k.sb([128, 128], F32, "wT"),
                      vnew=k.sb([128, 128], F32, "vnew"), o1=k.sb([128, 128], F32, "o1"), o=k.sb([128, 128], F32, "o"),
                      st=k.sb([128, 4], F32, "ost"), y=k.sb([128, 128], F32, "y"), yT=k.sb([128, 128], BF16, "yT"))
                 for _ in range(8 if prompt else 2)]
            ui = 0
            qrow = FM_ROWS["qkv"]
            for h in range(16):
                for i3 in range(3):
                    ch0 = i3 * 2048 + h * 128
                    if prompt:
                        A("pool", lambda: nc.gpsimd.memset(xp[i3][:, 0:3], 0.0), [], [xp[i3]])
                        k.dma("sp", xp[i3][:, 3:3 + L], P_fm.t[qrow + ch0:qrow + ch0 + 128, :], reads=[P_fm], writes=[xp[i3]], merge=True)
                    else:
                        sg = SG["qkv"] + ch0 // 128
                        k.dma("sp", xp[i3][:, 0:3], state_conv.t[l, bj * 3:(bj + 1) * 3, ch0:ch0 + 128].rearrange("j d -> d j"),
                              reads=[state_conv], writes=[xp[i3]], allow_slow_non_contiguous=True)
                        A("dve", lambda: nc.vector.tensor_copy(xp[i3][:, 3:3 + L], sFM[:, sg, bj * TS:(bj + 1) * TS]), [sFM], [xp[i3]], True)
                    g = ch0 // 128
                    o_ = qkv[i3]
                    A("dve", lambda: nc.vector.tensor_scalar(o_[:, :], xp[i3][:, 0:L], convw[:, g, 0:1], None, op0=ALU.mult), [xp[i3], convw], [o_])
                    for j in range(1, 4):
                        A("dve",
                          lambda: nc.vector.scalar_tensor_tensor(out=o_[:, :], in0=xp[i3][:, j:j + L], scalar=convw[:, g, j:j + 1],
                                                                                              in1=o_[:, :], op0=ALU.mult, op1=ALU.add),
                          [xp[i3], convw, o_], [o_])
                    A("act", lambda: nc.scalar.activation(out=o_[:, :], in_=o_[:, :], func=AF.Silu), [o_], [o_])
                for i3 in range(2):
                    o_ = qkv[i3]
                    for t0 in range(0, L, 512):
                        wd = min(512, L - t0)
                        A("act", lambda: nc.scalar.activation(out=sq[:, 0:wd], in_=o_[:, t0:t0 + wd], func=AF.Square), [o_], [sq])
                        bank = k.bank()
                        k.mm(bank[:, 0:wd], [(ones[:, :], sq[:, 0:wd])], reads=[ones, sq], bank=bank)
                        A("act", lambda: nc.scalar.activation(out=rs[:, 0:wd], in_=bank[:, 0:wd], func=AF.Sqrt, bias=EPS, scale=1.0), [bank], [rs])
                        A("dve", lambda: nc.vector.reciprocal(rs[:, 0:wd], rs[:, 0:wd]), [rs], [rs])
                        if i3 == 0:
                            A("dve", lambda: nc.vector.scalar_tensor_tensor(out=o_[:, t0:t0 + wd], in0=o_[:, t0:t0 + wd], scalar=128.0 ** -0.5,
                                                                             in1=rs[:, 0:wd], op0=ALU.mult, op1=ALU.mult), [o_, rs], [o_])
                        else:
                            A("dve", lambda: nc.vector.tensor_tensor(o_[:, t0:t0 + wd], o_[:, t0:t0 + wd], rs[:, 0:wd], op=ALU.mult), [o_, rs], [o_])
                qT, kT, vT = qkv
                if prompt:
                    A("pool", lambda: nc.gpsimd.memset(S[:], 0.0), [], [S])
                else:
                    k.dma("sp", S[:], state_dn.t[l, bj * 16 + h], reads=[state_dn], writes=[S])
                def par(n, u_):
                    cs = slice(n * C, (n + 1) * C)
                    col = lambda tl: tl[0:C, n, h:h + 1]
                    X, XT, R = u_["X"], u_["XT"], u_["R"]
                    bank = k.bank()
                    k.tr(bank[0:C, 0:128], kT[:, cs], ident[:, :], reads=[kT, ident], bank=bank, merge=False)
                    k.tr(bank[0:C, 128:256], vT[:, cs], ident[:, :], reads=[vT, ident], bank=bank, merge=True)
                    yield
                    k.copy("act", u_["kv"][0:C, :], bank[0:C, 0:256], reads=[bank], writes=[u_["kv"]])
                    A("pool", lambda: nc.gpsimd.tensor_scalar(u_["vb"][0:C, :], u_["kv"][0:C, 128:256], col(beta), None, op0=ALU.mult), [u_["kv"], beta], [u_["vb"]])
                    A("pool", lambda: nc.gpsimd.tensor_scalar(u_["kbg"][0:C, :], u_["kv"][0:C, 0:128], col(bE), None, op0=ALU.mult), [u_["kv"], bE], [u_["kbg"]])
                    A("pool", lambda: nc.gpsimd.tensor_scalar(u_["kg"][0:C, :], u_["kv"][0:C, 0:128], col(eGd), None, op0=ALU.mult), [u_["kv"], eGd], [u_["kg"]])
                    A("pool", lambda: nc.gpsimd.tensor_scalar(u_["dg"][0:C, 0:C], ident[0:C, 0:C], col(Gt), None, op0=ALU.mult), [ident, Gt], [u_["dg"]])
                    yield
                    bk = k.bank()
                    k.mm(bk[0:C, 0:C], [(kT[:, cs], kT[:, cs])], reads=[kT], bank=bk)
                    k.mm(bk[0:C, 128:128 + C], [(kT[:, cs], qT[:, cs])], reads=[kT, qT], bank=bk, merge=True)
                    k.mm(bk[0:C, 256:256 + C], [(ones[0:C, 0:C], u_["dg"][0:C, 0:C])], reads=[ones, u_["dg"]], bank=bk, merge=True)
                    yield
                    A("dve", lambda: nc.vector.scalar_tensor_tensor(out=u_["z1"][0:C, 0:C], in0=bk[0:C, 256:256 + C], scalar=col(Gt),
                                                                     in1=CT["pos_le"][0:C, 0:C], op0=ALU.subtract, op1=ALU.add), [bk, Gt, CT["pos_le"]], [u_["z1"]])
                    A("dve", lambda: nc.vector.scalar_tensor_tensor(out=u_["z2"][0:C, 0:C], in0=bk[0:C, 256:256 + C], scalar=col(Gt),
                                                                     in1=CT["neg_gt"][0:C, 0:C], op0=ALU.subtract, op1=ALU.add), [bk, Gt, CT["neg_gt"]], [u_["z2"]])
                    yield
                    A("act", lambda: nc.scalar.activation(out=u_["z1"][0:C, 0:C], in_=u_["z1"][0:C, 0:C], func=AF.Exp, scale=-1.0), [u_["z1"]], [u_["z1"]])
                    A("act", lambda: nc.scalar.activation(out=u_["z2"][0:C, 0:C], in_=u_["z2"][0:C, 0:C], func=AF.Exp), [u_["z2"]], [u_["z2"]])
                    X, XT, R = u_["X"], u_["XT"], u_["R"]
                    yield
                    A("dve", lambda: nc.vector.scalar_tensor_tensor(out=XT[0][0:C, 0:C], in0=bk[0:C, 0:C], scalar=col(nbeta), in1=u_["z1"][0:C, 0:C],
                                                                     op0=ALU.mult, op1=ALU.mult), [bk, nbeta, u_["z1"]], [XT[0]])
                    A("dve", lambda: nc.vector.tensor_tensor(u_["attnT"][0:C, 0:C], bk[0:C, 128:128 + C], u_["z2"][0:C, 0:C], op=ALU.mult), [bk, u_["z2"]], [u_["attnT"]])
                    yield
                    b2 = k.bank()
                    k.tr(b2[0:C, 0:C], XT[0][0:C, 0:C], ident[0:C, 0:C], reads=[XT[0], ident], bank=b2, merge=False)
                    yield
                    k.copy("act", X[0][0:C, 0:C], b2[0:C, 0:C], reads=[b2], writes=[X[0]])
                    A("dve", lambda: nc.vector.tensor_tensor(R[0][0:C, 0:C], b2[0:C, 0:C], ident[0:C, 0:C], op=ALU.add), [b2, ident], [R[0]])
                    cur = 0
                    for kk in range(1, nsq + 1):
                        nx = 1 - cur
                        yield
                        b3 = k.bank()
                        last = (kk == nsq)
                        k.mm(b3[0:C, 0:C], [(X[cur][0:C, 0:C], XT[cur][0:C, 0:C])], reads=[X[cur], XT[cur]], bank=b3)
                        if not last:
                            k.mm(b3[0:C, 128:128 + C], [(XT[cur][0:C, 0:C], X[cur][0:C, 0:C])], reads=[X[cur], XT[cur]], bank=b3, merge=True)
                        yield
                        k.copy("act", XT[nx][0:C, 0:C], b3[0:C, 0:C], reads=[b3], writes=[XT[nx]])
                        if not last:
                            k.copy("dve", X[nx][0:C, 0:C], b3[0:C, 128:128 + C], reads=[b3], writes=[X[nx]])
                        yield
                        b4 = k.bank()
                        k.mm(b4[0:C, 0:C], [(XT[nx][0:C, 0:C], R[cur][0:C, 0:C])], reads=[XT[nx], R[cur]], bank=b4)
                        yield
                        A("dve", lambda: nc.vector.tensor_tensor(R[nx][0:C, 0:C], b4[0:C, 0:C], R[cur][0:C, 0:C], op=ALU.add), [b4, R[cur]], [R[nx]])
                        cur = nx
                    TT = R[cur]
                    yield
                    b5 = k.bank()
                    k.mm(b5[0:C, 0:128], [(TT[0:C, 0:C], u_["vb"][0:C, :])], reads=[TT, u_["vb"]], bank=b5)
                    k.mm(b5[:, 128:128 + C], [(u_["kbg"][0:C, :], TT[0:C, 0:C])], reads=[TT, u_["kbg"]], bank=b5, merge=True)
                    yield
                    k.copy("act", u_["u"][0:C, :], b5[0:C, 0:128], reads=[b5], writes=[u_["u"]])
                    k.copy("dve", u_["wT"][:, 0:C], b5[:, 128:128 + C], reads=[b5], writes=[u_["wT"]])
                    u_["TT"] = TT
                def seq(n, u_):
                    cs = slice(n * C, (n + 1) * C)
                    col = lambda tl: tl[0:C, n, h:h + 1]
                    yield
                    b6 = k.bank()
                    k.mm(b6[0:C, 0:128], [(u_["wT"][:, 0:C], S[:, :])], reads=[u_["wT"], S], bank=b6)
                    k.mm(b6[0:C, 128:256], [(qT[:, cs], S[:, :])], reads=[qT, S], bank=b6, merge=True)
                    yield
                    A("dve", lambda: nc.vector.tensor_tensor(u_["vnew"][0:C, :], u_["u"][0:C, :], b6[0:C, 0:128], op=ALU.subtract), [u_["u"], b6], [u_["vnew"]])
                    A("act", lambda: nc.scalar.activation(out=u_["o1"][0:C, :], in_=b6[0:C, 128:256], func=AF.Identity, scale=col(eG)), [b6, eG], [u_["o1"]])
                    yield
                    b7 = k.bank()
                    k.mm(b7[0:C, 0:128], [(u_["attnT"][0:C, 0:C], u_["vnew"][0:C, :])], reads=[u_["attnT"], u_["vnew"]], bank=b7)
                    k.mm(b7[:, 128:256], [(u_["kg"][0:C, :], u_["vnew"][0:C, :])], reads=[u_["kg"], u_["vnew"]], bank=b7, merge=True)
                    yield
                    A("dve", lambda: nc.vector.tensor_tensor(u_["o"][0:C, :], u_["o1"][0:C, :], b7[0:C, 0:128], op=ALU.add), [u_["o1"], b7], [u_["o"]])
                    A("dve", lambda: nc.vector.scalar_tensor_tensor(out=S[:, :], in0=S[:, :], scalar=eGl[:, n, h:h + 1], in1=b7[:, 128:256],
                                                                     op0=ALU.mult, op1=ALU.add), [S, eGl, b7], [S])
                    if prompt:
                        k.dma("act", zb_t[0:C, :], P_zb.t[cs, h * 128:(h + 1) * 128], reads=[P_zb], writes=[zb_t])
                    else:
                        c0 = SEG["zb"][0] + h * 128
                        k.dma("act", zb_t[0:C, :], sTM.t[bj * TS:(bj + 1) * TS, c0:c0 + 128], reads=[sTM], writes=[zb_t])
                    yield
                    A("act", lambda: nc.scalar.activation(out=u_["y"][0:C, :], in_=u_["o"][0:C, :], func=AF.Square, accum_out=u_["st"][0:C, 0:1]), [u_["o"]], [u_["y"], u_["st"]])
                    rstd_from_ss(u_["st"][0:C, 0:1], u_["st"][0:C, 1:2], 128, u_["st"], C)
                    yield
                    A("dve", lambda: nc.vector.scalar_tensor_tensor(out=u_["y"][0:C, :], in0=u_["o"][0:C, :], scalar=u_["st"][0:C, 1:2], in1=onb[0:C, :],
                                                                     op0=ALU.mult, op1=ALU.mult), [u_["o"], u_["st"], onb], [u_["y"]])
                    A("act", lambda: nc.scalar.activation(out=zb_t[0:C, :], in_=zb_t[0:C, :], func=AF.Silu), [zb_t], [zb_t])
                    A("dve", lambda: nc.vector.tensor_tensor(u_["y"][0:C, :], u_["y"][0:C, :], zb_t[0:C, :], op=ALU.mult), [u_["y"], zb_t], [u_["y"]])
                    yield
                    b8 = k.bank()
                    k.tr(b8[:, 0:C], u_["y"][0:C, :], ident[0:C, 0:C], reads=[u_["y"], ident], bank=b8, merge=False)
                    if prompt:
                        k.copy("act", u_["yT"][:, 0:C], b8[:, 0:C], reads=[b8], writes=[u_["yT"]])
                        k.dma("sp", mixT.t[1024 + h * 128:1024 + (h + 1) * 128, cs], u_["yT"][:, 0:C], reads=[u_["yT"]], writes=[mixT], merge=True)
                    else:
                        k.copy("act", mixTs[:, 8 + h, bj * TS:(bj + 1) * TS], b8[:, 0:C], reads=[b8], writes=[mixTs], merge=True)
                NU = len(U)
                G = NU // 2

                def run(gens):
                    gens = list(gens)
                    while gens:
                        for g_ in list(gens):
                            try:
                                next(g_)
                            except StopIteration:
                                gens.remove(g_)

                def seq_chain(chunks):
                    for n_ in chunks:
                        yield from seq(n_, U[n_ % NU])
                groups = [list(range(g0, min(nch, g0 + G))) for g0 in range(0, nch, G)]
                run([par(n_, U[n_ % NU]) for n_ in groups[0]])
                for gi, grp in enumerate(groups):
                    gens = [seq_chain(grp)]
                    if gi + 1 < len(groups):
                        gens += [par(n_, U[n_ % NU]) for n_ in groups[gi + 1]]
                    run(gens)
                dst = O["p_dn"] if prompt else O["s_dn"]
                k.dma("sp", dst.t[l, bj * 16 + h], S[:, :], reads=[S], writes=[dst], merge=True)

    def mixerC_prompt(l, b):
        with k.scope():
            kiT2 = k.sb([128, 2, T], F32, "kiT2")
            kT = k.sb([128, 2, T], F32, "kT")
            V = k.sb([128, NT, 256], F32, "V")
            A("pool", lambda: nc.gpsimd.memset(kiT2[:], 0.0), [], [kiT2])
            r0 = FM_ROWS["ki"]
            k.dma("sp", kiT2[0:64, 0, :], P_fm.t[r0:r0 + 64, :], reads=[P_fm], writes=[kiT2])
            k.dma("sp", kiT2[64:128, 1, :], P_fm.t[r0 + 64:r0 + 128, :], reads=[P_fm], writes=[kiT2], merge=True)
            r0 = FM_ROWS["kc"]
            k.dma("sp", kT[:], P_fm.t[r0:r0 + 256, :].rearrange("(h d) t -> d h t", d=128), reads=[P_fm], writes=[kT])
            k.dma("sp", V[:], O["p_v"].t[l, b * T:(b + 1) * T, :].rearrange("(n s) f -> s n f", s=128), reads=[O["p_v"]], writes=[V])
            qiT = [k.sb([128, 8, 128], F32, "qiT") for _ in range(2)]
            qT = [k.sb([128, 8, 128], F32, "qT") for _ in range(2)]
            zcT = [k.sb([128, 8, 128], F32, "zcT") for _ in range(2)]
            wi = [k.sb([128, 16], F32, "wi") for _ in range(2)]
            rr = [k.sb([128, 2, 256], F32, "rr") for _ in range(2)]
            Iacc = k.sb([128, T], F32, "Iacc")
            work = k.sb([128, T], F32, "work")
            M = k.sb([128, T], F32, "M")
            MT = k.sb([128, NT, 128], F32, "MT")
            m8 = k.sb([128, 8], F32, "m8")
            thr = k.sb([128, 1], F32, "thr")
            ee = [k.sb([128, 4, 128], F32, "ee") for _ in range(2)]
            pp = [k.sb([128, 4, 128], F32, "pp") for _ in range(2)]
            rden = k.sb([128, 4, 128], F32, "rden")
            oo = k.sb([128, 4, 128], F32, "oo")
            szc = k.sb([128, 8, 128], F32, "szc")
            yc = k.sb([128, 4, 128], BF16, "yc")
            for qi in range(NT):
                ts = slice(qi * 128, (qi + 1) * 128)
                Sk = (qi + 1) * 128
                q_i, q_, z_, w_ = qiT[qi % 2], qT[qi % 2], zcT[qi % 2], wi[qi % 2]
                for buf, nm in ((q_i, "qi"), (q_, "qc"), (z_, "zc")):
                    r0 = FM_ROWS[nm]
                    k.dma("act", buf[:], P_fm.t[r0:r0 + 1024, ts].rearrange("(h d) t -> d h t", d=128), reads=[P_fm], writes=[buf])
                k.dma("act", w_[:], P_wi.t[ts, :], reads=[P_wi], writes=[w_])
                A("dve", lambda: nc.vector.tensor_scalar(w_[:], w_[:], 1.0 / 32.0, None, op0=ALU.mult), [w_], [w_])
                for kb0 in range(0, Sk, 256):
                    wd = min(256, Sk - kb0)
                    for c in range(8):
                        bank = k.bank()
                        k.mm(bank[:, 0:2 * wd].rearrange("p (a s) -> p a s", a=2), [(q_i[:, c, :], kiT2[:, :, kb0:kb0 + wd])], reads=[q_i, kiT2], bank=bank)
                        r_ = rr[c % 2]
                        A("act", lambda: nc.scalar.activation(out=r_[:, :, 0:wd], in_=bank[:, 0:2 * wd].rearrange("p (a s) -> p a s", a=2), func=AF.Relu), [bank], [r_])
                        for h2 in range(2):
                            hh = 2 * c + h2
                            e = "dve"
                            E = nc.vector
                            if hh == 0:
                                A(e, lambda: E.tensor_scalar(Iacc[:, kb0:kb0 + wd], r_[:, h2, 0:wd], w_[:, hh:hh + 1], None, op0=ALU.mult), [r_, w_], [Iacc], True)
                            else:
                                A(e, lambda: E.scalar_tensor_tensor(out=Iacc[:, kb0:kb0 + wd], in0=r_[:, h2, 0:wd], scalar=w_[:, hh:hh + 1],
                                                                    in1=Iacc[:, kb0:kb0 + wd], op0=ALU.mult, op1=ALU.add), [r_, w_, Iacc], [Iacc])
                A("dve", lambda: nc.vector.tensor_tensor(Iacc[:, qi * 128:Sk], Iacc[:, qi * 128:Sk], CT["neg_lt"][:, :], op=ALU.add), [Iacc, CT["neg_lt"]], [Iacc])
                if qi < 2:
                    A("dve", lambda: nc.vector.memset(thr[:], -1.0e29), [], [thr])
                else:
                    src = Iacc
                    for rd in range(32):
                        A("dve", lambda: nc.vector.max(out=m8[:], in_=src[:, 0:Sk]), [src], [m8])
                        if rd < 31:
                            A("dve", lambda: nc.vector.match_replace(out=work[:, 0:Sk], in_to_replace=m8[:], in_values=src[:, 0:Sk], imm_value=NEG), [src, m8], [work])
                            src = work
                    A("dve", lambda: nc.vector.tensor_reduce(out=thr[:], in_=m8[:], axis=AX.X, op=ALU.min), [m8], [thr])
                A("dve", lambda: nc.vector.tensor_scalar(M[:, 0:Sk], Iacc[:, 0:Sk], thr[:, 0:1], None, op0=ALU.is_ge), [Iacc, thr], [M])
                for s4 in range(0, qi + 1, 4):
                    n4 = min(4, qi + 1 - s4)
                    bank = k.bank()
                    for j in range(n4):
                        k.tr(bank[:, j * 128:(j + 1) * 128], M[:, (s4 + j) * 128:(s4 + j + 1) * 128], ident[:, :], reads=[M, ident], bank=bank, merge=(j > 0))
                    k.copy(k.ev(), MT[:, s4:s4 + n4, :].rearrange("p a b -> p (a b)"), bank[:, 0:n4 * 128], reads=[bank], writes=[MT], merge=True)
                A("act", lambda: nc.scalar.activation(out=szc[:], in_=z_[:], func=AF.Silu), [z_], [szc])
                for hkv in range(2):
                    k.reserved = []
                    num = k.bank()
                    den = k.bank()
                    k.reserved = [num, den]
                    for sb_ in range(qi + 1):
                        ks = slice(sb_ * 128, (sb_ + 1) * 128)
                        sc = k.bank()
                        k.mm(sc[:, :].rearrange("p (g t) -> p g t", g=4), [(kT[:, hkv, ks], q_[:, 4 * hkv:4 * hkv + 4, :])], reads=[kT, q_], bank=sc)
                        e_, p_ = ee[sb_ % 2], pp[sb_ % 2]
                        A("act", lambda: nc.scalar.activation(out=e_[:].rearrange("p g t -> p (g t)"), in_=sc[:, :], func=AF.Exp, scale=128.0 ** -0.5), [sc], [e_])
                        for g in range(4):
                            e = "dve" if g % 2 == 0 else "pool"
                            E = nc.vector if g % 2 == 0 else nc.gpsimd
                            A(e, lambda: E.tensor_tensor(p_[:, g, :], e_[:, g, :], MT[:, sb_, :], op=ALU.mult), [e_, MT], [p_], g > 0)
                        k.op("pe", lambda: nc.tensor.matmul(num[:, :], lhsT=V[:, sb_, hkv * 128:(hkv + 1) * 128], rhs=p_[:].rearrange("p g t -> p (g t)"),
                                                            start=(sb_ == 0), stop=(sb_ == qi)), reads=[V, p_], writes=[num], merge=(sb_ > 0))
                        k.op("pe", lambda: nc.tensor.matmul(den[:, :], lhsT=ones[:, :], rhs=p_[:].rearrange("p g t -> p (g t)"),
                                                            start=(sb_ == 0), stop=(sb_ == qi)), reads=[ones, p_], writes=[den], merge=(sb_ > 0))
                    A("dve", lambda: nc.vector.reciprocal(rden[:].rearrange("p g t -> p (g t)"), den[:, :]), [den], [rden])
                    A("dve", lambda: nc.vector.tensor_tensor(oo[:].rearrange("p g t -> p (g t)"), num[:, :], rden[:].rearrange("p g t -> p (g t)"), op=ALU.mult), [num, rden], [oo])
                    A("dve", lambda: nc.vector.tensor_tensor(yc[:], oo[:], szc[:, 4 * hkv:4 * hkv + 4, :], op=ALU.mult), [oo, szc], [yc])
                    k.reserved = []
                    r0 = 3072 + hkv * 512
                    k.dma("sp", mixT.t[r0:r0 + 512, ts].rearrange("(g d) t -> d g t", d=128), yc[:], reads=[yc], writes=[mixT], merge=True)


    def mixerC_sample(l, j):
        NK = 16384
        j4 = j * TS
        ts = slice(j4, j4 + TS)
        with k.scope():
            pt = k.sb([128, 1], I32, "pt")
            k.dma("sp", pt[:], page_table.t[j * 128:(j + 1) * 128, :], reads=[page_table], writes=[pt])
            pt8 = k.sb([128, 1], I32, "pt8")
            A("dve", lambda: nc.vector.tensor_single_scalar(out=pt8[:], in_=pt[:], scalar=3, op=ALU.logical_shift_left), [pt], [pt8])
            MT = k.sb([128, 128, TS], F32, "MTs")
            MTn = k.sb([TS, TS], F32, "MTn")
            NK = 16384
            NKT = NK + TS
            with k.scope():
                I_s = k.sb([TS, NK + 16], F32, "I_s")
                qiT = k.sb([64, TS, 8, 2], F32, "qiT")
                wcol = k.sb([64, 1], F32, "wcol")
                Wsel = k.sb([64, TS], F32, "Wsel")
                gq = SG["qi"]
                A("dve", lambda: nc.vector.tensor_copy(qiT[:, :, :, 0], sFM[0:64, gq:gq + 8, ts].rearrange("p c t -> p t c")), [sFM], [qiT], True)
                bank = k.bank()
                k.mm(bank[0:64, 0:8 * TS], [(ident[:, 64:128], sFM[:, gq:gq + 8, ts])], reads=[ident, sFM], bank=bank)
                A("dve", lambda: nc.vector.tensor_copy(qiT[:, :, :, 1], bank[0:64, 0:8 * TS].rearrange("p (c t) -> p t c", t=TS)), [bank], [qiT], True)
                w0 = SEG["wi"][0]
                for t_ in range(TS):
                    k.dma("sp", wcol[t_ * 16:(t_ + 1) * 16, :], sTM.t[j4 + t_:j4 + t_ + 1, w0:w0 + 16].rearrange("o h -> h o"), reads=[sTM], writes=[wcol], merge=True,
                          allow_slow_non_contiguous=True)
                A("dve", lambda: nc.vector.tensor_scalar(Wsel[:, :], CT["bm16"][0:64, 0:TS], wcol[:, 0:1], 1.0 / 32.0, op0=ALU.mult, op1=ALU.mult), [CT["bm16"], wcol], [Wsel])
                qiT2 = qiT[:].rearrange("p t c h -> p (t c h)")

                def score_block(rhs_ap, rhs_buf, wd, col0, rl):
                    b1 = k.bank()
                    k.mm(b1[0:64, 0:wd], [(qiT2, rhs_ap)], reads=[qiT, rhs_buf], bank=b1)
                    A("act", lambda: nc.scalar.activation(out=rl[:, 0:wd], in_=b1[0:64, 0:wd], func=AF.Relu), [b1], [rl])
                    b2 = k.bank()
                    k.mm(b2[0:TS, 0:wd], [(Wsel[:, :], rl[:, 0:wd])], reads=[Wsel, rl], bank=b2)
                    k.copy("dve", I_s[:, col0:col0 + wd], b2[0:TS, 0:wd], reads=[b2], writes=[I_s], merge=True)
                with k.scope():
                    kx = k.sb([128, 128 * 64], F32, "kx")
                    k.dma("pool", kx[:, :], cache_kidx.t[:, :], reads=[cache_kidx, pt], writes=[kx], indirect=pt[:, 0:1], elem_off=l * NPOOL * 8192)
                    kxT = [k.sb([64, 512], F32, "kxT") for _ in range(2)]
                    rl = [k.sb([64, 512], F32, "rl") for _ in range(2)]
                    for blk in range(NK // 512):
                        bank = k.bank()
                        for q in range(4):
                            kk = blk * 4 + q
                            k.tr(bank[0:64, q * 128:(q + 1) * 128], kx[:, kk * 64:(kk + 1) * 64], ident[:, :], reads=[kx, ident], bank=bank, merge=(q > 0))
                        kt_ = kxT[blk % 2]
                        k.copy(k.ev(), kt_[:, :], bank[0:64, :], reads=[bank], writes=[kt_])
                        score_block(kt_[:, :], kt_, 512, blk * 512, rl[blk % 2])
                    score_block(sFM[0:64, SG["ki"], ts], sFM, TS, NK, rl[0])
                    A("dve", lambda: nc.vector.tensor_tensor(I_s[:, NK:NK + TS], I_s[:, NK:NK + TS], CT["neg_lt"][0:TS, 0:TS], op=ALU.add), [I_s, CT["neg_lt"]], [I_s])
                NKT = NK + TS
                M_s = k.sb([TS, NK + 16], F32, "M_s")
                m8 = k.sb([TS, 8], F32, "m8s")
                thr = k.sb([TS, 1], F32, "thrs")
                cand = k.sb([TS, 16], F32, "cand")
                A("dve", lambda: nc.vector.memset(cand[:, :], NEG), [], [cand])
                A("dve", lambda: nc.vector.tensor_copy(cand[:, 8:8 + TS], I_s[:, NK:NK + TS]), [I_s], [cand])
                src = I_s
                for rd in range(32):
                    A("dve", lambda: nc.vector.max(out=cand[:, 0:8], in_=src[:, 0:NK]), [src, cand], [cand])
                    A("dve", lambda: nc.vector.max(out=m8[:], in_=cand[:, 0:16]), [cand], [m8])
                    if rd < 31:
                        A("dve", lambda: nc.vector.match_replace(out=M_s[:, 0:NK], in_to_replace=m8[:], in_values=src[:, 0:NK], imm_value=NEG), [src, m8], [M_s])
                        A("dve", lambda: nc.vector.match_replace(out=cand[:, 8:16], in_to_replace=m8[:], in_values=cand[:, 8:16], imm_value=NEG), [cand, m8], [cand])
                        src = M_s
                A("dve", lambda: nc.vector.tensor_reduce(out=thr[:], in_=m8[:], axis=AX.X, op=ALU.min), [m8], [thr])
                A("dve", lambda: nc.vector.tensor_scalar(M_s[:, 0:NKT], I_s[:, 0:NKT], thr[:, 0:1], None, op0=ALU.is_ge), [I_s, thr], [M_s])
                bank = k.bank()
                for kk in range(128):
                    k.tr(bank[:, kk * TS:(kk + 1) * TS], M_s[:, kk * 128:(kk + 1) * 128], ident[0:TS, 0:TS], reads=[M_s, ident], bank=bank, merge=(kk > 0))
                k.copy("dve", MT[:].rearrange("p a b -> p (a b)"), bank[:, 0:128 * TS], reads=[bank], writes=[MT])
                bank = k.bank()
                k.tr(bank[0:TS, 0:TS], M_s[:, NK:NK + TS], ident[0:TS, 0:TS], reads=[M_s, ident], bank=bank, merge=False)
                k.copy("dve", MTn[:, :], bank[0:TS, 0:TS], reads=[bank], writes=[MTn])
            qTs = k.sb([128, 2, TS, 4], F32, "qTs")
            gc = SG["qc"]
            for hkv in range(2):
                A("dve", lambda: nc.vector.tensor_copy(qTs[:, hkv, :, :], sFM[:, gc + hkv * 4:gc + hkv * 4 + 4, ts].rearrange("p g t -> p t g")), [sFM], [qTs], True)
            Ks = [k.sb([128, 16 * 256], F32, "Ks") for _ in range(1)]
            Vs = [k.sb([128, 16 * 256], F32, "Vs") for _ in range(1)]
            KT = k.sb([128, 32, 128], F32, "KTs")
            E = k.sb([128, 16, 2, TS, 4], F32, "Es")
            PT = k.sb([128, 16, 2, TS, 4], F32, "PTs")
            vnew = k.sb([TS, 256], F32, "vnews")
            v0 = SEG["vc"][0]
            k.dma("sp", vnew[:, :], sTM.t[ts, v0:v0 + 256], reads=[sTM], writes=[vnew])
            k.reserved = []
            acc = [k.bank() for _ in range(4)]
            k.reserved = list(acc)
            first = [True, True, True, True]

            def accum(bi, out_ap, lhsT, rhs, rd_bufs, last):
                k.op("pe", lambda: nc.tensor.matmul(out_ap, lhsT=lhsT, rhs=rhs, start=first[bi], stop=last), reads=rd_bufs, writes=[acc[bi]], merge=(not first[bi]))
                first[bi] = False
            for grp in range(8):
                K_, V_ = Ks[0], Vs[0]
                c0 = grp * 16 * 256
                k.dma("pool", K_[:, :], cache_k.t[:, :], reads=[cache_k, pt8], writes=[K_], indirect=pt8[:, 0:1], elem_off=(l * NPOOL * 8 + grp) * 4096)
                k.dma("pool", V_[:, :], cache_v.t[:, :], reads=[cache_v, pt8], writes=[V_], indirect=pt8[:, 0:1], elem_off=(l * NPOOL * 8 + grp) * 4096)
                for b8 in range(8):
                    bank = k.bank()
                    for q in range(4):
                        idx = b8 * 4 + q
                        k.tr(bank[:, q * 128:(q + 1) * 128], K_[:, idx * 128:(idx + 1) * 128], ident[:, :], reads=[K_, ident], bank=bank, merge=(q > 0))
                    k.copy(k.ev(), KT[:, b8 * 4:(b8 + 1) * 4, :].rearrange("p a b -> p (a b)"), bank[:, :], reads=[bank], writes=[KT], merge=True)
                sbank = k.bank()
                for idx in range(32):
                    hkv = idx % 2
                    k.mm(sbank[:, idx * 16:(idx + 1) * 16], [(KT[:, idx, :], qTs[:, hkv, :, :].rearrange("p t g -> p (t g)"))], reads=[KT, qTs], bank=sbank, merge=(idx > 0))
                A("act", lambda: nc.scalar.activation(out=E[:].rearrange("p a b c d -> p (a b c d)"), in_=sbank[:, :], func=AF.Exp, scale=128.0 ** -0.5), [sbank], [E])
                for hkv in range(2):
                    for g in range(4):
                        e = "dve" if g % 2 == 0 else "pool"
                        En = nc.vector if g % 2 == 0 else nc.gpsimd
                        A(e, lambda: En.tensor_tensor(PT[:, :, hkv, :, g], E[:, :, hkv, :, g], MT[:, grp * 16:(grp + 1) * 16, :], op=ALU.mult), [E, MT], [PT], True)
                for kk in range(16):
                    for hkv in range(2):
                        lhsT = PT[:, kk, hkv, :, :].rearrange("p t g -> p (t g)")
                        accum(hkv, acc[hkv][0:16, 0:128], lhsT, V_[:, (kk * 2 + hkv) * 128:(kk * 2 + hkv + 1) * 128], [PT, V_], False)
                        accum(2 + hkv, acc[2 + hkv][0:16, 0:1], lhsT, ones[:, 0:1], [PT, ones], False)
            En_ = k.sb([TS, 2, TS, 4], F32, "Enew")
            Pn_ = k.sb([TS, 2, TS, 4], F32, "Pnew")
            sbank = k.bank()
            gk = SG["kc"]
            for hkv in range(2):
                k.mm(sbank[0:TS, hkv * 16:(hkv + 1) * 16], [(sFM[:, gk + hkv, ts], qTs[:, hkv, :, :].rearrange("p t g -> p (t g)"))], reads=[sFM, qTs], bank=sbank, merge=(hkv > 0))
            A("act", lambda: nc.scalar.activation(out=En_[:].rearrange("p b c d -> p (b c d)"), in_=sbank[0:TS, 0:32], func=AF.Exp, scale=128.0 ** -0.5), [sbank], [En_])
            for hkv in range(2):
                for g in range(4):
                    A("dve", lambda: nc.vector.tensor_tensor(Pn_[:, hkv, :, g], En_[:, hkv, :, g], MTn[:, :], op=ALU.mult), [En_, MTn], [Pn_], True)
            for hkv in range(2):
                lhsT = Pn_[:, hkv, :, :].rearrange("p t g -> p (t g)")
                accum(hkv, acc[hkv][0:16, 0:128], lhsT, vnew[:, hkv * 128:(hkv + 1) * 128], [Pn_, vnew], True)
                accum(2 + hkv, acc[2 + hkv][0:16, 0:1], lhsT, ones[0:TS, 0:1], [Pn_, ones], True)
            rd_ = k.sb([16, 2], F32, "rdens")
            oo = k.sb([16, 128], F32, "oos")
            zc = k.sb([16, 128], F32, "zcs")
            z0 = SEG["zc"][0]
            for hkv in range(2):
                A("dve", lambda: nc.vector.reciprocal(rd_[:, hkv:hkv + 1], acc[2 + hkv][0:16, 0:1]), [acc[2 + hkv]], [rd_], True)
                k.dma("sp", zc[:, :], sTM.t[ts, z0 + hkv * 512:z0 + (hkv + 1) * 512].rearrange("t (g d) -> t g d", d=128), reads=[sTM], writes=[zc])
                A("act", lambda: nc.scalar.activation(out=zc[:, :], in_=zc[:, :], func=AF.Silu), [zc], [zc])
                A("dve", lambda: nc.vector.scalar_tensor_tensor(out=oo[:, :], in0=acc[hkv][0:16, 0:128], scalar=rd_[:, hkv:hkv + 1], in1=zc[:, :],
                                                                 op0=ALU.mult, op1=ALU.mult), [acc[hkv], rd_, zc], [oo])
                bank = k.bank()
                k.tr(bank[:, 0:16], oo[:, :], ident[0:16, 0:16], reads=[oo, ident], bank=bank, merge=False)
                A("dve", lambda: nc.vector.tensor_copy(mixTs[:, 24 + hkv * 4:24 + hkv * 4 + 4, ts], bank[:, 0:16].rearrange("p (t g) -> p g t", g=4)), [bank], [mixTs], True)
            k.reserved = []

    def outproj(l, prompt, b, xin, xout):
        ntok = T if prompt else NS
        with k.scope():
            HT = T
            if prompt:
                mT = k.sb([128, 32, HT], BF16, "mT")
            else:
                mT = mixTs
            gate = k.sb([128, D], F32, "gate")
            gb = k.sb([128, 1024], F32, "gb")
            P = 128 if prompt else NS
            if prompt:
                k.dma("sp", gate[:], m_scr.t[b:b + 1, 2 * D:3 * D].to_broadcast([128, D]), reads=[m_scr], writes=[gate])
            else:
                for j in range(sbn):
                    k.dma("sp", gate[j * TS:(j + 1) * TS, :], m_scr.t[pb + j:pb + j + 1, 2 * D:3 * D].to_broadcast([TS, D]), reads=[m_scr], writes=[gate], merge=True)
            for q4 in range(4):
                k.dma("sp", gb[0:P, :], b_ada.t[l:l + 1, 2 * D + q4 * 1024:2 * D + (q4 + 1) * 1024].to_broadcast([P, 1024]), reads=[b_ada], writes=[gb])
                A("dve", lambda: nc.vector.tensor_tensor(gate[0:P, q4 * 1024:(q4 + 1) * 1024], gate[0:P, q4 * 1024:(q4 + 1) * 1024], gb[0:P, :], op=ALU.add), [gate, gb], [gate])
            xt = [k.sb([128, 256], F32, "xo") for _ in range(3)]
            xi = 0
            wv = w_out.t[l].rearrange("(c p) n -> p c n", p=128)
            row_base = b * T if prompt else 0
            for half, blk in [(hf, bl) for hf in range(1) for bl in range(D // 256)]:
                if prompt and blk == 0:
                    k.dma("sp", mT[:], mixT.t[:, half * HT:(half + 1) * HT].rearrange("(c p) t -> p c t", p=128), reads=[mixT], writes=[mT])
                wb = next_w()
                cs = slice(blk * 256, (blk + 1) * 256)
                k.dma("pool", wb[:], wv[:, :, cs], reads=[w_out], writes=[wb])
                ntl = (HT // P) if prompt else 1
                for ti in range(ntl):
                    r_off = half * HT + ti * P
                    rs = slice(row_base + r_off, row_base + r_off + P)
                    x_ = xt[xi % 3]
                    xi += 1
                    k.dma("act", x_[0:P, :], xin.t[rs, cs], reads=[xin], writes=[x_])
                    bank = k.bank()
                    k.mm(bank[0:P, 0:256], [(mT[:, c, ti * P:(ti + 1) * P], wb[:, c, :]) for c in range(32)], reads=[mT, wb], bank=bank)
                    st = next_stage()
                    A("dve", lambda: nc.vector.tensor_tensor(st[0:P, 0:256], bank[0:P, 0:256], gate[0:P, cs], op=ALU.mult), [bank, gate], [st])
                    A("pool", lambda: nc.gpsimd.tensor_tensor(st[0:P, 0:256], st[0:P, 0:256], x_[0:P, :], op=ALU.add), [st, x_], [st])
                    k.dma("sp", xout.t[rs, cs], st[0:P, 0:256], reads=[st], writes=[xout], merge=True)

    def final_norm(xin, yout, ntok):
        with k.scope():
            gf = k.sb([128, D], F32, "gf")
            k.dma("sp", gf[:], g_final.t[0:1, :].to_broadcast([128, D]), reads=[g_final], writes=[gf])
            xt = [k.sb([128, D], F32, "xf") for _ in range(2)]
            junk = k.sb([128, 2048], BF16, "junkf")
            ssq = k.sb([128, 4], F32, "ssqf")
            P = min(128, ntok)
            for ti in range(ntok // P):
                x_ = xt[ti % 2]
                rs = slice(ti * P, (ti + 1) * P)
                k.dma("act", x_[0:P, :], xin.t[rs, :], reads=[xin], writes=[x_])
                for hh in range(2):
                    A("act", lambda: nc.scalar.activation(out=junk[0:P, :], in_=x_[0:P, hh * 2048:(hh + 1) * 2048], func=AF.Square,
                                                          accum_out=ssq[0:P, hh:hh + 1]), [x_], [junk, ssq])
                A("dve", lambda: nc.vector.tensor_tensor(ssq[0:P, 2:3], ssq[0:P, 0:1], ssq[0:P, 1:2], op=ALU.add), [ssq], [ssq])
                rstd_from_ss(ssq[0:P, 2:3], ssq[0:P, 3:4], D, ssq, P)
                A("dve", lambda: nc.vector.scalar_tensor_tensor(out=x_[0:P, :], in0=x_[0:P, :], scalar=ssq[0:P, 3:4], in1=gf[0:P, :],
                                                                 op0=ALU.mult, op1=ALU.mult), [x_, ssq, gf], [x_])
                k.dma("sp", yout.t[rs, :], x_[0:P, :], reads=[x_], writes=[yout], merge=True)

    for l in range(nl):
        xin = [x_p, x_s] if l == 0 else xa
        xout = xa if l == 0 else xb
        adaln(l)
        if stage == "ada":
            break
        with k.scope():
            Acol = k.sb([128, NR, 32], F32, "Acol")
            Bcol = k.sb([128, NR, 32], F32, "Bcol")
            mod_cols(l, Acol, Bcol)
            hTs = k.sb([128, 32, NS], BF16, "hTs")
            h_phase(xin[1], 0, 1, NS, hTs, Acol, Bcol, lambda ti, P: [(j * TS, (j + 1) * TS, pb + j) for j in range(sbn)])
            for b in range(pb):
                with k.scope():
                    hT = k.sb([128, 32, T], BF16, "hT")
                    h_phase(xin[0], b * T, NT, 128, hT, Acol, Bcol, lambda ti, P, b=b: [(0, P, b)])
                    if stage != "h":
                        inproj(l, hT, T, True, b, hTs if b == 0 else None)
                if stage in ("inproj", "h"):
                    continue
                if stage in ("all", "A"):
                    mixerA_prompt(l)
                if stage in ("all", "B"):
                    mixerB(l, True, b)
                if stage in ("all", "C"):
                    mixerC_prompt(l, b)
                if stage == "all":
                    outproj(l, True, b, xin[0], xout[0])
            if stage in ("inproj", "h"):
                continue
            if stage in ("all", "A"):
                mixerA_sample(l)
            if stage in ("all", "B"):
                for j in range(sbn):
                    mixerB(l, False, j)
            if stage == "Cs":
                for j in range(sbn):
                    mixerC_sample(l, j)
            if stage == "all":
                if with_cache:
                    for j in range(sbn):
                        mixerC_sample(l, j)
                else:
                    A("dve", lambda: nc.vector.memset(mixTs[:, 24:32, :], 0.0), [], [mixTs], True)
                outproj(l, False, 0, xin[1], xout[1])
    if stage == "all":
        last = xa if nl == 1 else xb
        final_norm(last[0], O["y_p"], pb * T)
        final_norm(last[1], O["y_s"], NS)
    k.finish()
    return k


_BUILT = {}
NCORES = 8


def kernel(_build_args=None, **inputs):
    inp = {kk: np.asarray(v) for kk, v in inputs.items()}
    ba = dict(nl=DEPTH, pb=max(1, 4 // NCORES), sbn=8 // NCORES, stage="all", with_cache=True)
    if _build_args:
        ba.update(_build_args)
    ncores = ba.pop("ncores", NCORES)
    key = tuple(sorted(ba.items()))
    if key not in _BUILT:
        _BUILT[key] = build(**ba)
    k = _BUILT[key]
    nl, pb, sbn = ba["nl"], ba["pb"], ba["sbn"]
    f = np.float32
    in_maps = []
    for c in range(ncores):
        pc = c // 2 if ncores == 8 else c
        ps = slice(pc * pb, (pc + 1) * pb)
        ss = slice(c * sbn, (c + 1) * sbn)
        m = {
            "x_p": np.ascontiguousarray(inp["x_prompt"][ps], f).reshape(pb * T, D),
            "x_s": np.ascontiguousarray(inp["x_sample"][ss], f).reshape(sbn * TS, D),
            "c_ps": np.ascontiguousarray(np.concatenate([inp["c_prompt"][ps], inp["c_sample"][ss]], 0), f),
            "state_dn": np.ascontiguousarray(inp["state_dn"][:nl, ss], f).reshape(nl, sbn * 16, 128, 128),
            "state_conv": np.ascontiguousarray(inp["state_conv"][:nl, ss], f).reshape(nl, sbn * 3, 6144),
            "w_ada": np.ascontiguousarray(inp["w_ada"][:nl], f),
            "b_ada": np.ascontiguousarray(inp["b_ada"][:nl], f),
            "g_norm": np.ascontiguousarray(inp["g_norm"][:nl], f),
            "w_in": np.ascontiguousarray(inp["w_in"][:nl], f),
            "a_vnorm": np.ascontiguousarray(inp["a_vnorm"][:nl], f),
            "a_ws": np.ascontiguousarray(inp["a_ws"][:nl], f),
            "a_bs": np.ascontiguousarray(inp["a_bs"][:nl], f).reshape(nl, 1024),
            "dn_conv_w": np.ascontiguousarray(inp["dn_conv_w"][:nl], f),
            "dn_a_log": np.ascontiguousarray(inp["dn_a_log"][:nl], f),
            "dn_dt_bias": np.ascontiguousarray(inp["dn_dt_bias"][:nl], f),
            "dn_onorm": np.ascontiguousarray(inp["dn_onorm"][:nl], f),
            "w_out": np.ascontiguousarray(inp["w_out"][:nl], f),
            "g_final": np.ascontiguousarray(inp["g_final"].reshape(1, D), f),
            "consts": CONST_ARR,
        }
        if ba["with_cache"]:
            m["cache_k"] = np.ascontiguousarray(inp["cache_k"][:nl], f).reshape(nl * NPOOL * 8, 16 * 256)
            m["cache_v"] = np.ascontiguousarray(inp["cache_v"][:nl], f).reshape(nl * NPOOL * 8, 16 * 256)
            m["cache_kidx"] = np.ascontiguousarray(inp["cache_kidx"][:nl], f).reshape(nl * NPOOL, 128 * 64)
            m["page_table"] = np.ascontiguousarray(inp["page_table"][ss].reshape(sbn * 128, 1), np.int32)
        in_maps.append(m)
    res = run_bass_kernel_spmd(k.nc, in_maps, core_ids=list(range(ncores)))
    R = res.results

    pcores = list(range(0, 8, 2)) if ncores == 8 else list(range(ncores))

    def cat(name, tail, per):
        cores = pcores if name.startswith("p_") else list(range(ncores))
        parts = [R[c][name].reshape((nl, per) + tail) for c in cores]
        return np.concatenate(parts, 1)

    y_p = np.concatenate([R[c]["y_p"].reshape(pb, T, D) for c in pcores], 0)
    y_s = np.concatenate([R[c]["y_s"].reshape(sbn, TS, D) for c in range(ncores)], 0)
    outs = (y_p, y_s,
            cat("p_k", (T, 2, 128), pb), cat("p_v", (T, 2, 128), pb), cat("p_kidx", (T, 64), pb),
            cat("p_dn", (16, 128, 128), pb), cat("p_conv", (3, 6144), pb),
            cat("s_k", (TS, 2, 128), sbn), cat("s_v", (TS, 2, 128), sbn), cat("s_kidx", (TS, 64), sbn),
            cat("s_dn", (16, 128, 128), sbn), cat("s_conv", (3, 6144), sbn), cat("s_amlp_v", (TS, 1024), sbn))
    return tuple(np.ascontiguousarray(o, dtype=np.float32) for o in outs)
```
